# Optimizing a Trainium2 kernel written in Bass

```python
import jax, jax.numpy as jnp
from jax import lax
import numpy as np

D_MODEL = 1024
BATCH = 2
SEQ = 8192
DEPTH = 4

N_MIXERS = 4
N_MEM = 256
MIX_WIDTH = D_MODEL
XATTN_HEADS = 4
XATTN_HEAD_DIM = 64
XATTN_WIDTH = XATTN_HEADS * XATTN_HEAD_DIM
TOK_WIDTH = MIX_WIDTH - XATTN_WIDTH

GMLP_CHUNK = 128
GMLP_HEAD_DIM = 128
GMLP_HEADS = TOK_WIDTH // GMLP_HEAD_DIM

HGRN_HEAD_DIM = 128
HGRN_HEADS = TOK_WIDTH // HGRN_HEAD_DIM
HGRN_CHUNK = 16

POOL_WINDOWS = (2, 4, 8, 16)
POOL_GROUP = TOK_WIDTH // len(POOL_WINDOWS)

LRU_HEAD_DIM = 128
LRU_HEADS = TOK_WIDTH // LRU_HEAD_DIM
CONV_WIDTH = 4
LRU_C = 8.0

ALPHA = (2 * DEPTH) ** 0.25
BETA = (8 * DEPTH) ** -0.25
LN_EPS = 1e-5
RMS_EPS = 1e-6

IN_WIDTH_A = 2 * TOK_WIDTH + XATTN_WIDTH + MIX_WIDTH
IN_WIDTH_B = 3 * TOK_WIDTH + XATTN_WIDTH + MIX_WIDTH
IN_WIDTH_C = TOK_WIDTH + XATTN_WIDTH + MIX_WIDTH
IN_WIDTH_D = TOK_WIDTH + XATTN_WIDTH + MIX_WIDTH

kernel_name = 'hybrid_gmlp_hgrn2_pool_rglru_trunk'


def n_of_kind(kind):
    return len(range(kind, DEPTH, N_MIXERS))


def split_cols(t, widths):
    idx = [int(v) for v in np.cumsum(widths)[:-1]]
    return jnp.split(t, idx, axis=-1)


def layer_norm(x, g, b):
    xf = x.astype(jnp.float32)
    mu = jnp.mean(xf, -1, keepdims=True)
    var = jnp.mean(jnp.square(xf - mu), -1, keepdims=True)
    y = (xf - mu) * lax.rsqrt(var + LN_EPS) * g.astype(jnp.float32) + b.astype(jnp.float32)
    return y.astype(x.dtype)


def memory_cross_attention(q, mem_k, mem_v):
    B, S, _ = q.shape
    qh = q.reshape(B, S, XATTN_HEADS, XATTN_HEAD_DIM)
    s = jnp.einsum('bshd,bmhd->bhsm', qh, mem_k).astype(jnp.float32) * (XATTN_HEAD_DIM ** -0.5)
    p = jax.nn.softmax(s, axis=-1).astype(mem_v.dtype)
    o = jnp.einsum('bhsm,bmhd->bshd', p, mem_v)
    return o.reshape(B, S, XATTN_WIDTH).astype(q.dtype)


def chunked_spatial_gating(u, v, w_s, b_s):
    dt = u.dtype
    B, S, _ = v.shape
    n = S // GMLP_CHUNK
    u = jax.nn.gelu(u.astype(jnp.float32))
    v = jax.nn.gelu(v.astype(jnp.float32))
    vg = v.reshape(B, n, GMLP_CHUNK, GMLP_HEADS, GMLP_HEAD_DIM)
    mu = jnp.mean(vg, -1, keepdims=True)
    var = jnp.mean(jnp.square(vg - mu), -1, keepdims=True)
    vn = (vg - mu) * lax.rsqrt(var + LN_EPS)
    causal = jnp.tril(jnp.ones((GMLP_CHUNK, GMLP_CHUNK), dtype=bool))
    w = jnp.where(causal[None], w_s.astype(jnp.float32), 0.0)
    mixed = jnp.einsum('gts,bnsgc->bntgc', w, vn) + b_s.astype(jnp.float32).T[:, :, None]
    return (u * mixed.reshape(B, S, TOK_WIDTH)).astype(dt)


def hgrn2(q, f_logit, i, lb, norm_g):
    dt = q.dtype
    B, S, _ = q.shape
    H, K, C = HGRN_HEADS, HGRN_HEAD_DIM, HGRN_CHUNK
    n = S // C
    f = lb + (1.0 - lb) * jax.nn.sigmoid(f_logit.astype(jnp.float32))
    log_f = jnp.log(f)
    k = 1.0 - f
    qf = jax.nn.silu(q.astype(jnp.float32))
    vf = i.astype(jnp.float32)

    def chunks(t):
        return t.reshape(B, n, C, H, K).transpose(1, 0, 2, 3, 4)

    causal = jnp.tril(jnp.ones((C, C), dtype=bool))

    def step(state, xs):
        qc, kc, vc, lfc = xs
        g = jnp.cumsum(lfc, axis=1)
        g_last = g[:, -1]
        q_dec = qc * jnp.exp(g)
        k_inv = kc * jnp.exp(-g)
        scores = jnp.einsum('bthk,bshk->bhts', q_dec, k_inv)
        scores = jnp.where(causal, scores, 0.0)
        o = (jnp.einsum('bhts,bshv->bthv', scores, vc)
             + jnp.einsum('bthk,bhkv->bthv', q_dec, state))
        k_end = kc * jnp.exp(g_last[:, None] - g)
        state = jnp.exp(g_last)[..., None] * state + jnp.einsum('bshk,bshv->bhkv', k_end, vc)
        return state, o

    s0 = jnp.zeros((B, H, K, K), jnp.float32)
    _, o = lax.scan(step, s0, (chunks(qf), chunks(k), chunks(vf), chunks(log_f)))
    o = o.transpose(1, 0, 2, 3, 4).reshape(B, S, H, K)
    o = o * lax.rsqrt(jnp.mean(jnp.square(o), -1, keepdims=True) + RMS_EPS)
    return (o.reshape(B, S, TOK_WIDTH) * norm_g.astype(jnp.float32)).astype(dt)


def multiscale_pool(p, w_pool, scale):
    dt = p.dtype
    B, S, _ = p.shape
    grp = p.astype(jnp.float32).reshape(B, S, len(POOL_WINDOWS), POOL_GROUP)
    cs = jnp.cumsum(grp, axis=1)
    pos = jnp.arange(S)
    pooled = []
    for g, w in enumerate(POOL_WINDOWS):
        csg = cs[:, :, g]
        prev = jnp.pad(csg, ((0, 0), (w, 0), (0, 0)))[:, :S]
        cnt = jnp.minimum(pos + 1, w).astype(jnp.float32)[None, :, None]
        pooled.append((csg - prev) / cnt)
    pooled = jnp.stack(pooled, axis=2)
    y = jnp.einsum('bsgc,gcd->bsgd', pooled - grp, w_pool.astype(jnp.float32))
    return (y.reshape(B, S, TOK_WIDTH) * scale.astype(jnp.float32)).astype(dt)


def rg_lru_branch(xb, conv_w, conv_b, w_gx, b_gx, w_ga, b_ga, a_param):
    dt = xb.dtype
    B, S, _ = xb.shape
    xc = lax.conv_general_dilated(
        xb, conv_w.astype(dt)[:, None, :], window_strides=(1,),
        padding=[(CONV_WIDTH - 1, 0)], dimension_numbers=('NWC', 'WIO', 'NWC'),
        feature_group_count=TOK_WIDTH) + conv_b.astype(dt)
    xh = xc.astype(jnp.float32).reshape(B, S, LRU_HEADS, LRU_HEAD_DIM)
    gate_x = jax.nn.sigmoid(jnp.einsum('bshi,hij->bshj', xh, w_gx.astype(jnp.float32)) + b_gx.astype(jnp.float32))
    gate_a = jax.nn.sigmoid(jnp.einsum('bshi,hij->bshj', xh, w_ga.astype(jnp.float32)) + b_ga.astype(jnp.float32))
    log_a = -LRU_C * gate_a * jax.nn.softplus(-a_param.astype(jnp.float32).reshape(LRU_HEADS, LRU_HEAD_DIM))
    a = jnp.exp(log_a)
    mult = jnp.sqrt(-jnp.expm1(2.0 * log_a))
    first = (jnp.arange(S) == 0)[None, :, None, None]
    mult = jnp.where(first, 1.0, mult)
    b_term = mult * gate_x * xh

    def combine(l, r):
        a1, b1 = l
        a2, b2 = r
        return a1 * a2, a2 * b1 + b2

    _, h = lax.associative_scan(combine, (a, b_term), axis=1)
    return h.reshape(B, S, TOK_WIDTH).astype(dt)


def setup_inputs(seed: int = 0) -> dict:
    key = jax.random.key(seed)
    ks = jax.random.split(key, 24)
    f32 = jnp.float32

    def nrm(k, shape, scale):
        return jax.random.normal(k, shape, f32) * scale

    nA, nB, nC, nD = n_of_kind(0), n_of_kind(1), n_of_kind(2), n_of_kind(3)
    u = jax.random.uniform(ks[22], (nD, TOK_WIDTH), f32, minval=0.9, maxval=0.999)
    s = u ** (1.0 / LRU_C)
    return {
        'x': nrm(ks[0], (BATCH, SEQ, D_MODEL), 1.0),
        'mem': nrm(ks[1], (BATCH, N_MEM, D_MODEL), 1.0),
        'mem_kv_w': nrm(ks[2], (D_MODEL, 2 * XATTN_WIDTH), D_MODEL ** -0.5),
        'ln_g': 1.0 + nrm(ks[3], (DEPTH, D_MODEL), 0.02),
        'ln_b': nrm(ks[4], (DEPTH, D_MODEL), 0.02),
        'w_out': nrm(ks[5], (DEPTH, MIX_WIDTH, D_MODEL), BETA * MIX_WIDTH ** -0.5),
        'hgrn_lb_logits': 1.0 + nrm(ks[6], (DEPTH, TOK_WIDTH), 0.1),
        'a_w_in': nrm(ks[7], (nA, D_MODEL, IN_WIDTH_A), D_MODEL ** -0.5),
        'a_w_s': nrm(ks[8], (nA, GMLP_HEADS, GMLP_CHUNK, GMLP_CHUNK), GMLP_CHUNK ** -0.5),
        'a_b_s': 1.0 + nrm(ks[9], (nA, GMLP_HEADS, GMLP_CHUNK), 0.02),
        'b_w_in': nrm(ks[10], (nB, D_MODEL, IN_WIDTH_B), D_MODEL ** -0.5),
        'b_norm_g': 1.0 + nrm(ks[11], (nB, TOK_WIDTH), 0.02),
        'c_w_in': nrm(ks[12], (nC, D_MODEL, IN_WIDTH_C), D_MODEL ** -0.5),
        'c_w_pool': nrm(ks[13], (nC, len(POOL_WINDOWS), POOL_GROUP, POOL_GROUP), POOL_GROUP ** -0.5),
        'c_scale': 1.0 + nrm(ks[14], (nC, TOK_WIDTH), 0.02),
        'd_w_in': nrm(ks[15], (nD, D_MODEL, IN_WIDTH_D), D_MODEL ** -0.5),
        'd_conv_w': nrm(ks[16], (nD, CONV_WIDTH, TOK_WIDTH), CONV_WIDTH ** -0.5),
        'd_conv_b': nrm(ks[17], (nD, TOK_WIDTH), 0.02),
        'd_w_gx': nrm(ks[18], (nD, LRU_HEADS, LRU_HEAD_DIM, LRU_HEAD_DIM), LRU_HEAD_DIM ** -0.5),
        'd_b_gx': nrm(ks[19], (nD, LRU_HEADS, LRU_HEAD_DIM), 0.02),
        'd_w_ga': nrm(ks[20], (nD, LRU_HEADS, LRU_HEAD_DIM, LRU_HEAD_DIM), LRU_HEAD_DIM ** -0.5),
        'd_b_ga': nrm(ks[21], (nD, LRU_HEADS, LRU_HEAD_DIM), 0.02),
        'd_a_param': jnp.log(s) - jnp.log1p(-s),
    }


def reference(x, mem, mem_kv_w, ln_g, ln_b, w_out, hgrn_lb_logits,
              a_w_in, a_w_s, a_b_s,
              b_w_in, b_norm_g,
              c_w_in, c_w_pool, c_scale,
              d_w_in, d_conv_w, d_conv_b, d_w_gx, d_b_gx, d_w_ga, d_b_ga, d_a_param):
    B = x.shape[0]
    M = mem.shape[1]
    kv = jnp.einsum('bmd,de->bme', mem, mem_kv_w)
    mem_k = kv[..., :XATTN_WIDTH].reshape(B, M, XATTN_HEADS, XATTN_HEAD_DIM)
    mem_v = kv[..., XATTN_WIDTH:].reshape(B, M, XATTN_HEADS, XATTN_HEAD_DIM)
    lb_p = jax.nn.softmax(hgrn_lb_logits.astype(jnp.float32), axis=0)
    lower_bounds = jnp.cumsum(lb_p, axis=0) - lb_p[0]

    for i in range(DEPTH):
        kind, j = i % N_MIXERS, i // N_MIXERS
        if kind == 0:
            proj = jnp.einsum('bsd,de->bse', x, a_w_in[j])
            u, v, q_x, gate = split_cols(proj, [TOK_WIDTH, TOK_WIDTH, XATTN_WIDTH, MIX_WIDTH])
            tok = chunked_spatial_gating(u, v, a_w_s[j], a_b_s[j])
        elif kind == 1:
            proj = jnp.einsum('bsd,de->bse', x, b_w_in[j])
            q, f_logit, inp, q_x, gate = split_cols(
                proj, [TOK_WIDTH, TOK_WIDTH, TOK_WIDTH, XATTN_WIDTH, MIX_WIDTH])
            tok = hgrn2(q, f_logit, inp, lower_bounds[i], b_norm_g[j])
        elif kind == 2:
            proj = jnp.einsum('bsd,de->bse', x, c_w_in[j])
            p, q_x, gate = split_cols(proj, [TOK_WIDTH, XATTN_WIDTH, MIX_WIDTH])
            tok = multiscale_pool(p, c_w_pool[j], c_scale[j])
        else:
            proj = jnp.einsum('bsd,de->bse', x, d_w_in[j])
            xb, q_x, gate = split_cols(proj, [TOK_WIDTH, XATTN_WIDTH, MIX_WIDTH])
            tok = rg_lru_branch(xb, d_conv_w[j], d_conv_b[j], d_w_gx[j], d_b_gx[j],
                                d_w_ga[j], d_b_ga[j], d_a_param[j])
        xo = memory_cross_attention(q_x, mem_k, mem_v)
        mixed = jnp.concatenate([tok.astype(x.dtype), xo.astype(x.dtype)], axis=-1) * jax.nn.silu(gate)
        y = jnp.einsum('bse,ed->bsd', mixed, w_out[i]).astype(x.dtype)
        x = layer_norm(ALPHA * x + y, ln_g[i], ln_b[i])
    return x
```

```python
import contextlib
import numpy as np
import concourse.bass as bass
import concourse.mybir as mybir
from concourse.bass_utils import run_bass_kernel_spmd

F32 = mybir.dt.float32
BF16 = mybir.dt.bfloat16
AF = mybir.ActivationFunctionType
ALU = mybir.AluOpType
AX = mybir.AxisListType

D = 1024
NMEM = 256
TOK = 768
ALPHA = float((2 * 4) ** 0.25)
LN_EPS = 1e-5
RMS_EPS = 1e-6
LRU_C = 8.0
IN_W = [2816, 3584, 2048, 2048]
ST = 512
NCONST = 640
NCSB = 416


class Buf:
    __slots__ = ("ap", "name", "lastw", "readers", "psum", "gen")

    def __init__(self, ap, name, psum=False):
        self.ap = ap
        self.name = name
        self.psum = psum
        self.lastw = None
        self.readers = {}
        self.gen = 0

    def __getitem__(self, k):
        return self.ap[k]


class View:
    __slots__ = ("base", "gen")

    def __init__(self, base):
        base.gen += 1
        self.base, self.gen = base, base.gen

    def __getitem__(self, k):
        return self.base.ap[k]

    @property
    def ap(self):
        return self.base.ap

    @property
    def name(self):
        return self.base.name


class Eng:
    def __init__(self, name, h, sem):
        self.name, self.h, self.sem, self.cnt = name, h, sem, 0
        self.waited = {}


class KB:
    def __init__(self, nc, es):
        self.nc, self.es = nc, es
        self.eng = {}
        for name, h in (("pe", nc.tensor), ("act", nc.scalar), ("dve", nc.vector),
                        ("pool", nc.gpsimd), ("sp", nc.sync)):
            self.eng[name] = Eng(name, h, es.enter_context(nc.semaphore("s_" + name)))
        self.dsems = {}
        for q in ("sp", "pool"):
            self.dsems[q] = [[es.enter_context(nc.semaphore("d_%s%d" % (q, i))), 0] for i in range(20)]
        self.dnext = {"sp": 0, "pool": 0}
        self.ccsem = es.enter_context(nc.semaphore("s_cc"))
        self.cccnt = 0
        self.nb = 0
        self.ninstr = 0

    def sb(self, shape, dt, name=None):
        self.nb += 1
        name = name or "b%d" % self.nb
        t = self.es.enter_context(self.nc.sbuf_tensor(name, list(shape), dt))
        return Buf(t, name)

    def ps(self, shape, dt, name=None):
        self.nb += 1
        name = name or "p%d" % self.nb
        t = self.es.enter_context(self.nc.psum_tensor(name, list(shape), dt))
        return Buf(t, name, psum=True)

    def dram(self, shape, dt, name):
        t = self.nc.dram_tensor(name, list(shape), dt)
        return Buf(t.ap(), name)

    @staticmethod
    def _res(lst):
        out = []
        for x in lst:
            if isinstance(x, View):
                if x.gen != x.base.gen:
                    raise RuntimeError("stale ring buffer use: %s (gen %d, now %d)" % (x.base.name, x.gen, x.base.gen))
                x = x.base
            out.append(x)
        return out

    def _wait(self, e, tok):
        sem, val, sid = tok
        if e.waited.get(sid, 0) < val:
            e.h.wait_ge(sem, val)
            e.waited[sid] = val
            self.ninstr += 1

    def _deps(self, e, reads, writes):
        for b in reads:
            if b.lastw is not None:
                tok, src = b.lastw
                if not (src == "pe" and e.name == "pe"):
                    self._wait(e, tok)
            if b.psum:
                for key, (tok, src) in b.readers.items():
                    if src != e.name:
                        self._wait(e, tok)
        for b in writes:
            if b.lastw is not None:
                tok, src = b.lastw
                if not (src == "pe" and e.name == "pe"):
                    self._wait(e, tok)
            for key, (tok, src) in b.readers.items():
                if not (src == "pe" and e.name == "pe"):
                    self._wait(e, tok)

    def _commit(self, tok, src, reads, writes):
        for b in writes:
            b.lastw = (tok, src)
            b.readers = {}
        for b in reads:
            key = src if not src.startswith("dma") else "dma%d" % id(tok[0])
            b.readers[key] = (tok, src)

    def op(self, engname, fn, reads=(), writes=()):
        e = self.eng[engname]
        reads, writes = self._res(reads), self._res(writes)
        self._deps(e, reads, writes)
        ins = fn(e.h)
        e.cnt += 1
        ins.then_inc(e.sem, 1)
        self.ninstr += 1
        tok = (e.sem, e.cnt, id(e.sem))
        self._commit(tok, engname, reads, writes)
        return tok

    def dma(self, q, out_ap, in_ap, reads=(), writes=()):
        e = self.eng[q]
        reads, writes = self._res(reads), self._res(writes)
        self._deps(e, reads, writes)
        lst = self.dsems[q]
        i = self.dnext[q]
        self.dnext[q] = (i + 1) % len(lst)
        sem, tot = lst[i]
        if tot > 0:
            self._wait(e, (sem, tot, id(sem)))
        e.h.dma_start(out=out_ap, in_=in_ap).then_inc(sem, 16)
        self.ninstr += 1
        lst[i][1] = tot + 16
        tok = (sem, tot + 16, id(sem))
        self._commit(tok, "dma_" + q, reads, writes)
        return tok

    def allgather(self, groups, src, dst):
        e = self.eng["pool"]
        self._deps(e, [src], [dst])
        e.h.collective_compute("AllGather", ALU.bypass, groups, ins=[src.ap.opt()], outs=[dst.ap.opt()]).then_inc(self.ccsem)
        self.cccnt += 1
        self.ninstr += 1
        tok = (self.ccsem, self.cccnt, id(self.ccsem))
        self._commit(tok, "cc", [src], [dst])
        return tok

    def finish(self, bufs):
        e = self.eng["sp"]
        for b in bufs:
            if b.lastw is not None:
                self._wait(e, b.lastw[0])


class Ring:
    def __init__(self, bufs):
        self.bufs, self.i = bufs, 0

    def __call__(self):
        b = self.bufs[self.i]
        self.i = (self.i + 1) % len(self.bufs)
        return View(b)


def run_gen(g):
    for _ in g:
        pass


def stagger(gens, depth):
    it = iter(gens)
    active, more = [], True
    while True:
        if more and len(active) < depth:
            try:
                active.append(next(it))
            except StopIteration:
                more = False
        if not active:
            break
        for g in list(active):
            try:
                next(g)
            except StopIteration:
                active.remove(g)


def stagger_ticks(gens, depth):
    it = iter(gens)
    active, more = [], True
    while True:
        if more and len(active) < depth:
            try:
                active.append(next(it))
            except StopIteration:
                more = False
        if not active:
            break
        for g in list(active):
            try:
                next(g)
            except StopIteration:
                active.remove(g)
        yield


def alias_bufs(parent, aps, prefix):
    out = []
    for i, ap in enumerate(aps):
        b = Buf(ap, "%s%d" % (prefix, i))
        b.lastw = parent.lastw
        b.readers = dict(parent.readers)
        out.append(b)
    return out


def retire(aliases, parent):
    for a in aliases:
        if a.lastw is not None:
            parent.readers[a.name + "#w"] = a.lastw
        for key, v in a.readers.items():
            parent.readers[a.name + "#" + key] = v


def interleave(gx, gy, r):
    dx = dy = False
    while not (dx and dy):
        for _ in range(r):
            if not dx:
                try:
                    next(gx)
                except StopIteration:
                    dx = True
        if not dy:
            try:
                next(gy)
            except StopIteration:
                dy = True


def build(T, NR, nlayers=4, layer_list=None):
    assert T % ST == 0
    NS = T // ST
    nc = bass.Bass("TRN2", target_bir_lowering=False)
    es = contextlib.ExitStack()
    groups = [list(range(g * NR, (g + 1) * NR)) for g in range(8 // NR)]

    def din(name, shape):
        return Buf(nc.dram_tensor(name, list(shape), F32, kind="ExternalInput").ap(), name)

    x_in = din("x", [T, D])
    memT_in = din("memT", [128, 8, NMEM])
    kvw_in = din("mem_kv_w", [D, 512])
    lng_in = din("ln_g", [4, D])
    lnb_in = din("ln_b", [4, D])
    wout_in = din("w_out", [4, D, D])
    lb_in = din("lbvec", [2, TOK])
    hlb_in = din("hgrn_lb_logits", [4, TOK])
    w_in = [din("a_w_in", [D, IN_W[0]]), din("b_w_in", [D, IN_W[1]]),
            din("c_w_in", [D, IN_W[2]]), din("d_w_in", [D, IN_W[3]])]
    a_wsT_in = din("a_w_sT", [128, 6, 128])
    a_bs_in = din("a_b_s", [1, TOK])
    b_ng_in = din("b_norm_g", [128, 6])
    c_wp_in = din("c_w_pool", [4, 192, 192])
    c_sc_in = din("c_scale", [128, 6])
    d_cw_in = din("d_conv_w", [128, 6, 4])
    d_cb_in = din("d_conv_b", [128, 6])
    d_wgx_in = din("d_w_gx", [128, 6, 128])
    d_bgx_in = din("d_b_gx", [128, 6])
    d_wga_in = din("d_w_ga", [128, 6, 128])
    d_bga_in = din("d_b_ga", [128, 6])
    d_ap_in = din("d_a_param", [128, 6])
    consts_in = din("consts", [128, NCONST])
    rankc_in = din("rankc", [128, 128])
    out_d = Buf(nc.dram_tensor("out", [T, D], F32, kind="ExternalOutput").ap(), "out")

    with es:
        kb = KB(nc, es)
        op, dma = kb.op, kb.dma
        xres = [kb.dram([T, D], F32, "xres%d" % i) for i in range(2)]
        xT_d = [kb.dram([128, 8, T], BF16, "xT_d%d" % i) for i in range(2)]
        WA = kb.sb([128, 8, 1536], BF16, "WA")
        WB = kb.sb([128, 8, 1280], BF16, "WB")
        WC = kb.sb([128, 8, 1024], BF16, "WC")
        CONST = kb.sb([128, NCSB], F32, "CONST")
        RANKC = kb.sb([128, 128], F32, "RANKC")
        IDB = kb.sb([128, 128], BF16, "IDB")
        KTP = kb.sb([128, 4, NMEM], BF16, "KTP")
        VP = kb.sb([128, 2, 4, 128], BF16, "VP")
        OP_ = kb.sb([128, 2, 128], BF16, "OP")
        LNG = kb.sb([128, D], F32, "LNG")
        LNB = kb.sb([128, D], F32, "LNB")
        xT_t = Ring([kb.sb([128, 8, ST], BF16, "xT%d" % i) for i in range(2)])
        xst = Ring([kb.sb([128, 8, 128], BF16, "xst%d" % i) for i in range(2)])
        SGs = [kb.sb([128, 8, ST], BF16, "SG%d" % i) for i in range(2)]
        QT = kb.sb([128, 2, ST], BF16, "QT")
        wk32 = Ring([kb.sb([128, ST + 16], F32, "wk32_%d" % i) for i in range(10)])
        wk16 = Ring([kb.sb([128, ST], BF16, "wk16_%d" % i) for i in range(6)])
        ln32 = Ring([kb.sb([128, D], F32, "ln32_%d" % i) for i in range(4)])
        tm32 = Ring([kb.sb([128, TOK], F32, "tm32_%d" % i) for i in range(5)])
        tm16 = Ring([kb.sb([128, TOK], BF16, "tm16_%d" % i) for i in range(4)])
        xnb_r = Ring([kb.sb([128, D], BF16, "xnb%d" % i) for i in range(1)])
        small = Ring([kb.sb([128, 16], F32, "sm%d" % i) for i in range(18)])
        smallY = Ring([kb.sb([128, 24], F32, "smY%d" % i) for i in range(4)])
        LC = kb.sb([128, 1536], F32, "LC")
        LCB = kb.sb([128, 6, 768], BF16, "LCB")
        TOKB = kb.sb([128, 6, ST], F32, "TOKB")
        HALO = kb.sb([128, 6, 16], F32, "HALO")
        HPREV = kb.sb([128, 16], F32, "HPREV")
        EPSC = kb.sb([128, 4], F32, "EPSC")
        XTH = kb.sb([128, 8, 16], BF16, "XTH")
        print("sbuf remaining", nc.sbuf_bytes_remaining)
        PB = [kb.ps([128, 512], F32, "psb%d" % i) for i in range(7)]
        pst = kb.ps([128, 8, 128], BF16, "pst")
        psb = Ring(PB[0:6])
        psX = Ring(PB[0:3])
        psY = Ring(PB[3:7])
        P6 = PB[6]
        p6bf = P6.ap[:, :].bitcast(BF16).rearrange("p (a b) -> p a b", b=128)

        dma("sp", CONST[:, :], consts_in[:, 128:128 + NCSB], [consts_in], [CONST])
        dma("sp", RANKC[:, :], rankc_in[:, :], [rankc_in], [RANKC])
        dma("pool", IDB[:, :], consts_in[:, 0:128], [consts_in], [IDB])
        C_TRI, C_TRI64, C_TRIREV64, C_IND, C_CK = 0, 128, 256, 384, 388
        op("dve", lambda h: h.memset(EPSC[:, 0:1], LN_EPS), [], [EPSC])
        op("dve", lambda h: h.memset(EPSC[:, 1:2], RMS_EPS), [], [EPSC])
        op("dve", lambda h: h.memset(EPSC[:, 2:3], 1.0), [], [EPSC])
        op("dve", lambda h: h.memset(EPSC[:, 3:4], 0.0), [], [EPSC])

        def load_w(dst, src_ap, ncols, c0=0):
            for k in range(8):
                dma("pool", dst[:, k, c0:c0 + ncols], src_ap[k * 128:(k + 1) * 128, :], [], [dst])

        wspec = {
            0: {"WA": (WA, lambda: w_in[0][:, 0:1536], 1536), "WB": (WB, lambda: w_in[0][:, 1536:2816], 1280)},
            1: {"WA": (WA, lambda: w_in[1][:, 768:2304], 1536), "WB1": (WB, lambda: w_in[1][:, 0:768], 768),
                "WB": (WB, lambda: w_in[1][:, 2304:3584], 1280)},
            2: {"WA": (WA, lambda: w_in[2][:, 0:768], 768), "WB": (WB, lambda: w_in[2][:, 768:2048], 1280)},
            3: {"WA": (WA, lambda: w_in[3][:, 0:768], 768), "WB": (WB, lambda: w_in[3][:, 768:2048], 1280)},
        }
        for l_ in range(4):
            wspec[l_]["WC"] = (WC, (lambda l_=l_: wout_in[l_, :, :]), 1024)
        resident = {}

        def ensure_w(l, key):
            if l is None:
                return
            dst, src, ncols = wspec[l][key]
            if resident.get(dst.name) != (l, key):
                load_w(dst, src(), ncols)
                resident[dst.name] = (l, key)

        def first_wb(l):
            return "WB1" if l == 1 else "WB"

        MEMT = xT_t()
        dma("pool", MEMT[:, :, 0:NMEM], memT_in[:, :, :], [memT_in], [MEMT])
        load_w(WB, kvw_in.ap, 512)
        op("pool", lambda h: h.memset(KTP[:, :, :], 0.0), [], [KTP])
        op("pool", lambda h: h.memset(VP[:, :, :, :], 0.0), [], [VP])
        op("pool", lambda h: h.memset(OP_[:, :, :], 0.0), [], [OP_])
        op("pool", lambda h: h.memset(OP_[:, 0, 0:64], 1.0), [], [OP_])
        op("pool", lambda h: h.memset(OP_[:, 1, 64:128], 1.0), [], [OP_])
        for c in range(2):
            p = psb()
            for k in range(8):
                op("pe", lambda h, k=k, c=c, p=p: h.matmul(p[:, 0:NMEM], WB[:, k, c * 128:(c + 1) * 128], MEMT[:, k, 0:NMEM],
                                                           start=(k == 0), stop=(k == 7)), [WB, MEMT], [p])
            op("act", lambda h, c=c, p=p: h.copy(KTP[0:64, 2 * c, :], p[0:64, 0:NMEM]), [p], [KTP])
            op("act", lambda h, c=c, p=p: h.copy(KTP[64:128, 2 * c + 1, :], p[64:128, 0:NMEM]), [p], [KTP])
        for mc in range(2):
            p = psb()
            for k in range(8):
                op("pe", lambda h, k=k, mc=mc, p=p: h.matmul(p[:, 0:256], MEMT[:, k, mc * 128:(mc + 1) * 128], WB[:, k, 256:512],
                                                             start=(k == 0), stop=(k == 7)), [WB, MEMT], [p])
            for hd in range(4):
                o = (hd % 2) * 64
                op("act", lambda h, hd=hd, mc=mc, p=p, o=o: h.copy(VP[:, mc, hd, o:o + 64], p[:, hd * 64:(hd + 1) * 64]), [p], [VP])

        ll = layer_list if layer_list is not None else list(range(nlayers))

        def transposes_to_xT(xn_f32, xT_dst, t0):
            xb = xnb_r()
            op("act", lambda h: h.copy(xb[:, :], xn_f32[:, :]), [xn_f32], [xb])
            for k in range(8):
                op("pe", lambda h, k=k: h.transpose(pst[:, k, :], xb[:, k * 128:(k + 1) * 128], IDB[:, :]), [xb, IDB], [pst])
            stg = xst()
            op("dve", lambda h: h.tensor_copy(stg[:, :, :], pst[:, :, :]), [pst], [stg])
            dma("pool", xT_dst[:, :, t0:t0 + 128], stg[:, :, :], [stg], [xT_dst])

        def pro_unit(t0):
            xr = ln32()
            dma("sp", xr[:, :], x_in[t0:t0 + 128, :], [x_in], [xr])
            yield
            transposes_to_xT(xr, xT_d[0], t0)

        def prologue(s):
            stagger((pro_unit(s * ST + sub * 128) for sub in range(4)), 2)

        lazy_pro = (ll[0] in (0, 2))
        for s in range(min(2, NS) if lazy_pro else NS):
            prologue(s)
        ensure_w(ll[0], "WA")
        ensure_w(ll[0], first_wb(ll[0]))

        def load_ln(l):
            dma("sp", LNG[:, :], lng_in[l:l + 1, :].partition_broadcast(128), [], [LNG])
            dma("sp", LNB[:, :], lnb_in[l:l + 1, :].partition_broadcast(128), [], [LNB])

        def load_xT(pi, s):
            xt = xT_t()
            dma("sp", xt[:, :, :], xT_d[pi % 2][:, :, s * ST:(s + 1) * ST], [xT_d[pi % 2]], [xt])
            return xt

        def proj_fm(W, c0, xt, ring, n=ST):
            p = ring()
            for k in range(8):
                op("pe", lambda h, k=k: h.matmul(p[:, 0:n], W[:, k, c0:c0 + 128], xt[:, k, 0:n],
                                                 start=(k == 0), stop=(k == 7)), [W, xt], [p])
            return p

        def proj_tm(W, c0, ncols, xt, sub, ring):
            p = ring()
            for k in range(8):
                op("pe", lambda h, k=k: h.matmul(p[:, 0:ncols], xt[:, k, sub * 128:(sub + 1) * 128], W[:, k, c0:c0 + ncols],
                                                 start=(k == 0), stop=(k == 7)), [W, xt], [p])
            return p

        def qx_gate_attn(xt, SG, act_recip=False):
            for c in range(2):
                p = proj_fm(WB, c * 128, xt, psX)
                op("act", lambda h, c=c, p=p: h.copy(QT[:, c, :], p[:, :]), [p], [QT])
            yield

            def gates(js):
                for j in js:
                    p = proj_fm(WB, 256 + j * 128, xt, psX)
                    op("act", lambda h, j=j, p=p: h.activation(SG[:, j, :], p[:, :], AF.Silu), [p], [SG])
                    yield

            for pr, gj in ((0, (6, 7, 0, 1)), (1, (2, 3, 4, 5))):
                pts = []
                for hd in (2 * pr, 2 * pr + 1):
                    for mc in range(2):
                        sc = psX()
                        op("pe", lambda h, hd=hd, mc=mc, sc=sc: h.matmul(sc[:, :], KTP[:, hd, mc * 128:(mc + 1) * 128], QT[:, pr, :],
                                                                         start=True, stop=True), [KTP, QT], [sc])
                        pt = wk16()
                        op("act", lambda h, sc=sc, pt=pt: h.activation(pt[:, :], sc[:, :], AF.Exp, scale=0.125), [sc], [pt])
                        pts.append((hd, mc, pt))
                yield
                yield from gates(gj)
                num, den = psX(), psX()
                for i, (hd, mc, pt) in enumerate(pts):
                    op("pe", lambda h, hd=hd, mc=mc, pt=pt, i=i: h.matmul(num[:, :], VP[:, mc, hd, :], pt[:, :],
                                                                          start=(i == 0), stop=(i == 3)), [VP, pt], [num])
                for i, (hd, mc, pt) in enumerate(pts):
                    op("pe", lambda h, hd=hd, pt=pt, i=i: h.matmul(den[:, :], OP_[:, hd % 2, :], pt[:, :],
                                                                   start=(i == 0), stop=(i == 3)), [OP_, pt], [den])
                rec = wk32()
                if act_recip:
                    op("act", lambda h: h.activation(rec[:, 0:ST], den[:, :], AF.Ln), [den], [rec])
                    op("act", lambda h: h.activation(rec[:, 0:ST], rec[:, 0:ST], AF.Exp, scale=-1.0), [rec], [rec])
                else:
                    op("dve", lambda h: h.reciprocal(rec[:, 0:ST], den[:, :]), [den], [rec])
                xo = wk32()
                op("dve", lambda h: h.tensor_tensor(xo[:, 0:ST], num[:, :], rec[:, 0:ST], ALU.mult), [num, rec], [xo])
                op("dve", lambda h, pr=pr: h.tensor_tensor(SG[:, 6 + pr, :], xo[:, 0:ST], SG[:, 6 + pr, :], ALU.mult), [xo, SG], [SG])
                yield

        def Ygen(pi, s, last, SG, mul_eng="dve"):
            xsrc = x_in if pi == 0 else xres[pi % 2]
            xdst = out_d if last else xres[(pi + 1) % 2]

            def unit(sub):
                t0 = s * ST + sub * 128
                xr = ln32()
                dma("sp", xr[:, :], xsrc[t0:t0 + 128, :], [xsrc], [xr])
                ys = [psY(), psY()]
                for hf in range(2):
                    for e in range(8):
                        op("pe", lambda h, e=e, hf=hf: h.matmul(ys[hf][:, :], SG[:, e, sub * 128:(sub + 1) * 128], WC[:, e, hf * 512:(hf + 1) * 512],
                                                                start=(e == 0), stop=(e == 7)), [SG, WC], [ys[hf]])
                yield
                smy = smallY()
                st6 = View.__new__(View); st6.base, st6.gen = smy.base, smy.gen
                mv = smy
                for hf in range(2):
                    op("dve", lambda h, hf=hf: h.scalar_tensor_tensor(xr[:, hf * 512:(hf + 1) * 512], xr[:, hf * 512:(hf + 1) * 512], ALPHA,
                                                                      ys[hf][:, :], ALU.mult, ALU.add), [xr, ys[hf]], [xr])
                    op("dve", lambda h, hf=hf: h.bn_stats(st6[:, hf * 6:(hf + 1) * 6], xr[:, hf * 512:(hf + 1) * 512]), [xr], [st6])
                op("dve", lambda h: h.bn_aggr(mv[:, 16:18], st6[:, 0:12]), [st6], [mv])
                yield
                op("act", lambda h: h.activation(mv[:, 20:21], mv[:, 17:18], AF.Sqrt, bias=EPSC[:, 0:1], scale=1.0), [mv, EPSC], [mv])
                yield
                op("dve", lambda h: h.reciprocal(mv[:, 18:19], mv[:, 20:21]), [mv], [mv])
                op("dve", lambda h: h.scalar_tensor_tensor(mv[:, 19:20], mv[:, 16:17], -1.0, mv[:, 18:19], ALU.mult, ALU.mult), [mv], [mv])
                yield
                op("act", lambda h: h.activation(xr[:, :], xr[:, :], AF.Identity, bias=mv[:, 19:20], scale=mv[:, 18:19]), [xr, mv], [xr])
                yield
                op(mul_eng, lambda h: h.tensor_tensor(xr[:, :], xr[:, :], LNG[:, :], ALU.mult), [xr, LNG], [xr])
                if not last:
                    xb = xnb_r()
                    op("dve", lambda h: h.tensor_tensor(xb[:, :], xr[:, :], LNB[:, :], ALU.add), [xr, LNB], [xb])
                yield
                op("pool", lambda h: h.tensor_tensor(xr[:, :], xr[:, :], LNB[:, :], ALU.add), [xr, LNB], [xr])
                dma("pool", xdst[t0:t0 + 128, :], xr[:, :], [xr], [xdst])
                if not last:
                    for k in range(8):
                        op("pe", lambda h, k=k: h.transpose(pst[:, k, :], xb[:, k * 128:(k + 1) * 128], IDB[:, :]), [xb, IDB], [pst])
                    yield
                    stg = xst()
                    op("act", lambda h: h.copy(stg[:, :, :], pst[:, :, :]), [pst], [stg])
                    dma("pool", xT_d[(pi + 1) % 2][:, :, t0:t0 + 128], stg[:, :, :], [stg], [xT_d[(pi + 1) % 2]])

            yield from stagger_ticks((unit(sub) for sub in range(4)), 4)

        def pipeline(X, pi, last, r, hook=None, mul_eng="dve"):
            run_gen(X(0, SGs[0]))
            for s in range(1, NS):
                if pi == 0 and lazy_pro and s + 1 < NS:
                    prologue(s + 1)
                interleave(X(s, SGs[s % 2]), Ygen(pi, s - 1, last, SGs[(s - 1) % 2], mul_eng), r)
            if hook is not None:
                hook()
            run_gen(Ygen(pi, NS - 1, last, SGs[(NS - 1) % 2], mul_eng))

        cc_n = [0]

        def gather(src_sb_ap, src_buf, rows, cols):
            cc_n[0] += 1
            csrc = kb.dram([rows, cols], F32, "cc_src%d" % cc_n[0])
            cdst = kb.dram([NR * rows, cols], F32, "cc_dst%d" % cc_n[0])
            dma("pool", csrc[:, :], src_sb_ap, [src_buf], [csrc])
            kb.allgather(groups, csrc, cdst)
            return cdst

        def halo_A(pi):
            xsrc = x_in if pi == 0 else xres[pi % 2]
            xl = ln32()
            dma("sp", xl[0:16, :], xsrc[T - 16:T, :], [xsrc], [xl])
            cdst = gather(xl[0:16, :], xl, 16, D)
            g = ln32()
            dma("sp", g[0:NR * 16, :], cdst[:, :], [cdst], [g])
            return g

        def halo_B(g):
            ph = psb()
            for k in range(8):
                op("pe", lambda h, k=k: h.matmul(ph[:, k * 16:(k + 1) * 16], g[0:NR * 16, k * 128:(k + 1) * 128], RANKC[0:NR * 16, 80:96],
                                                 start=True, stop=True), [g, RANKC], [ph])
            op("act", lambda h: h.copy(XTH.ap.rearrange("p k t -> p (k t)"), ph[:, 0:128]), [ph], [XTH])
            for j in range(6):
                p = psb()
                for k in range(8):
                    op("pe", lambda h, k=k, j=j, p=p: h.matmul(p[:, 0:16], WA[:, k, j * 128:(j + 1) * 128], XTH[:, k, :],
                                                               start=(k == 0), stop=(k == 7)), [WA, XTH], [p])
                op("act", lambda h, j=j, p=p: h.copy(HALO[:, j, :], p[:, 0:16]), [p], [HALO])

        def layer0(l, pi, last, nxt):
            ensure_w(l, "WA")
            ensure_w(l, "WB")
            ensure_w(l, "WC")
            load_ln(l)
            WST32 = tm32()
            dma("sp", WST32[:, :], a_wsT_in.ap.rearrange("p g t -> p (g t)"), [a_wsT_in], [WST32])
            for g in range(6):
                op("dve", lambda h, g=g: h.tensor_tensor(LCB[:, g, 0:128], WST32[:, g * 128:(g + 1) * 128], CONST[:, C_TRI:C_TRI + 128], ALU.mult),
                   [WST32, CONST], [LCB])
            dma("sp", LC[:, 0:TOK], a_bs_in[0:1, :].partition_broadcast(128), [a_bs_in], [LC])

            def X(s, SG):
                xt = load_xT(pi, s)
                gus = []
                for j in range(6):
                    p = proj_fm(WA, j * 128, xt, psX)
                    gu = wk32()
                    op("act", lambda h, p=p, gu=gu: h.activation(gu[:, 0:ST], p[:, :], AF.Gelu_apprx_tanh), [p], [gu])
                    gus.append(gu)
                    yield
                vns = {}

                def stA(sub):
                    p1 = proj_tm(WA, 768, 512, xt, sub, psX)
                    p2 = proj_tm(WA, 1280, 256, xt, sub, psX)
                    gv, sq = tm32(), tm32()
                    op("act", lambda h: h.activation(gv[:, 0:512], p1[:, :], AF.Gelu_apprx_tanh), [p1], [gv])
                    op("act", lambda h: h.activation(gv[:, 512:768], p2[:, 0:256], AF.Gelu_apprx_tanh), [p2], [gv])
                    op("act", lambda h: h.activation(sq[:, :], gv[:, :], AF.Square), [gv], [sq])
                    sm = small()
                    op("dve", lambda h: h.reduce_sum(sm[:, 0:6], gv.ap.rearrange("p (g c) -> p g c", c=128), axis=AX.X), [gv], [sm])
                    op("dve", lambda h: h.reduce_sum(sm[:, 6:12], sq.ap.rearrange("p (g c) -> p g c", c=128), axis=AX.X), [sq], [sm])
                    sm2 = small()
                    op("dve", lambda h: h.tensor_scalar(sm2[:, 0:6], sm[:, 0:6], 1.0 / 128, None, ALU.mult), [sm], [sm2])
                    op("dve", lambda h: h.tensor_tensor(sm2[:, 6:12], sm2[:, 0:6], sm2[:, 0:6], ALU.mult), [sm2], [sm2])
                    sm3 = small()
                    op("dve", lambda h: h.scalar_tensor_tensor(sm3[:, 0:6], sm[:, 6:12], 1.0 / 128, sm2[:, 6:12], ALU.mult, ALU.subtract), [sm, sm2], [sm3])
                    sm4 = small()
                    op("act", lambda h: h.activation(sm4[:, 0:6], sm3[:, 0:6], AF.Sqrt, bias=EPSC[:, 0:1], scale=1.0), [sm3, EPSC], [sm4])
                    op("dve", lambda h: h.reciprocal(sm3[:, 6:12], sm4[:, 0:6]), [sm4], [sm3])
                    gv3 = gv.ap.rearrange("p (g c) -> p g c", c=128)
                    op("dve", lambda h: h.tensor_tensor(gv3, gv3, sm2[:, 0:6].unsqueeze(2).to_broadcast([128, 6, 128]), ALU.subtract), [gv, sm2], [gv])
                    vn = tm16()
                    op("dve", lambda h: h.tensor_tensor(vn.ap.rearrange("p (g c) -> p g c", c=128), gv3,
                                                        sm3[:, 6:12].unsqueeze(2).to_broadcast([128, 6, 128]), ALU.mult), [gv, sm3], [vn])
                    vns[sub] = vn

                def stB(sub):
                    vn = vns[sub]
                    pa, pb = psX(), psX()
                    for g in range(6):
                        dst = pa[:, g * 128:(g + 1) * 128] if g < 4 else pb[:, (g - 4) * 128:(g - 3) * 128]
                        op("pe", lambda h, g=g, dst=dst: h.matmul(dst, vn[:, g * 128:(g + 1) * 128], LCB[:, g, 0:128], start=True, stop=True),
                           [vn, LCB], [pa if g < 4 else pb])
                    op("dve", lambda h: h.tensor_tensor(TOKB[:, 0:4, sub * 128:(sub + 1) * 128], pa.ap.rearrange("p (g t) -> p g t", t=128),
                                                        LC[:, 0:512].rearrange("p (g t) -> p g t", t=128), ALU.add), [pa, LC], [TOKB])
                    op("dve", lambda h: h.tensor_tensor(TOKB[:, 4:6, sub * 128:(sub + 1) * 128], pb[:, 0:256].rearrange("p (g t) -> p g t", t=128),
                                                        LC[:, 512:768].rearrange("p (g t) -> p g t", t=128), ALU.add), [pb, LC], [TOKB])

                for f, a in ((stA, 0), (stA, 1), (stB, 0), (stA, 2), (stB, 1), (stA, 3), (stB, 2), (stB, 3)):
                    f(a)
                    yield
                if s == NS - 1:
                    ensure_w(nxt, "WA")
                yield from qx_gate_attn(xt, SG)
                if s == NS - 1 and nxt is not None:
                    ensure_w(nxt, first_wb(nxt))
                for j in range(6):
                    op("pool", lambda h, j=j: h.tensor_tensor(TOKB[:, j, :], TOKB[:, j, :], gus[j][:, 0:ST], ALU.mult), [TOKB, gus[j]], [TOKB])
                    op("dve", lambda h, j=j: h.tensor_tensor(SG[:, j, :], TOKB[:, j, :], SG[:, j, :], ALU.mult), [TOKB, SG], [SG])
                    yield

            def hook():
                ensure_w(nxt, "WA")
                if nxt is not None:
                    ensure_w(nxt, first_wb(nxt))
            pipeline(X, pi, last, 3, hook)

        def layer2(l, pi, last, nxt):
            ensure_w(l, "WA")
            op("dve", lambda h: h.memset(HALO[:, :, :], 0.0), [], [HALO])
            hg = halo_A(pi) if NR > 1 else None
            ensure_w(l, "WB")
            ensure_w(l, "WC")
            load_ln(l)
            op("pool", lambda h: h.memset(LCB[:, :, :], 0.0), [], [LCB])
            for g in range(4):
                r0 = 192 * g
                done = 0
                while done < 192:
                    r = r0 + done
                    ck, p0 = r // 128, r % 128
                    n = min(128 - p0, 192 - done)
                    dma("pool", LCB[p0:p0 + n, ck, 192 * g:192 * g + 192], c_wp_in[g, done:done + n, :], [c_wp_in], [LCB])
                    done += n
            dma("sp", LC[:, 0:6], c_sc_in[:, :], [c_sc_in], [LC])
            wkc_ring = Ring(wk32.bufs + tm32.bufs)
            overlap = {}
            for jd in range(6):
                gs = set(c // 192 for c in (128 * jd, 128 * jd + 127))
                overlap[jd] = [ci for ci in range(6) if set(r // 192 for r in (128 * ci, 128 * ci + 127)) & gs]

            def X(s, SG):
                xt = load_xT(pi, s)
                yield from qx_gate_attn(xt, SG, act_recip=True)
                if s == NS - 1 and nxt is not None:
                    ensure_w(nxt, first_wb(nxt))
                if s == 0 and hg is not None:
                    halo_B(hg)
                diffs = [None] * 6
                wkc = wkc_ring

                def cgen(j):
                    p = proj_fm(WA, j * 128, xt, psX)
                    pe_ = wkc()
                    op("dve", lambda h: h.tensor_copy(pe_[:, 0:16], HALO[:, j, :]), [HALO], [pe_])
                    op("act", lambda h: h.copy(pe_[:, 16:528], p[:, :]), [p], [pe_])
                    op("pool", lambda h: h.tensor_copy(HALO[:, j, :], pe_[:, 512:528]), [pe_], [HALO])
                    yield
                    g_lo, g_hi = (128 * j) // 192, (128 * j + 127) // 192

                    def shadd(eng, cur, prev, sh):
                        lo = 2 * sh - 1
                        op(eng, lambda h: h.tensor_tensor(cur[:, lo:528], prev[:, lo:528], prev[:, lo - sh:528 - sh], ALU.add), [prev], [cur])

                    def corr(t, k):
                        if s == 0:
                            op("dve", lambda h: h.tensor_tensor(t[:, 16:32], t[:, 16:32], RANKC[:, 16 * k:16 * k + 16], ALU.mult), [t, RANKC], [t])

                    ss = [pe_]
                    for k in range(g_hi + 1):
                        cur = wkc()
                        shadd("dve" if k == 0 else "pool", cur, ss[-1], 1 << k)
                        ss.append(cur)
                        if k % 2 == 1:
                            yield
                    yield
                    df = wk16()
                    ranges = [(0, 128, g_lo)] if g_lo == g_hi else [(0, 192 * g_hi - 128 * j, g_lo), (192 * g_hi - 128 * j, 128, g_hi)]
                    for (p0, p1, g) in ranges:
                        sk = ss[g + 1]
                        corr(sk, g)
                        op("dve", lambda h, p0=p0, p1=p1, g=g, sk=sk: h.scalar_tensor_tensor(df[p0:p1, :], sk[p0:p1, 16:528], 1.0 / (2 << g),
                                                                                         pe_[p0:p1, 16:528], ALU.mult, ALU.subtract), [sk, pe_], [df])
                    diffs[j] = df

                stagger((cgen(j) for j in range(6)), 3)
                if s == NS - 1:
                    ensure_w(nxt, "WA")
                yield
                for jd in range(6):
                    p = psX()
                    lst = overlap[jd]
                    for i, ci in enumerate(lst):
                        op("pe", lambda h, ci=ci, jd=jd, i=i, n=len(lst): h.matmul(p[:, :], LCB[:, ci, jd * 128:(jd + 1) * 128], diffs[ci][:, :],
                                                                                  start=(i == 0), stop=(i == n - 1)), [LCB, diffs[ci]], [p])
                    op("act", lambda h, jd=jd, p=p: h.activation(TOKB[:, jd, :], p[:, :], AF.Identity, scale=LC[:, jd:jd + 1]), [p, LC], [TOKB])
                    op("dve", lambda h, jd=jd: h.tensor_tensor(SG[:, jd, :], TOKB[:, jd, :], SG[:, jd, :], ALU.mult), [TOKB, SG], [SG])
                    yield

            def hook():
                ensure_w(nxt, "WA")
                if nxt is not None:
                    ensure_w(nxt, first_wb(nxt))
            pipeline(X, pi, last, 4, hook)

        def layer3(l, pi, last, nxt):
            ensure_w(l, "WA")
            dma("pool", LCB[:, :, 0:128], d_wgx_in[:, :, :], [d_wgx_in], [LCB])
            dma("pool", LCB[:, :, 128:256], d_wga_in[:, :, :], [d_wga_in], [LCB])
            dma("sp", LC[:, 0:24], d_cw_in.ap.rearrange("p h j -> p (h j)"), [d_cw_in], [LC])
            dma("sp", LC[:, 24:30], d_cb_in[:, :], [d_cb_in], [LC])
            dma("sp", LC[:, 30:36], d_bgx_in[:, :], [d_bgx_in], [LC])
            dma("sp", LC[:, 36:42], d_bga_in[:, :], [d_bga_in], [LC])
            dma("sp", LC[:, 42:48], d_ap_in[:, :], [d_ap_in], [LC])
            op("act", lambda h: h.activation(LC[:, 54:60], LC[:, 42:48], AF.Exp, scale=-1.0), [LC], [LC])
            op("dve", lambda h: h.tensor_scalar(LC[:, 54:60], LC[:, 54:60], 1.0, None, ALU.add), [LC], [LC])
            op("act", lambda h: h.activation(LC[:, 60:66], LC[:, 54:60], AF.Ln), [LC], [LC])
            op("dve", lambda h: h.tensor_scalar(LC[:, 48:54], LC[:, 60:66], -LRU_C, None, ALU.mult), [LC], [LC])
            op("dve", lambda h: h.memset(HALO[:, :, :], 0.0), [], [HALO])
            op("dve", lambda h: h.memset(HPREV[:, :], 0.0), [], [HPREV])
            op("dve", lambda h: h.memset(HPREV[:, 8:14], 1.0), [], [HPREV])
            op("dve", lambda h: h.memset(TOKB[:, 0, :], 0.0), [], [TOKB])
            if NR > 1:
                halo_B(halo_A(pi))
            ensure_w(l, "WB")
            ensure_w(l, "WC")
            load_ln(l)
            hloc = kb.dram([NS, 128, 6, ST], F32, "hloc")
            aloc = kb.dram([NS, 128, 6, ST], F32, "aloc")

            sgf = [SGs[i].ap.rearrange("p a b -> p (a b)").bitcast(F32) for i in range(2)]
            al_sg = [alias_bufs(SGs[i], [sgf[i][:, 512 * k:512 * (k + 1)] for k in range(4)], "sgal%d_" % i) for i in range(2)]
            al_tk = alias_bufs(TOKB, [TOKB.ap[:, k, :] for k in range(1, 6)], "tkal")
            r32 = Ring(al_sg[0] + al_sg[1] + tm32.bufs + ln32.bufs)
            rHA = Ring(al_tk)
            xts = {}

            def chunk_gen(s, j):
                if j == 0:
                    xts[s] = load_xT(pi, s)
                xt = xts[s]
                p = proj_fm(WA, j * 128, xt, psb)
                xb = wk32()
                op("dve", lambda h: h.tensor_copy(xb[:, 0:16], HALO[:, j, :]), [HALO], [xb])
                op("act", lambda h: h.copy(xb[:, 16:528], p[:, :]), [p], [xb])
                op("pool", lambda h: h.tensor_copy(HALO[:, j, :], xb[:, 512:528]), [xb], [HALO])
                yield
                xc = wk32()
                cw = lambda k: LC[:, j * 4 + k:j * 4 + k + 1]
                op("dve", lambda h: h.tensor_scalar(xc[:, 0:512], xb[:, 13:525], cw(0), LC[:, 24 + j:25 + j], ALU.mult, ALU.add), [xb, LC], [xc])
                op("dve", lambda h: h.scalar_tensor_tensor(xc[:, 0:512], xb[:, 14:526], cw(1), xc[:, 0:512], ALU.mult, ALU.add), [xb, LC, xc], [xc])
                op("dve", lambda h: h.scalar_tensor_tensor(xc[:, 0:512], xb[:, 15:527], cw(2), xc[:, 0:512], ALU.mult, ALU.add), [xb, LC, xc], [xc])
                op("dve", lambda h: h.scalar_tensor_tensor(xc[:, 0:512], xb[:, 16:528], cw(3), xc[:, 0:512], ALU.mult, ALU.add), [xb, LC, xc], [xc])
                yield
                xcb = wk16()
                op("act", lambda h: h.copy(xcb[:, :], xc[:, 0:512]), [xc], [xcb])
                pgx, pga = psb(), psb()
                op("pe", lambda h: h.matmul(pgx[:, 0:512], LCB[:, j, 0:128], xcb[:, :], start=True, stop=True), [LCB, xcb], [pgx])
                op("pe", lambda h: h.matmul(pga[:, 0:512], LCB[:, j, 128:256], xcb[:, :], start=True, stop=True), [LCB, xcb], [pga])
                yield
                gx, ga = r32(), r32()
                op("act", lambda h: h.activation(gx[:, 0:512], pgx[:, 0:512], AF.Sigmoid, bias=LC[:, 30 + j:31 + j], scale=1.0), [pgx, LC], [gx])
                op("act", lambda h: h.activation(ga[:, 0:512], pga[:, 0:512], AF.Sigmoid, bias=LC[:, 36 + j:37 + j], scale=1.0), [pga, LC], [ga])
                op("act", lambda h: h.activation(ga[:, 0:512], ga[:, 0:512], AF.Exp, scale=LC[:, 48 + j:49 + j]), [ga, LC], [ga])
                yield
                m = r32()
                op("dve", lambda h: h.tensor_tensor(m[:, 0:512], ga[:, 0:512], ga[:, 0:512], ALU.mult), [ga], [m])
                op("dve", lambda h: h.tensor_scalar(m[:, 0:512], m[:, 0:512], -1.0, 1.0, ALU.mult, ALU.add), [m], [m])
                op("dve", lambda h: h.tensor_tensor(gx[:, 0:512], gx[:, 0:512], xc[:, 0:512], ALU.mult), [gx, xc], [gx])
                yield
                op("act", lambda h: h.activation(m[:, 0:512], m[:, 0:512], AF.Sqrt), [m], [m])
                if s == 0:
                    t = small()
                    op("dve", lambda h: h.tensor_scalar(t[:, 0:1], m[:, 0:1], -1.0, 1.0, ALU.mult, ALU.add), [m], [t])
                    op("dve", lambda h: h.scalar_tensor_tensor(m[:, 0:1], t[:, 0:1], RANKC[:, 64:65], m[:, 0:1], ALU.mult, ALU.add), [t, RANKC, m], [m])
                yield
                op("pool", lambda h: h.tensor_tensor(gx[:, 0:512], gx[:, 0:512], m[:, 0:512], ALU.mult), [gx, m], [gx])
                yield
                H, A = rHA(), rHA()
                op("dve", lambda h: h.tensor_tensor_scan(H[:, :], ga[:, 0:512], gx[:, 0:512], HPREV[:, j:j + 1], ALU.mult, ALU.add), [ga, gx, HPREV], [H])
                op("dve", lambda h: h.tensor_copy(HPREV[:, j:j + 1], H[:, 511:512]), [H], [HPREV])
                op("dve", lambda h: h.tensor_tensor_scan(A[:, :], ga[:, 0:512], TOKB[:, 0, :], HPREV[:, 8 + j:9 + j], ALU.mult, ALU.add),
                   [ga, TOKB, HPREV], [A])
                op("dve", lambda h: h.tensor_copy(HPREV[:, 8 + j:9 + j], A[:, 511:512]), [A], [HPREV])
                dma("pool", hloc[s, :, j, :], H[:, :], [H], [hloc])
                dma("pool", aloc[s, :, j, :], A[:, :], [A], [aloc])

            stagger((chunk_gen(s, j) for s in range(NS) for j in range(6)), 4)
            for i in range(2):
                retire(al_sg[i], SGs[i])
            retire(al_tk, TOKB)
            hin = small()
            op("dve", lambda h: h.memset(hin[:, :], 0.0), [], [hin])
            if NR > 1:
                cdst = gather(HPREV[:, :], HPREV, 128, 16)
                G = tm32()
                dma("sp", G.ap[:, 0:NR * 16].rearrange("p (r c) -> p r c", c=16), cdst.ap.rearrange("(r p) c -> p r c", p=128), [cdst], [G])
                for j in range(NR - 1):
                    dp = small()
                    op("dve", lambda h, j=j, dp=dp: h.tensor_scalar(dp[:, 0:6], G[:, 16 * j + 8:16 * j + 14], RANKC[:, 65 + j:66 + j], RANKC[:, 69 + j:70 + j],
                                                                    ALU.mult, ALU.add), [G, RANKC], [dp])
                    op("dve", lambda h, j=j, dp=dp: h.tensor_scalar(dp[:, 6:12], G[:, 16 * j:16 * j + 6], RANKC[:, 65 + j:66 + j], None, ALU.mult), [G, RANKC], [dp])
                    op("dve", lambda h, dp=dp: h.tensor_tensor(hin[:, 0:6], hin[:, 0:6], dp[:, 0:6], ALU.mult), [hin, dp], [hin])
                    op("dve", lambda h, dp=dp: h.tensor_tensor(hin[:, 0:6], hin[:, 0:6], dp[:, 6:12], ALU.add), [hin, dp], [hin])
            op("dve", lambda h: h.tensor_copy(HPREV[:, 0:6], hin[:, 0:6]), [hin], [HPREV])

            def X(s, SG):
                xt = load_xT(pi, s)
                yield from qx_gate_attn(xt, SG)
                for j in range(6):
                    H, A = wk32(), wk32()
                    dma("sp", H[:, 0:512], hloc[s, :, j, :], [hloc], [H])
                    dma("sp", A[:, 0:512], aloc[s, :, j, :], [aloc], [A])
                    op("dve", lambda h, j=j, H=H, A=A: h.scalar_tensor_tensor(H[:, 0:512], A[:, 0:512], HPREV[:, j:j + 1], H[:, 0:512], ALU.mult, ALU.add),
                       [A, HPREV, H], [H])
                    op("dve", lambda h, j=j, H=H: h.tensor_tensor(SG[:, j, :], H[:, 0:512], SG[:, j, :], ALU.mult), [H, SG], [SG])
                    yield
            pipeline(X, pi, last, 2)

        def layer1(l, pi, last, nxt):
            oloc = kb.dram([NS, 128, 6, ST], F32, "oloc")
            qtl = kb.dram([NS, 128, 6, ST], BF16, "qtl")
            ensure_w(l, "WA")
            ensure_w(l, "WB1")
            ensure_w(l, "WC")
            SG = SGs[0]
            if NR > 1:
                op("dve", lambda h: h.memset(HPREV[:, 0:6], 0.0), [], [HPREV])
            flat = LCB.ap.rearrange("p a b -> p (a b)")
            S32 = [Buf(flat[:, 256 * h:256 * h + 256].bitcast(F32), "S32_%d" % h) for h in range(6)]
            SBF = [Buf(flat[:, 1536 + 128 * h:1536 + 128 * h + 128], "SBF_%d" % h) for h in range(6)]
            rqk = Ring([Buf(flat[:, 2304 + 256 * i:2304 + 256 * i + 256], "rqk_%d" % i) for i in range(3)]
                       + [Buf(flat[:, 3712 + 256 * i:3712 + 256 * i + 256], "rqk_%d" % (3 + i)) for i in range(2)])
            r16 = Ring([Buf(flat[:, 3072 + 128 * i:3072 + 128 * i + 128], "r16_%d" % i) for i in range(3)]
                       + [Buf(flat[:, 4224 + 128 * i:4224 + 128 * i + 128], "r16_%d" % (3 + i)) for i in range(3)])
            psU = [Ring(PB[0:3]), Ring(PB[3:6])]
            sg1f = SGs[1].ap.rearrange("p a b -> p (a b)")
            al_b = alias_bufs(SGs[1], [sg1f[:, 768 * k:768 * (k + 1)] for k in range(5)], "sg1al")
            tmA = Ring(tm32.bufs + ln32.bufs)
            tmB = Ring(tm16.bufs + al_b)
            ONESM = Buf(flat[:, 3456:3712].bitcast(F32), "ONESM")
            op("dve", lambda h: h.memset(LCB[:, :, :], 0.0), [], [LCB])
            for b_ in S32 + SBF + r16.bufs + rqk.bufs + [ONESM]:
                b_.lastw = LCB.lastw
            op("dve", lambda h: h.memset(ONESM[:, :], 1.0 / 128), [], [ONESM])
            es_ = []
            for k in range(4):
                t = tm32()
                dma("sp", t[:, :], hlb_in[k:k + 1, :].partition_broadcast(128), [hlb_in], [t])
                op("act", lambda h, t=t: h.activation(t[:, :], t[:, :], AF.Exp), [t], [t])
                es_.append(t)
            tot = tm32()
            op("dve", lambda h: h.tensor_tensor(tot[:, :], es_[0][:, :], es_[1][:, :], ALU.add), [es_[0], es_[1]], [tot])
            op("dve", lambda h: h.tensor_tensor(tot[:, :], tot[:, :], es_[2][:, :], ALU.add), [tot, es_[2]], [tot])
            op("dve", lambda h: h.tensor_tensor(tot[:, :], tot[:, :], es_[3][:, :], ALU.add), [tot, es_[3]], [tot])
            op("dve", lambda h: h.reciprocal(tot[:, :], tot[:, :]), [tot], [tot])
            for k in range(2, l + 1):
                op("dve", lambda h, k=k: h.tensor_tensor(es_[1][:, :], es_[1][:, :], es_[k][:, :], ALU.add), [es_[1], es_[k]], [es_[1]])
            op("dve", lambda h: h.tensor_tensor(LC[:, 0:768], es_[1][:, :], tot[:, :], ALU.mult), [es_[1], tot], [LC])
            op("dve", lambda h: h.tensor_scalar(LC[:, 768:1536], LC[:, 0:768], -1.0, 1.0, ALU.mult, ALU.add), [LC], [LC])
            def subtile_gen(s, sub, xt, par):
                    psb = psU[par]
                    pq1, pq2 = proj_tm(WB, 0, 512, xt, sub, psb), proj_tm(WB, 512, 256, xt, sub, psb)
                    QS = tmA()
                    op("act", lambda h: h.activation(QS[:, 0:512], pq1[:, :], AF.Silu), [pq1], [QS])
                    op("act", lambda h: h.activation(QS[:, 512:768], pq2[:, 0:256], AF.Silu), [pq2], [QS])
                    yield
                    pf1, pf2 = proj_tm(WA, 0, 512, xt, sub, psb), proj_tm(WA, 512, 256, xt, sub, psb)
                    FK = tmA()
                    op("act", lambda h: h.activation(FK[:, 0:512], pf1[:, :], AF.Sigmoid), [pf1], [FK])
                    op("act", lambda h: h.activation(FK[:, 512:768], pf2[:, 0:256], AF.Sigmoid), [pf2], [FK])
                    op("dve", lambda h: h.tensor_tensor(FK[:, 0:768], FK[:, 0:768], LC[:, 768:1536], ALU.mult), [FK, LC], [FK])
                    op("pool", lambda h: h.tensor_tensor(FK[:, 0:768], FK[:, 0:768], LC[:, 0:768], ALU.add), [FK, LC], [FK])
                    LOGF = tmA()
                    op("act", lambda h: h.activation(LOGF[:, 0:768], FK[:, 0:768], AF.Ln), [FK], [LOGF])
                    op("pool", lambda h: h.tensor_scalar(FK[:, 0:768], FK[:, 0:768], -1.0, 1.0, ALU.mult, ALU.add), [FK], [FK])
                    yield
                    pi1, pi2 = proj_tm(WA, 768, 512, xt, sub, psb), proj_tm(WA, 1280, 256, xt, sub, psb)
                    V = tmB()
                    op("act", lambda h: h.copy(V[:, 0:512], pi1[:, :]), [pi1], [V])
                    op("dve", lambda h: h.tensor_copy(V[:, 512:768], pi2[:, 0:256]), [pi2], [V])
                    yield
                    pg1, pg2 = psb(), psb()
                    op("pe", lambda h: h.matmul(pg1[:, :], CONST[:, C_TRI64:C_TRI64 + 128], LOGF[:, 0:512], start=True, stop=True), [CONST, LOGF], [pg1])
                    op("pe", lambda h: h.matmul(pg2[:, 0:256], CONST[:, C_TRI64:C_TRI64 + 128], LOGF[:, 512:768], start=True, stop=True), [CONST, LOGF], [pg2])
                    E1, E2 = tmA(), tmA()
                    op("act", lambda h: h.activation(E1[:, 0:512], pg1[:, :], AF.Exp), [pg1], [E1])
                    op("act", lambda h: h.activation(E1[:, 512:768], pg2[:, 0:256], AF.Exp), [pg2], [E1])
                    op("act", lambda h: h.activation(E2[:, 0:512], pg1[:, :], AF.Exp, scale=-1.0), [pg1], [E2])
                    op("act", lambda h: h.activation(E2[:, 512:768], pg2[:, 0:256], AF.Exp, scale=-1.0), [pg2], [E2])
                    QD, KI = tmB(), tmB()
                    op("dve", lambda h: h.tensor_tensor(QD[:, :], QS[:, 0:768], E1[:, 0:768], ALU.mult), [QS, E1], [QD])
                    op("pool", lambda h: h.tensor_tensor(KI[:, :], FK[:, 0:768], E2[:, 0:768], ALU.mult), [FK, E2], [KI])
                    yield
                    pr1, pr2 = psb(), psb()
                    op("pe", lambda h: h.matmul(pr1[:, :], CONST[:, C_TRIREV64:C_TRIREV64 + 128], LOGF[:, 0:512], start=True, stop=True), [CONST, LOGF], [pr1])
                    op("pe", lambda h: h.matmul(pr2[:, 0:256], CONST[:, C_TRIREV64:C_TRIREV64 + 128], LOGF[:, 512:768], start=True, stop=True), [CONST, LOGF], [pr2])
                    E3 = tmA()
                    op("act", lambda h: h.activation(E3[:, 0:512], pr1[:, :], AF.Exp), [pr1], [E3])
                    op("act", lambda h: h.activation(E3[:, 512:768], pr2[:, 0:256], AF.Exp), [pr2], [E3])
                    KE = tmB()
                    op("dve", lambda h: h.tensor_tensor(KE[:, :], FK[:, 0:768], E3[:, 0:768], ALU.mult), [FK, E3], [KE])
                    yield
                    pgl = psb()
                    for hd in range(6):
                        op("pe", lambda h, hd=hd: h.matmul(pgl[:, hd * 4:hd * 4 + 2], LOGF[:, hd * 128:(hd + 1) * 128], CONST[:, C_IND:C_IND + 2],
                                                           start=True, stop=True), [LOGF, CONST], [pgl])
                    DEC = small()
                    op("act", lambda h: h.activation(DEC[:, 0:12].rearrange("p (h c) -> p h c", c=2),
                                                     pgl[:, 0:24].rearrange("p (h f) -> p h f", f=4)[:, :, 0:2], AF.Exp), [pgl], [DEC])
                    if NR > 1:
                        gl, gs, cdec = small(), small(), small()
                        glv = gl.ap[:, 0:12].rearrange("p (h c) -> p h c", c=2)
                        op("dve", lambda h: h.tensor_copy(glv, pgl[:, 0:24].rearrange("p (h f) -> p h f", f=4)[:, :, 0:2]), [pgl], [gl])
                        op("dve", lambda h: h.tensor_copy(gs[:, 0:6], HPREV[:, 0:6]), [HPREV], [gs])
                        op("dve", lambda h: h.tensor_tensor(gs[:, 6:12], HPREV[:, 0:6], glv[:, :, 0], ALU.add), [HPREV, gl], [gs])
                        op("dve", lambda h: h.tensor_tensor(HPREV[:, 0:6], gs[:, 6:12], glv[:, :, 1], ALU.add), [gs, gl], [HPREV])
                        op("act", lambda h: h.activation(cdec[:, 0:12], gs[:, 0:12], AF.Exp), [gs], [cdec])
                    yield
                    for hd in range(6):
                        hc = slice(hd * 128, (hd + 1) * 128)
                        sl = (hd % 4) * 2
                        op("pe", lambda h: h.transpose(p6bf[:, sl, :], QD[:, hc], IDB[:, :]), [QD, IDB], [P6])
                        op("pe", lambda h: h.transpose(p6bf[:, sl + 1, :], KI[:, hc], IDB[:, :]), [KI, IDB], [P6])
                        QK, SCM = rqk(), r16()
                        op("act", lambda h: h.copy(QK[:, :], p6bf[:, sl:sl + 2, :].rearrange("p a b -> p (a b)")), [P6], [QK])
                        QDT, KIT = QK.ap[:, 0:128], QK.ap[:, 128:256]
                        if NR > 1:
                            for c in range(2):
                                op("dve", lambda h, c=c: h.tensor_scalar(SG[:, hd, sub * 128 + 64 * c:sub * 128 + 64 * c + 64], QDT[:, 64 * c:64 * c + 64],
                                                                         cdec[:, c * 6 + hd:c * 6 + hd + 1], None, ALU.mult), [QK, cdec], [SG])
                        yield
                        SC = psb()
                        op("pe", lambda h: h.matmul(SC[:, 0:128], KIT, QDT, start=True, stop=True), [QK], [SC])
                        op("dve", lambda h: h.tensor_tensor(SCM[:, :], SC[:, 0:128], CONST[:, C_TRI64:C_TRI64 + 128], ALU.mult), [SC, CONST], [SCM])
                        yield
                        OT = psb()
                        op("pe", lambda h: h.matmul(OT[:, 0:128], V[:, hc], SCM[:, :], start=True, stop=False), [V, SCM], [OT])
                        for c in range(2):
                            op("pe", lambda h, c=c: h.matmul(OT[:, 64 * c:64 * c + 64], SBF[hd][:, :], QDT[:, 64 * c:64 * c + 64],
                                                             start=False, stop=(c == 1)), [SBF[hd], QK], [OT])
                            U = psb()
                            op("pe", lambda h, c=c, U=U: h.matmul(U[:, 0:128], KE[64 * c:64 * c + 64, hc], V[64 * c:64 * c + 64, hc],
                                                                  start=True, stop=True), [KE, V], [U])
                            op("dve", lambda h, c=c, U=U: h.scalar_tensor_tensor(SBF[hd][:, :], S32[hd][:, :], DEC[:, hd * 2 + c:hd * 2 + c + 1], U[:, 0:128],
                                                                                 ALU.mult, ALU.add), [S32[hd], DEC, U], [SBF[hd]])
                            op("dve", lambda h, c=c, U=U: h.scalar_tensor_tensor(S32[hd][:, :], S32[hd][:, :], DEC[:, hd * 2 + c:hd * 2 + c + 1], U[:, 0:128],
                                                                                 ALU.mult, ALU.add), [S32[hd], DEC, U], [S32[hd]])
                        op("act", lambda h: h.copy(TOKB[:, hd, sub * 128:(sub + 1) * 128], OT[:, 0:128]), [OT], [TOKB])
                        yield

            for s in range(NS):
                xt = load_xT(pi, s)
                stagger((subtile_gen(s, sub, xt, sub % 2) for sub in range(4)), 2)
                dma("pool", oloc[s, :, :, :], TOKB[:, :, :], [TOKB], [oloc])
                if NR > 1:
                    dma("pool", qtl[s, :, :, :], SG[:, 0:6, :], [SG], [qtl])
            retire(al_b, SGs[1])
            if NR == 1:
                ensure_w(l, "WB")
                ensure_w(nxt, "WA")
            if NR > 1:
                tf = TOKB.ap.rearrange("p a b -> p (a b)")
                for hd in range(6):
                    op("dve", lambda h, hd=hd: h.tensor_copy(tf[:, hd * 128:(hd + 1) * 128], S32[hd][:, :]), [S32[hd]], [TOKB])
                op("act", lambda h: h.activation(tf[:, 768:774], HPREV[:, 0:6], AF.Exp), [HPREV], [TOKB])
                cdst = gather(tf[:, 0:774], TOKB, 128, 774)
                ensure_w(l, "WB")
                ensure_w(nxt, "WA")
                dma("sp", tf[:, 0:(NR - 1) * 774].rearrange("p (r c) -> p r c", c=774),
                    cdst.ap[0:(NR - 1) * 128, :].rearrange("(r p) c -> p r c", p=128), [cdst], [TOKB])
                for hd in range(6):
                    op("dve", lambda h, hd=hd: h.memset(S32[hd][:, :], 0.0), [], [S32[hd]])
                for j in range(NR - 1):
                    dp = small()
                    o = j * 774
                    op("dve", lambda h, j=j, o=o, dp=dp: h.tensor_scalar(dp[:, 0:6], tf[:, o + 768:o + 774], RANKC[:, 65 + j:66 + j], RANKC[:, 69 + j:70 + j],
                                                                         ALU.mult, ALU.add), [TOKB, RANKC], [dp])
                    op("dve", lambda h, j=j, o=o: h.tensor_scalar(tf[:, o:o + 768], tf[:, o:o + 768], RANKC[:, 65 + j:66 + j], None, ALU.mult), [TOKB, RANKC], [TOKB])
                    for hd in range(6):
                        op("dve", lambda h, hd=hd, o=o, dp=dp: h.scalar_tensor_tensor(S32[hd][:, :], S32[hd][:, :], dp[:, hd:hd + 1], tf[:, o + hd * 128:o + (hd + 1) * 128],
                                                                                      ALU.mult, ALU.add), [S32[hd], dp, TOKB], [S32[hd]])
                for hd in range(6):
                    op("act", lambda h, hd=hd: h.copy(SBF[hd][:, :], S32[hd][:, :]), [S32[hd]], [SBF[hd]])
            load_ln(l)
            dma("sp", LC[:, 0:6], b_ng_in[:, :], [b_ng_in], [LC])

            def X(s, SG):
                xt = load_xT(pi, s)
                dma("sp", TOKB[:, :, :], oloc[s, :, :, :], [oloc], [TOKB])
                yield from qx_gate_attn(xt, SG, act_recip=True)
                if NR > 1:
                    for hd in range(6):
                        qt = wk16()
                        dma("sp", qt[:, :], qtl[s, :, hd, :], [qtl], [qt])
                        pc = psX()
                        op("pe", lambda h, hd=hd, qt=qt, pc=pc: h.matmul(pc[:, :], SBF[hd][:, :], qt[:, :], start=True, stop=True), [SBF[hd], qt], [pc])
                        op("dve", lambda h, hd=hd, pc=pc: h.tensor_tensor(TOKB[:, hd, :], TOKB[:, hd, :], pc[:, :], ALU.add), [TOKB, pc], [TOKB])
                        yield
                if s == NS - 1 and nxt is not None:
                    ensure_w(nxt, first_wb(nxt))
                rs = {}
                for grp in ((0, 1, 2), (3, 4, 5)):
                    pms = {}
                    for hd in grp:
                        sq = wk32()
                        op("act", lambda h, hd=hd, sq=sq: h.activation(sq[:, 0:512], TOKB[:, hd, :], AF.Square), [TOKB], [sq])
                        pm = psX()
                        op("pe", lambda h, sq=sq, pm=pm: h.matmul(pm[:, :], ONESM[:, :], sq[:, 0:512], start=True, stop=True), [ONESM, sq], [pm])
                        rs[hd], pms[hd] = sq, pm
                    yield
                    for hd in grp:
                        op("act", lambda h, hd=hd: h.activation(rs[hd][:, 0:512], pms[hd][:, :], AF.Ln, bias=EPSC[:, 1:2], scale=1.0),
                           [pms[hd], EPSC], [rs[hd]])
                    yield
                for hd in range(6):
                    r = rs[hd]
                    op("act", lambda h, r=r: h.activation(r[:, 0:512], r[:, 0:512], AF.Exp, scale=-0.5), [r], [r])
                    op("dve", lambda h, hd=hd, r=r: h.scalar_tensor_tensor(r[:, 0:512], TOKB[:, hd, :], LC[:, hd:hd + 1], r[:, 0:512], ALU.mult, ALU.mult),
                       [TOKB, LC, r], [r])
                    op("pool", lambda h, hd=hd, r=r: h.tensor_tensor(SG[:, hd, :], r[:, 0:512], SG[:, hd, :], ALU.mult), [r, SG], [SG])
                    yield

            def hook():
                if nxt is not None:
                    ensure_w(nxt, first_wb(nxt))
            pipeline(X, pi, last, 3, hook, mul_eng="pool")

        layers = [layer0, layer1, layer2, layer3]
        for i, l in enumerate(ll):
            layers[l](l, i, i == len(ll) - 1, ll[i + 1] if i + 1 < len(ll) else None)
        kb.finish([out_d])
        print("instructions:", kb.ninstr)
    return nc


def host_inputs(inputs, T, NR, S):
    f = lambda a: np.ascontiguousarray(np.asarray(a, dtype=np.float32))
    x = f(inputs["x"])
    mem = f(inputs["mem"])
    consts = np.zeros((128, NCONST), np.float32)
    consts[:, 0:128] = np.eye(128, dtype=np.float32)
    idx = np.arange(128)
    consts[:, 128:256] = (idx[:, None] <= idx[None, :]).astype(np.float32)
    same = (idx[:, None] // 64) == (idx[None, :] // 64)
    consts[:, 256:384] = ((idx[:, None] <= idx[None, :]) & same).astype(np.float32)
    consts[:, 384:512] = ((idx[:, None] > idx[None, :]) & same).astype(np.float32)
    consts[0:64, 512] = 1.0
    consts[64:128, 513] = 1.0
    for j in range(6):
        for p in range(128):
            g = (j * 128 + p) // 192
            consts[p, 516 + j * 4 + g] = 1.0 / (2 ** (g + 1))
    common = {
        "mem_kv_w": f(inputs["mem_kv_w"]), "ln_g": f(inputs["ln_g"]), "ln_b": f(inputs["ln_b"]),
        "w_out": f(inputs["w_out"]), "hgrn_lb_logits": f(inputs["hgrn_lb_logits"]),
        "lbvec": np.zeros((2, TOK), np.float32),
        "a_w_in": f(inputs["a_w_in"][0]), "b_w_in": f(inputs["b_w_in"][0]),
        "c_w_in": f(inputs["c_w_in"][0]), "d_w_in": f(inputs["d_w_in"][0]),
        "a_w_sT": f(np.transpose(np.asarray(inputs["a_w_s"][0]), (2, 0, 1))),
        "a_b_s": f(np.asarray(inputs["a_b_s"][0]).reshape(1, TOK)),
        "b_norm_g": f(np.asarray(inputs["b_norm_g"][0]).reshape(6, 128).T),
        "c_w_pool": f(inputs["c_w_pool"][0]),
        "c_scale": f(np.asarray(inputs["c_scale"][0]).reshape(6, 128).T),
        "d_conv_w": f(np.transpose(np.asarray(inputs["d_conv_w"][0]).reshape(4, 6, 128), (2, 1, 0))),
        "d_conv_b": f(np.asarray(inputs["d_conv_b"][0]).reshape(6, 128).T),
        "d_w_gx": f(np.transpose(np.asarray(inputs["d_w_gx"][0]), (1, 0, 2))),
        "d_b_gx": f(np.asarray(inputs["d_b_gx"][0]).T),
        "d_w_ga": f(np.transpose(np.asarray(inputs["d_w_ga"][0]), (1, 0, 2))),
        "d_b_ga": f(np.asarray(inputs["d_b_ga"][0]).T),
        "d_a_param": f(np.asarray(inputs["d_a_param"][0]).reshape(6, 128).T),
        "consts": consts,
    }
    maps = []
    for c in range(8):
        b, r = (c // NR) % 2, c % NR
        m = dict(common)
        m["x"] = np.ascontiguousarray(x[b, r * T:(r + 1) * T, :])
        m["memT"] = np.ascontiguousarray(mem[b].T.reshape(8, 128, NMEM).transpose(1, 0, 2))
        rk = np.zeros((128, 128), np.float32)
        rk[:, 64] = 1.0 if r == 0 else 0.0
        for j in range(4):
            rk[:, 65 + j] = 1.0 if j < r else 0.0
            rk[:, 69 + j] = 0.0 if j < r else 1.0
        if r > 0:
            for t_ in range(16):
                rk[(r - 1) * 16 + t_, 80 + t_] = 1.0
        for k, wdw in enumerate((2, 4, 8, 16)):
            tt = np.arange(16)
            rk[:, 16 * k:16 * k + 16] = (wdw / np.minimum(tt + 1, wdw))[None, :] if r == 0 else 1.0
        m["rankc"] = rk
        maps.append(m)
    return maps


_NC_CACHE = {}


def run(inputs, S, NR, nlayers=4, layer_list=None):
    T = S // NR
    key = (T, NR, nlayers, tuple(layer_list) if layer_list else None)
    if key not in _NC_CACHE:
        _NC_CACHE[key] = build(T, NR, nlayers, layer_list)
    nc = _NC_CACHE[key]
    maps = host_inputs(inputs, T, NR, S)
    res = run_bass_kernel_spmd(nc, maps, core_ids=list(range(8)))
    out = np.zeros((2, S, D), np.float32)
    for c in range(8):
        b, r = (c // NR) % 2, c % NR
        if c < 2 * NR:
            out[b, r * T:(r + 1) * T, :] = res.results[c]["out"]
    return out


def kernel(**inputs):
    S = inputs["x"].shape[1]
    return run(inputs, S, 4)
```

```python
import contextlib
import numpy as np
import concourse.bass as bass
import concourse.mybir as mybir
from concourse.bass_utils import run_bass_kernel_spmd

F32 = mybir.dt.float32
BF16 = mybir.dt.bfloat16
AF = mybir.ActivationFunctionType
ALU = mybir.AluOpType
AX = mybir.AxisListType

D = 1024
NMEM = 256
TOK = 768
ALPHA = float((2 * 4) ** 0.25)
LN_EPS = 1e-5
RMS_EPS = 1e-6
LRU_C = 8.0
IN_W = [2816, 3584, 2048, 2048]
ST = 512
NCONST = 640
NCSB = 416


class Buf:
    __slots__ = ("ap", "name", "lastw", "readers", "psum", "gen")

    def __init__(self, ap, name, psum=False):
        self.ap = ap
        self.name = name
        self.psum = psum
        self.lastw = None
        self.readers = {}
        self.gen = 0

    def __getitem__(self, k):
        return self.ap[k]


class View:
    __slots__ = ("base", "gen")

    def __init__(self, base):
        base.gen += 1
        self.base, self.gen = base, base.gen

    def __getitem__(self, k):
        return self.base.ap[k]

    @property
    def ap(self):
        return self.base.ap

    @property
    def name(self):
        return self.base.name


class Eng:
    def __init__(self, name, h, sem):
        self.name, self.h, self.sem, self.cnt = name, h, sem, 0
        self.waited = {}


class KB:
    def __init__(self, nc, es):
        self.nc, self.es = nc, es
        self.eng = {}
        for name, h in (("pe", nc.tensor), ("act", nc.scalar), ("dve", nc.vector),
                        ("pool", nc.gpsimd), ("sp", nc.sync)):
            self.eng[name] = Eng(name, h, es.enter_context(nc.semaphore("s_" + name)))
        self.dsems = {}
        for q in ("sp", "pool"):
            self.dsems[q] = [[es.enter_context(nc.semaphore("d_%s%d" % (q, i))), 0] for i in range(20)]
        self.dnext = {"sp": 0, "pool": 0}
        self.ccsem = es.enter_context(nc.semaphore("s_cc"))
        self.cccnt = 0
        self.nb = 0
        self.ninstr = 0

    def sb(self, shape, dt, name=None):
        self.nb += 1
        name = name or "b%d" % self.nb
        t = self.es.enter_context(self.nc.sbuf_tensor(name, list(shape), dt))
        return Buf(t, name)

    def ps(self, shape, dt, name=None):
        self.nb += 1
        name = name or "p%d" % self.nb
        t = self.es.enter_context(self.nc.psum_tensor(name, list(shape), dt))
        return Buf(t, name, psum=True)

    def dram(self, shape, dt, name):
        t = self.nc.dram_tensor(name, list(shape), dt)
        return Buf(t.ap(), name)

    @staticmethod
    def _res(lst):
        out = []
        for x in lst:
            if isinstance(x, View):
                if x.gen != x.base.gen:
                    raise RuntimeError("stale ring buffer use: %s (gen %d, now %d)" % (x.base.name, x.gen, x.base.gen))
                x = x.base
            out.append(x)
        return out

    def _wait(self, e, tok):
        sem, val, sid = tok
        if e.waited.get(sid, 0) < val:
            e.h.wait_ge(sem, val)
            e.waited[sid] = val
            self.ninstr += 1

    def _deps(self, e, reads, writes):
        for b in reads:
            if b.lastw is not None:
                tok, src = b.lastw
                if not (src == "pe" and e.name == "pe"):
                    self._wait(e, tok)
            if b.psum:
                for key, (tok, src) in b.readers.items():
                    if src != e.name:
                        self._wait(e, tok)
        for b in writes:
            if b.lastw is not None:
                tok, src = b.lastw
                if not (src == "pe" and e.name == "pe"):
                    self._wait(e, tok)
            for key, (tok, src) in b.readers.items():
                if not (src == "pe" and e.name == "pe"):
                    self._wait(e, tok)

    def _commit(self, tok, src, reads, writes):
        for b in writes:
            b.lastw = (tok, src)
            b.readers = {}
        for b in reads:
            key = src if not src.startswith("dma") else "dma%d" % id(tok[0])
            b.readers[key] = (tok, src)

    def op(self, engname, fn, reads=(), writes=()):
        e = self.eng[engname]
        reads, writes = self._res(reads), self._res(writes)
        self._deps(e, reads, writes)
        ins = fn(e.h)
        e.cnt += 1
        ins.then_inc(e.sem, 1)
        self.ninstr += 1
        tok = (e.sem, e.cnt, id(e.sem))
        self._commit(tok, engname, reads, writes)
        return tok

    def dma(self, q, out_ap, in_ap, reads=(), writes=()):
        e = self.eng[q]
        reads, writes = self._res(reads), self._res(writes)
        self._deps(e, reads, writes)
        lst = self.dsems[q]
        i = self.dnext[q]
        self.dnext[q] = (i + 1) % len(lst)
        sem, tot = lst[i]
        if tot > 0:
            self._wait(e, (sem, tot, id(sem)))
        e.h.dma_start(out=out_ap, in_=in_ap).then_inc(sem, 16)
        self.ninstr += 1
        lst[i][1] = tot + 16
        tok = (sem, tot + 16, id(sem))
        self._commit(tok, "dma_" + q, reads, writes)
        return tok

    def allgather(self, groups, src, dst):
        e = self.eng["pool"]
        self._deps(e, [src], [dst])
        e.h.collective_compute("AllGather", ALU.bypass, groups, ins=[src.ap.opt()], outs=[dst.ap.opt()]).then_inc(self.ccsem)
        self.cccnt += 1
        self.ninstr += 1
        tok = (self.ccsem, self.cccnt, id(self.ccsem))
        self._commit(tok, "cc", [src], [dst])
        return tok

    def finish(self, bufs):
        e = self.eng["sp"]
        for b in bufs:
            if b.lastw is not None:
                self._wait(e, b.lastw[0])


class Ring:
    def __init__(self, bufs):
        self.bufs, self.i = bufs, 0

    def __call__(self):
        b = self.bufs[self.i]
        self.i = (self.i + 1) % len(self.bufs)
        return View(b)


def run_gen(g):
    for _ in g:
        pass


def stagger(gens, depth):
    it = iter(gens)
    active, more = [], True
    while True:
        if more and len(active) < depth:
            try:
                active.append(next(it))
            except StopIteration:
                more = False
        if not active:
            break
        for g in list(active):
            try:
                next(g)
            except StopIteration:
                active.remove(g)


def stagger_ticks(gens, depth):
    it = iter(gens)
    active, more = [], True
    while True:
        if more and len(active) < depth:
            try:
                active.append(next(it))
            except StopIteration:
                more = False
        if not active:
            break
        for g in list(active):
            try:
                next(g)
            except StopIteration:
                active.remove(g)
        yield


def alias_bufs(parent, aps, prefix):
    out = []
    for i, ap in enumerate(aps):
        b = Buf(ap, "%s%d" % (prefix, i))
        b.lastw = parent.lastw
        b.readers = dict(parent.readers)
        out.append(b)
    return out


def retire(aliases, parent):
    for a in aliases:
        if a.lastw is not None:
            parent.readers[a.name + "#w"] = a.lastw
        for key, v in a.readers.items():
            parent.readers[a.name + "#" + key] = v


def interleave(gx, gy, r):
    dx = dy = False
    while not (dx and dy):
        for _ in range(r):
            if not dx:
                try:
                    next(gx)
                except StopIteration:
                    dx = True
        if not dy:
            try:
                next(gy)
            except StopIteration:
                dy = True


def build(T, NR, nlayers=4, layer_list=None):
    assert T % ST == 0
    NS = T // ST
    nc = bass.Bass("TRN2", target_bir_lowering=False)
    es = contextlib.ExitStack()
    groups = [list(range(g * NR, (g + 1) * NR)) for g in range(8 // NR)]

    def din(name, shape):
        return Buf(nc.dram_tensor(name, list(shape), F32, kind="ExternalInput").ap(), name)

    x_in = din("x", [T, D])
    memT_in = din("memT", [128, 8, NMEM])
    kvw_in = din("mem_kv_w", [D, 512])
    lng_in = din("ln_g", [4, D])
    lnb_in = din("ln_b", [4, D])
    wout_in = din("w_out", [4, D, D])
    lb_in = din("lbvec", [2, TOK])
    hlb_in = din("hgrn_lb_logits", [4, TOK])
    w_in = [din("a_w_in", [D, IN_W[0]]), din("b_w_in", [D, IN_W[1]]),
            din("c_w_in", [D, IN_W[2]]), din("d_w_in", [D, IN_W[3]])]
    a_wsT_in = din("a_w_sT", [128, 6, 128])
    a_bs_in = din("a_b_s", [1, TOK])
    b_ng_in = din("b_norm_g", [128, 6])
    c_wp_in = din("c_w_pool", [4, 192, 192])
    c_sc_in = din("c_scale", [128, 6])
    d_cw_in = din("d_conv_w", [128, 6, 4])
    d_cb_in = din("d_conv_b", [128, 6])
    d_wgx_in = din("d_w_gx", [128, 6, 128])
    d_bgx_in = din("d_b_gx", [128, 6])
    d_wga_in = din("d_w_ga", [128, 6, 128])
    d_bga_in = din("d_b_ga", [128, 6])
    d_ap_in = din("d_a_param", [128, 6])
    consts_in = din("consts", [128, NCONST])
    rankc_in = din("rankc", [128, 128])
    out_d = Buf(nc.dram_tensor("out", [T, D], F32, kind="ExternalOutput").ap(), "out")

    with es:
        kb = KB(nc, es)
        op, dma = kb.op, kb.dma
        xres = [kb.dram([T, D], F32, "xres%d" % i) for i in range(2)]
        xT_d = [kb.dram([128, 8, T], BF16, "xT_d%d" % i) for i in range(2)]
        WA = kb.sb([128, 8, 1536], BF16, "WA")
        WB = kb.sb([128, 8, 1280], BF16, "WB")
        WC = kb.sb([128, 8, 1024], BF16, "WC")
        CONST = kb.sb([128, NCSB], F32, "CONST")
        RANKC = kb.sb([128, 128], F32, "RANKC")
        IDB = kb.sb([128, 128], BF16, "IDB")
        KTP = kb.sb([128, 4, NMEM], BF16, "KTP")
        VP = kb.sb([128, 2, 4, 128], BF16, "VP")
        OP_ = kb.sb([128, 2, 128], BF16, "OP")
        LNG = kb.sb([128, D], F32, "LNG")
        LNB = kb.sb([128, D], F32, "LNB")
        xT_t = Ring([kb.sb([128, 8, ST], BF16, "xT%d" % i) for i in range(2)])
        xst = Ring([kb.sb([128, 8, 128], BF16, "xst%d" % i) for i in range(2)])
        SGs = [kb.sb([128, 8, ST], BF16, "SG%d" % i) for i in range(2)]
        QT = kb.sb([128, 2, ST], BF16, "QT")
        wk32 = Ring([kb.sb([128, ST + 16], F32, "wk32_%d" % i) for i in range(10)])
        wk16 = Ring([kb.sb([128, ST], BF16, "wk16_%d" % i) for i in range(6)])
        ln32 = Ring([kb.sb([128, D], F32, "ln32_%d" % i) for i in range(4)])
        tm32 = Ring([kb.sb([128, TOK], F32, "tm32_%d" % i) for i in range(5)])
        tm16 = Ring([kb.sb([128, TOK], BF16, "tm16_%d" % i) for i in range(4)])
        xnb_r = Ring([kb.sb([128, D], BF16, "xnb%d" % i) for i in range(1)])
        small = Ring([kb.sb([128, 16], F32, "sm%d" % i) for i in range(18)])
        smallY = Ring([kb.sb([128, 24], F32, "smY%d" % i) for i in range(4)])
        LC = kb.sb([128, 1536], F32, "LC")
        LCB = kb.sb([128, 6, 768], BF16, "LCB")
        TOKB = kb.sb([128, 6, ST], F32, "TOKB")
        HALO = kb.sb([128, 6, 16], F32, "HALO")
        HPREV = kb.sb([128, 16], F32, "HPREV")
        EPSC = kb.sb([128, 4], F32, "EPSC")
        XTH = kb.sb([128, 8, 16], BF16, "XTH")
        print("sbuf remaining", nc.sbuf_bytes_remaining)
        PB = [kb.ps([128, 512], F32, "psb%d" % i) for i in range(7)]
        pst = kb.ps([128, 8, 128], BF16, "pst")
        psb = Ring(PB[0:6])
        psX = Ring(PB[0:3])
        psY = Ring(PB[3:7])
        P6 = PB[6]
        p6bf = P6.ap[:, :].bitcast(BF16).rearrange("p (a b) -> p a b", b=128)

        dma("sp", CONST[:, :], consts_in[:, 128:128 + NCSB], [consts_in], [CONST])
        dma("sp", RANKC[:, :], rankc_in[:, :], [rankc_in], [RANKC])
        dma("pool", IDB[:, :], consts_in[:, 0:128], [consts_in], [IDB])
        C_TRI, C_TRI64, C_TRIREV64, C_IND, C_CK = 0, 128, 256, 384, 388
        op("dve", lambda h: h.memset(EPSC[:, 0:1], LN_EPS), [], [EPSC])
        op("dve", lambda h: h.memset(EPSC[:, 1:2], RMS_EPS), [], [EPSC])
        op("dve", lambda h: h.memset(EPSC[:, 2:3], 1.0), [], [EPSC])
        op("dve", lambda h: h.memset(EPSC[:, 3:4], 0.0), [], [EPSC])

        def load_w(dst, src_ap, ncols, c0=0):
            for k in range(8):
                dma("pool", dst[:, k, c0:c0 + ncols], src_ap[k * 128:(k + 1) * 128, :], [], [dst])

        wspec = {
            0: {"WA": (WA, lambda: w_in[0][:, 0:1536], 1536), "WB": (WB, lambda: w_in[0][:, 1536:2816], 1280)},
            1: {"WA": (WA, lambda: w_in[1][:, 768:2304], 1536), "WB1": (WB, lambda: w_in[1][:, 0:768], 768),
                "WB": (WB, lambda: w_in[1][:, 2304:3584], 1280)},
            2: {"WA": (WA, lambda: w_in[2][:, 0:768], 768), "WB": (WB, lambda: w_in[2][:, 768:2048], 1280)},
            3: {"WA": (WA, lambda: w_in[3][:, 0:768], 768), "WB": (WB, lambda: w_in[3][:, 768:2048], 1280)},
        }
        for l_ in range(4):
            wspec[l_]["WC"] = (WC, (lambda l_=l_: wout_in[l_, :, :]), 1024)
        resident = {}

        def ensure_w(l, key):
            if l is None:
                return
            dst, src, ncols = wspec[l][key]
            if resident.get(dst.name) != (l, key):
                load_w(dst, src(), ncols)
                resident[dst.name] = (l, key)

        def first_wb(l):
            return "WB1" if l == 1 else "WB"

        MEMT = xT_t()
        dma("pool", MEMT[:, :, 0:NMEM], memT_in[:, :, :], [memT_in], [MEMT])
        load_w(WB, kvw_in.ap, 512)
        op("pool", lambda h: h.memset(KTP[:, :, :], 0.0), [], [KTP])
        op("pool", lambda h: h.memset(VP[:, :, :, :], 0.0), [], [VP])
        op("pool", lambda h: h.memset(OP_[:, :, :], 0.0), [], [OP_])
        op("pool", lambda h: h.memset(OP_[:, 0, 0:64], 1.0), [], [OP_])
        op("pool", lambda h: h.memset(OP_[:, 1, 64:128], 1.0), [], [OP_])
        for c in range(2):
            p = psb()
            for k in range(8):
                op("pe", lambda h, k=k, c=c, p=p: h.matmul(p[:, 0:NMEM], WB[:, k, c * 128:(c + 1) * 128], MEMT[:, k, 0:NMEM],
                                                           start=(k == 0), stop=(k == 7)), [WB, MEMT], [p])
            op("act", lambda h, c=c, p=p: h.copy(KTP[0:64, 2 * c, :], p[0:64, 0:NMEM]), [p], [KTP])
            op("act", lambda h, c=c, p=p: h.copy(KTP[64:128, 2 * c + 1, :], p[64:128, 0:NMEM]), [p], [KTP])
        for mc in range(2):
            p = psb()
            for k in range(8):
                op("pe", lambda h, k=k, mc=mc, p=p: h.matmul(p[:, 0:256], MEMT[:, k, mc * 128:(mc + 1) * 128], WB[:, k, 256:512],
                                                             start=(k == 0), stop=(k == 7)), [WB, MEMT], [p])
            for hd in range(4):
                o = (hd % 2) * 64
                op("act", lambda h, hd=hd, mc=mc, p=p, o=o: h.copy(VP[:, mc, hd, o:o + 64], p[:, hd * 64:(hd + 1) * 64]), [p], [VP])

        ll = layer_list if layer_list is not None else list(range(nlayers))

        def transposes_to_xT(xn_f32, xT_dst, t0):
            xb = xnb_r()
            op("act", lambda h: h.copy(xb[:, :], xn_f32[:, :]), [xn_f32], [xb])
            for k in range(8):
                op("pe", lambda h, k=k: h.transpose(pst[:, k, :], xb[:, k * 128:(k + 1) * 128], IDB[:, :]), [xb, IDB], [pst])
            stg = xst()
            op("dve", lambda h: h.tensor_copy(stg[:, :, :], pst[:, :, :]), [pst], [stg])
            dma("pool", xT_dst[:, :, t0:t0 + 128], stg[:, :, :], [stg], [xT_dst])

        def pro_unit(t0):
            xr = ln32()
            dma("sp", xr[:, :], x_in[t0:t0 + 128, :], [x_in], [xr])
            yield
            transposes_to_xT(xr, xT_d[0], t0)

        def prologue(s):
            stagger((pro_unit(s * ST + sub * 128) for sub in range(4)), 2)

        lazy_pro = (ll[0] in (0, 2))
        ensure_w(ll[0], "WA")
        for s in range(min(2, NS) if lazy_pro else NS):
            prologue(s)
        ensure_w(ll[0], first_wb(ll[0]))

        def load_ln(l):
            dma("sp", LNG[:, :], lng_in[l:l + 1, :].partition_broadcast(128), [], [LNG])
            dma("sp", LNB[:, :], lnb_in[l:l + 1, :].partition_broadcast(128), [], [LNB])

        def load_xT(pi, s):
            xt = xT_t()
            dma("sp", xt[:, :, :], xT_d[pi % 2][:, :, s * ST:(s + 1) * ST], [xT_d[pi % 2]], [xt])
            return xt

        def proj_fm(W, c0, xt, ring, n=ST):
            p = ring()
            for k in range(8):
                op("pe", lambda h, k=k: h.matmul(p[:, 0:n], W[:, k, c0:c0 + 128], xt[:, k, 0:n],
                                                 start=(k == 0), stop=(k == 7)), [W, xt], [p])
            return p

        def proj_tm(W, c0, ncols, xt, sub, ring):
            p = ring()
            for k in range(8):
                op("pe", lambda h, k=k: h.matmul(p[:, 0:ncols], xt[:, k, sub * 128:(sub + 1) * 128], W[:, k, c0:c0 + ncols],
                                                 start=(k == 0), stop=(k == 7)), [W, xt], [p])
            return p

        def qx_gate_attn(xt, SG, act_recip=False):
            for c in range(2):
                p = proj_fm(WB, c * 128, xt, psX)
                op("act", lambda h, c=c, p=p: h.copy(QT[:, c, :], p[:, :]), [p], [QT])
            yield

            def gates(js):
                for j in js:
                    p = proj_fm(WB, 256 + j * 128, xt, psX)
                    op("act", lambda h, j=j, p=p: h.activation(SG[:, j, :], p[:, :], AF.Silu), [p], [SG])
                    yield

            for pr, gj in ((0, (6, 7, 0, 1)), (1, (2, 3, 4, 5))):
                pts = []
                for hd in (2 * pr, 2 * pr + 1):
                    for mc in range(2):
                        sc = psX()
                        op("pe", lambda h, hd=hd, mc=mc, sc=sc: h.matmul(sc[:, :], KTP[:, hd, mc * 128:(mc + 1) * 128], QT[:, pr, :],
                                                                         start=True, stop=True), [KTP, QT], [sc])
                        pt = wk16()
                        op("act", lambda h, sc=sc, pt=pt: h.activation(pt[:, :], sc[:, :], AF.Exp, scale=0.125), [sc], [pt])
                        pts.append((hd, mc, pt))
                yield
                yield from gates(gj)
                num, den = psX(), psX()
                for i, (hd, mc, pt) in enumerate(pts):
                    op("pe", lambda h, hd=hd, mc=mc, pt=pt, i=i: h.matmul(num[:, :], VP[:, mc, hd, :], pt[:, :],
                                                                          start=(i == 0), stop=(i == 3)), [VP, pt], [num])
                for i, (hd, mc, pt) in enumerate(pts):
                    op("pe", lambda h, hd=hd, pt=pt, i=i: h.matmul(den[:, :], OP_[:, hd % 2, :], pt[:, :],
                                                                   start=(i == 0), stop=(i == 3)), [OP_, pt], [den])
                rec = wk32()
                if act_recip:
                    op("act", lambda h: h.activation(rec[:, 0:ST], den[:, :], AF.Ln), [den], [rec])
                    op("act", lambda h: h.activation(rec[:, 0:ST], rec[:, 0:ST], AF.Exp, scale=-1.0), [rec], [rec])
                else:
                    op("dve", lambda h: h.reciprocal(rec[:, 0:ST], den[:, :]), [den], [rec])
                xo = wk32()
                op("dve", lambda h: h.tensor_tensor(xo[:, 0:ST], num[:, :], rec[:, 0:ST], ALU.mult), [num, rec], [xo])
                op("dve", lambda h, pr=pr: h.tensor_tensor(SG[:, 6 + pr, :], xo[:, 0:ST], SG[:, 6 + pr, :], ALU.mult), [xo, SG], [SG])
                yield

        def Ygen(pi, s, last, SG):
            xsrc = x_in if pi == 0 else xres[pi % 2]
            xdst = out_d if last else xres[(pi + 1) % 2]

            def unit(sub):
                t0 = s * ST + sub * 128
                xr = ln32()
                dma("sp", xr[:, :], xsrc[t0:t0 + 128, :], [xsrc], [xr])
                ys = [psY(), psY()]
                for hf in range(2):
                    for e in range(8):
                        op("pe", lambda h, e=e, hf=hf: h.matmul(ys[hf][:, :], SG[:, e, sub * 128:(sub + 1) * 128], WC[:, e, hf * 512:(hf + 1) * 512],
                                                                start=(e == 0), stop=(e == 7)), [SG, WC], [ys[hf]])
                yield
                smy = smallY()
                st6 = View.__new__(View); st6.base, st6.gen = smy.base, smy.gen
                mv = smy
                for hf in range(2):
                    op("dve", lambda h, hf=hf: h.scalar_tensor_tensor(xr[:, hf * 512:(hf + 1) * 512], xr[:, hf * 512:(hf + 1) * 512], ALPHA,
                                                                      ys[hf][:, :], ALU.mult, ALU.add), [xr, ys[hf]], [xr])
                    op("dve", lambda h, hf=hf: h.bn_stats(st6[:, hf * 6:(hf + 1) * 6], xr[:, hf * 512:(hf + 1) * 512]), [xr], [st6])
                op("dve", lambda h: h.bn_aggr(mv[:, 16:18], st6[:, 0:12]), [st6], [mv])
                yield
                op("act", lambda h: h.activation(mv[:, 20:21], mv[:, 17:18], AF.Sqrt, bias=EPSC[:, 0:1], scale=1.0), [mv, EPSC], [mv])
                yield
                op("dve", lambda h: h.reciprocal(mv[:, 18:19], mv[:, 20:21]), [mv], [mv])
                op("dve", lambda h: h.scalar_tensor_tensor(mv[:, 19:20], mv[:, 16:17], -1.0, mv[:, 18:19], ALU.mult, ALU.mult), [mv], [mv])
                yield
                op("act", lambda h: h.activation(xr[:, :], xr[:, :], AF.Identity, bias=mv[:, 19:20], scale=mv[:, 18:19]), [xr, mv], [xr])
                yield
                op("dve", lambda h: h.tensor_tensor(xr[:, :], xr[:, :], LNG[:, :], ALU.mult), [xr, LNG], [xr])
                if not last:
                    xb = xnb_r()
                    op("dve", lambda h: h.tensor_tensor(xb[:, :], xr[:, :], LNB[:, :], ALU.add), [xr, LNB], [xb])
                yield
                op("pool", lambda h: h.tensor_tensor(xr[:, :], xr[:, :], LNB[:, :], ALU.add), [xr, LNB], [xr])
                dma("pool", xdst[t0:t0 + 128, :], xr[:, :], [xr], [xdst])
                if not last:
                    for k in range(8):
                        op("pe", lambda h, k=k: h.transpose(pst[:, k, :], xb[:, k * 128:(k + 1) * 128], IDB[:, :]), [xb, IDB], [pst])
                    yield
                    stg = xst()
                    op("act", lambda h: h.copy(stg[:, :, :], pst[:, :, :]), [pst], [stg])
                    dma("pool", xT_d[(pi + 1) % 2][:, :, t0:t0 + 128], stg[:, :, :], [stg], [xT_d[(pi + 1) % 2]])

            yield from stagger_ticks((unit(sub) for sub in range(4)), 4)

        def pipeline(X, pi, last, r, hook=None):
            run_gen(X(0, SGs[0]))
            for s in range(1, NS):
                if pi == 0 and lazy_pro and s + 1 < NS:
                    prologue(s + 1)
                interleave(X(s, SGs[s % 2]), Ygen(pi, s - 1, last, SGs[(s - 1) % 2]), r)
            if hook is not None:
                hook()
            run_gen(Ygen(pi, NS - 1, last, SGs[(NS - 1) % 2]))

        cc_n = [0]

        def gather(src_sb_ap, src_buf, rows, cols):
            cc_n[0] += 1
            csrc = kb.dram([rows, cols], F32, "cc_src%d" % cc_n[0])
            cdst = kb.dram([NR * rows, cols], F32, "cc_dst%d" % cc_n[0])
            dma("pool", csrc[:, :], src_sb_ap, [src_buf], [csrc])
            kb.allgather(groups, csrc, cdst)
            return cdst

        def halo_A(pi):
            xsrc = x_in if pi == 0 else xres[pi % 2]
            xl = ln32()
            dma("sp", xl[0:16, :], xsrc[T - 16:T, :], [xsrc], [xl])
            cdst = gather(xl[0:16, :], xl, 16, D)
            g = ln32()
            dma("sp", g[0:NR * 16, :], cdst[:, :], [cdst], [g])
            return g

        def halo_B(g):
            ph = psb()
            for k in range(8):
                op("pe", lambda h, k=k: h.matmul(ph[:, k * 16:(k + 1) * 16], g[0:NR * 16, k * 128:(k + 1) * 128], RANKC[0:NR * 16, 80:96],
                                                 start=True, stop=True), [g, RANKC], [ph])
            op("act", lambda h: h.copy(XTH.ap.rearrange("p k t -> p (k t)"), ph[:, 0:128]), [ph], [XTH])
            for j in range(6):
                p = psb()
                for k in range(8):
                    op("pe", lambda h, k=k, j=j, p=p: h.matmul(p[:, 0:16], WA[:, k, j * 128:(j + 1) * 128], XTH[:, k, :],
                                                               start=(k == 0), stop=(k == 7)), [WA, XTH], [p])
                op("act", lambda h, j=j, p=p: h.copy(HALO[:, j, :], p[:, 0:16]), [p], [HALO])

        def layer0(l, pi, last, nxt):
            ensure_w(l, "WA")
            ensure_w(l, "WB")
            ensure_w(l, "WC")
            load_ln(l)
            WST32 = tm32()
            dma("sp", WST32[:, :], a_wsT_in.ap.rearrange("p g t -> p (g t)"), [a_wsT_in], [WST32])
            for g in range(6):
                op("dve", lambda h, g=g: h.tensor_tensor(LCB[:, g, 0:128], WST32[:, g * 128:(g + 1) * 128], CONST[:, C_TRI:C_TRI + 128], ALU.mult),
                   [WST32, CONST], [LCB])
            dma("sp", LC[:, 0:TOK], a_bs_in[0:1, :].partition_broadcast(128), [a_bs_in], [LC])

            def X(s, SG):
                xt = load_xT(pi, s)
                gus = []
                for j in range(6):
                    p = proj_fm(WA, j * 128, xt, psX)
                    gu = wk32()
                    op("act", lambda h, p=p, gu=gu: h.activation(gu[:, 0:ST], p[:, :], AF.Gelu_apprx_tanh), [p], [gu])
                    gus.append(gu)
                    yield
                vns = {}

                def stA(sub):
                    p1 = proj_tm(WA, 768, 512, xt, sub, psX)
                    p2 = proj_tm(WA, 1280, 256, xt, sub, psX)
                    gv, sq = tm32(), tm32()
                    op("act", lambda h: h.activation(gv[:, 0:512], p1[:, :], AF.Gelu_apprx_tanh), [p1], [gv])
                    op("act", lambda h: h.activation(gv[:, 512:768], p2[:, 0:256], AF.Gelu_apprx_tanh), [p2], [gv])
                    op("act", lambda h: h.activation(sq[:, :], gv[:, :], AF.Square), [gv], [sq])
                    sm = small()
                    op("dve", lambda h: h.reduce_sum(sm[:, 0:6], gv.ap.rearrange("p (g c) -> p g c", c=128), axis=AX.X), [gv], [sm])
                    op("dve", lambda h: h.reduce_sum(sm[:, 6:12], sq.ap.rearrange("p (g c) -> p g c", c=128), axis=AX.X), [sq], [sm])
                    sm2 = small()
                    op("dve", lambda h: h.tensor_scalar(sm2[:, 0:6], sm[:, 0:6], 1.0 / 128, None, ALU.mult), [sm], [sm2])
                    op("dve", lambda h: h.tensor_tensor(sm2[:, 6:12], sm2[:, 0:6], sm2[:, 0:6], ALU.mult), [sm2], [sm2])
                    sm3 = small()
                    op("dve", lambda h: h.scalar_tensor_tensor(sm3[:, 0:6], sm[:, 6:12], 1.0 / 128, sm2[:, 6:12], ALU.mult, ALU.subtract), [sm, sm2], [sm3])
                    sm4 = small()
                    op("act", lambda h: h.activation(sm4[:, 0:6], sm3[:, 0:6], AF.Sqrt, bias=EPSC[:, 0:1], scale=1.0), [sm3, EPSC], [sm4])
                    op("dve", lambda h: h.reciprocal(sm3[:, 6:12], sm4[:, 0:6]), [sm4], [sm3])
                    gv3 = gv.ap.rearrange("p (g c) -> p g c", c=128)
                    op("dve", lambda h: h.tensor_tensor(gv3, gv3, sm2[:, 0:6].unsqueeze(2).to_broadcast([128, 6, 128]), ALU.subtract), [gv, sm2], [gv])
                    vn = tm16()
                    op("dve", lambda h: h.tensor_tensor(vn.ap.rearrange("p (g c) -> p g c", c=128), gv3,
                                                        sm3[:, 6:12].unsqueeze(2).to_broadcast([128, 6, 128]), ALU.mult), [gv, sm3], [vn])
                    vns[sub] = vn

                def stB(sub):
                    vn = vns[sub]
                    pa, pb = psX(), psX()
                    for g in range(6):
                        dst = pa[:, g * 128:(g + 1) * 128] if g < 4 else pb[:, (g - 4) * 128:(g - 3) * 128]
                        op("pe", lambda h, g=g, dst=dst: h.matmul(dst, vn[:, g * 128:(g + 1) * 128], LCB[:, g, 0:128], start=True, stop=True),
                           [vn, LCB], [pa if g < 4 else pb])
                    op("dve", lambda h: h.tensor_tensor(TOKB[:, 0:4, sub * 128:(sub + 1) * 128], pa.ap.rearrange("p (g t) -> p g t", t=128),
                                                        LC[:, 0:512].rearrange("p (g t) -> p g t", t=128), ALU.add), [pa, LC], [TOKB])
                    op("dve", lambda h: h.tensor_tensor(TOKB[:, 4:6, sub * 128:(sub + 1) * 128], pb[:, 0:256].rearrange("p (g t) -> p g t", t=128),
                                                        LC[:, 512:768].rearrange("p (g t) -> p g t", t=128), ALU.add), [pb, LC], [TOKB])

                for f, a in ((stA, 0), (stA, 1), (stB, 0), (stA, 2), (stB, 1), (stA, 3), (stB, 2), (stB, 3)):
                    f(a)
                    yield
                if s == NS - 1:
                    ensure_w(nxt, "WA")
                yield from qx_gate_attn(xt, SG)
                if s == NS - 1 and nxt is not None:
                    ensure_w(nxt, first_wb(nxt))
                for j in range(6):
                    op("pool", lambda h, j=j: h.tensor_tensor(TOKB[:, j, :], TOKB[:, j, :], gus[j][:, 0:ST], ALU.mult), [TOKB, gus[j]], [TOKB])
                    op("dve", lambda h, j=j: h.tensor_tensor(SG[:, j, :], TOKB[:, j, :], SG[:, j, :], ALU.mult), [TOKB, SG], [SG])
                    yield

            def hook():
                ensure_w(nxt, "WA")
                if nxt is not None:
                    ensure_w(nxt, first_wb(nxt))
            pipeline(X, pi, last, 3, hook)

        def layer2(l, pi, last, nxt):
            ensure_w(l, "WA")
            op("dve", lambda h: h.memset(HALO[:, :, :], 0.0), [], [HALO])
            hg = halo_A(pi) if NR > 1 else None
            ensure_w(l, "WB")
            ensure_w(l, "WC")
            load_ln(l)
            op("pool", lambda h: h.memset(LCB[:, :, :], 0.0), [], [LCB])
            for g in range(4):
                r0 = 192 * g
                done = 0
                while done < 192:
                    r = r0 + done
                    ck, p0 = r // 128, r % 128
                    n = min(128 - p0, 192 - done)
                    dma("pool", LCB[p0:p0 + n, ck, 192 * g:192 * g + 192], c_wp_in[g, done:done + n, :], [c_wp_in], [LCB])
                    done += n
            dma("sp", LC[:, 0:6], c_sc_in[:, :], [c_sc_in], [LC])
            wkc_ring = Ring(wk32.bufs + tm32.bufs)
            overlap = {}
            for jd in range(6):
                gs = set(c // 192 for c in (128 * jd, 128 * jd + 127))
                overlap[jd] = [ci for ci in range(6) if set(r // 192 for r in (128 * ci, 128 * ci + 127)) & gs]

            def X(s, SG):
                xt = load_xT(pi, s)
                yield from qx_gate_attn(xt, SG, act_recip=True)
                if s == NS - 1 and nxt is not None:
                    ensure_w(nxt, first_wb(nxt))
                if s == 0 and hg is not None:
                    halo_B(hg)
                diffs = [None] * 6
                wkc = wkc_ring

                def cgen(j):
                    p = proj_fm(WA, j * 128, xt, psX)
                    pe_ = wkc()
                    op("dve", lambda h: h.tensor_copy(pe_[:, 0:16], HALO[:, j, :]), [HALO], [pe_])
                    op("act", lambda h: h.copy(pe_[:, 16:528], p[:, :]), [p], [pe_])
                    op("pool", lambda h: h.tensor_copy(HALO[:, j, :], pe_[:, 512:528]), [pe_], [HALO])
                    yield
                    g_lo, g_hi = (128 * j) // 192, (128 * j + 127) // 192

                    def shadd(eng, cur, prev, sh):
                        lo = 2 * sh - 1
                        op(eng, lambda h: h.tensor_tensor(cur[:, lo:528], prev[:, lo:528], prev[:, lo - sh:528 - sh], ALU.add), [prev], [cur])

                    def corr(t, k):
                        if s == 0:
                            op("dve", lambda h: h.tensor_tensor(t[:, 16:32], t[:, 16:32], RANKC[:, 16 * k:16 * k + 16], ALU.mult), [t, RANKC], [t])

                    ss = [pe_]
                    for k in range(g_hi + 1):
                        cur = wkc()
                        shadd("dve" if k == 0 else "pool", cur, ss[-1], 1 << k)
                        ss.append(cur)
                        if k % 2 == 1:
                            yield
                    yield
                    df = wk16()
                    ranges = [(0, 128, g_lo)] if g_lo == g_hi else [(0, 192 * g_hi - 128 * j, g_lo), (192 * g_hi - 128 * j, 128, g_hi)]
                    for (p0, p1, g) in ranges:
                        sk = ss[g + 1]
                        corr(sk, g)
                        op("dve", lambda h, p0=p0, p1=p1, g=g, sk=sk: h.scalar_tensor_tensor(df[p0:p1, :], sk[p0:p1, 16:528], 1.0 / (2 << g),
                                                                                         pe_[p0:p1, 16:528], ALU.mult, ALU.subtract), [sk, pe_], [df])
                    diffs[j] = df

                stagger((cgen(j) for j in range(6)), 3)
                if s == NS - 1:
                    ensure_w(nxt, "WA")
                yield
                for jd in range(6):
                    p = psX()
                    lst = overlap[jd]
                    for i, ci in enumerate(lst):
                        op("pe", lambda h, ci=ci, jd=jd, i=i, n=len(lst): h.matmul(p[:, :], LCB[:, ci, jd * 128:(jd + 1) * 128], diffs[ci][:, :],
                                                                                  start=(i == 0), stop=(i == n - 1)), [LCB, diffs[ci]], [p])
                    op("act", lambda h, jd=jd, p=p: h.activation(TOKB[:, jd, :], p[:, :], AF.Identity, scale=LC[:, jd:jd + 1]), [p, LC], [TOKB])
                    op("dve", lambda h, jd=jd: h.tensor_tensor(SG[:, jd, :], TOKB[:, jd, :], SG[:, jd, :], ALU.mult), [TOKB, SG], [SG])
                    yield

            def hook():
                ensure_w(nxt, "WA")
                if nxt is not None:
                    ensure_w(nxt, first_wb(nxt))
            pipeline(X, pi, last, 4, hook)

        def layer3(l, pi, last, nxt):
            ensure_w(l, "WA")
            dma("pool", LCB[:, :, 0:128], d_wgx_in[:, :, :], [d_wgx_in], [LCB])
            dma("pool", LCB[:, :, 128:256], d_wga_in[:, :, :], [d_wga_in], [LCB])
            dma("sp", LC[:, 0:24], d_cw_in.ap.rearrange("p h j -> p (h j)"), [d_cw_in], [LC])
            dma("sp", LC[:, 24:30], d_cb_in[:, :], [d_cb_in], [LC])
            dma("sp", LC[:, 30:36], d_bgx_in[:, :], [d_bgx_in], [LC])
            dma("sp", LC[:, 36:42], d_bga_in[:, :], [d_bga_in], [LC])
            dma("sp", LC[:, 42:48], d_ap_in[:, :], [d_ap_in], [LC])
            op("act", lambda h: h.activation(LC[:, 54:60], LC[:, 42:48], AF.Exp, scale=-1.0), [LC], [LC])
            op("dve", lambda h: h.tensor_scalar(LC[:, 54:60], LC[:, 54:60], 1.0, None, ALU.add), [LC], [LC])
            op("act", lambda h: h.activation(LC[:, 60:66], LC[:, 54:60], AF.Ln), [LC], [LC])
            op("dve", lambda h: h.tensor_scalar(LC[:, 48:54], LC[:, 60:66], -LRU_C, None, ALU.mult), [LC], [LC])
            op("dve", lambda h: h.memset(HALO[:, :, :], 0.0), [], [HALO])
            op("dve", lambda h: h.memset(HPREV[:, :], 0.0), [], [HPREV])
            op("dve", lambda h: h.memset(HPREV[:, 8:14], 1.0), [], [HPREV])
            op("dve", lambda h: h.memset(TOKB[:, 0, :], 0.0), [], [TOKB])
            if NR > 1:
                halo_B(halo_A(pi))
            ensure_w(l, "WB")
            ensure_w(l, "WC")
            load_ln(l)
            hloc = kb.dram([NS, 128, 6, ST], F32, "hloc")
            aloc = kb.dram([NS, 128, 6, ST], F32, "aloc")

            sgf = [SGs[i].ap.rearrange("p a b -> p (a b)").bitcast(F32) for i in range(2)]
            al_sg = [alias_bufs(SGs[i], [sgf[i][:, 512 * k:512 * (k + 1)] for k in range(4)], "sgal%d_" % i) for i in range(2)]
            al_tk = alias_bufs(TOKB, [TOKB.ap[:, k, :] for k in range(1, 6)], "tkal")
            r32 = Ring(al_sg[0] + al_sg[1] + tm32.bufs + ln32.bufs)
            rHA = Ring(al_tk)
            xts = {}

            def chunk_gen(s, j):
                if j == 0:
                    xts[s] = load_xT(pi, s)
                xt = xts[s]
                p = proj_fm(WA, j * 128, xt, psb)
                xb = wk32()
                op("dve", lambda h: h.tensor_copy(xb[:, 0:16], HALO[:, j, :]), [HALO], [xb])
                op("act", lambda h: h.copy(xb[:, 16:528], p[:, :]), [p], [xb])
                op("pool", lambda h: h.tensor_copy(HALO[:, j, :], xb[:, 512:528]), [xb], [HALO])
                yield
                xc = wk32()
                cw = lambda k: LC[:, j * 4 + k:j * 4 + k + 1]
                op("dve", lambda h: h.tensor_scalar(xc[:, 0:512], xb[:, 13:525], cw(0), LC[:, 24 + j:25 + j], ALU.mult, ALU.add), [xb, LC], [xc])
                op("dve", lambda h: h.scalar_tensor_tensor(xc[:, 0:512], xb[:, 14:526], cw(1), xc[:, 0:512], ALU.mult, ALU.add), [xb, LC, xc], [xc])
                op("dve", lambda h: h.scalar_tensor_tensor(xc[:, 0:512], xb[:, 15:527], cw(2), xc[:, 0:512], ALU.mult, ALU.add), [xb, LC, xc], [xc])
                op("dve", lambda h: h.scalar_tensor_tensor(xc[:, 0:512], xb[:, 16:528], cw(3), xc[:, 0:512], ALU.mult, ALU.add), [xb, LC, xc], [xc])
                yield
                xcb = wk16()
                op("act", lambda h: h.copy(xcb[:, :], xc[:, 0:512]), [xc], [xcb])
                pgx, pga = psb(), psb()
                op("pe", lambda h: h.matmul(pgx[:, 0:512], LCB[:, j, 0:128], xcb[:, :], start=True, stop=True), [LCB, xcb], [pgx])
                op("pe", lambda h: h.matmul(pga[:, 0:512], LCB[:, j, 128:256], xcb[:, :], start=True, stop=True), [LCB, xcb], [pga])
                yield
                gx, ga = r32(), r32()
                op("act", lambda h: h.activation(gx[:, 0:512], pgx[:, 0:512], AF.Sigmoid, bias=LC[:, 30 + j:31 + j], scale=1.0), [pgx, LC], [gx])
                op("act", lambda h: h.activation(ga[:, 0:512], pga[:, 0:512], AF.Sigmoid, bias=LC[:, 36 + j:37 + j], scale=1.0), [pga, LC], [ga])
                op("act", lambda h: h.activation(ga[:, 0:512], ga[:, 0:512], AF.Exp, scale=LC[:, 48 + j:49 + j]), [ga, LC], [ga])
                yield
                m = r32()
                op("dve", lambda h: h.tensor_tensor(m[:, 0:512], ga[:, 0:512], ga[:, 0:512], ALU.mult), [ga], [m])
                op("dve", lambda h: h.tensor_scalar(m[:, 0:512], m[:, 0:512], -1.0, 1.0, ALU.mult, ALU.add), [m], [m])
                op("dve", lambda h: h.tensor_tensor(gx[:, 0:512], gx[:, 0:512], xc[:, 0:512], ALU.mult), [gx, xc], [gx])
                yield
                op("act", lambda h: h.activation(m[:, 0:512], m[:, 0:512], AF.Sqrt), [m], [m])
                if s == 0:
                    t = small()
                    op("dve", lambda h: h.tensor_scalar(t[:, 0:1], m[:, 0:1], -1.0, 1.0, ALU.mult, ALU.add), [m], [t])
                    op("dve", lambda h: h.scalar_tensor_tensor(m[:, 0:1], t[:, 0:1], RANKC[:, 64:65], m[:, 0:1], ALU.mult, ALU.add), [t, RANKC, m], [m])
                yield
                op("pool", lambda h: h.tensor_tensor(gx[:, 0:512], gx[:, 0:512], m[:, 0:512], ALU.mult), [gx, m], [gx])
                yield
                H, A = rHA(), rHA()
                op("dve", lambda h: h.tensor_tensor_scan(H[:, :], ga[:, 0:512], gx[:, 0:512], HPREV[:, j:j + 1], ALU.mult, ALU.add), [ga, gx, HPREV], [H])
                op("dve", lambda h: h.tensor_copy(HPREV[:, j:j + 1], H[:, 511:512]), [H], [HPREV])
                op("dve", lambda h: h.tensor_tensor_scan(A[:, :], ga[:, 0:512], TOKB[:, 0, :], HPREV[:, 8 + j:9 + j], ALU.mult, ALU.add),
                   [ga, TOKB, HPREV], [A])
                op("dve", lambda h: h.tensor_copy(HPREV[:, 8 + j:9 + j], A[:, 511:512]), [A], [HPREV])
                dma("pool", hloc[s, :, j, :], H[:, :], [H], [hloc])
                dma("pool", aloc[s, :, j, :], A[:, :], [A], [aloc])

            stagger((chunk_gen(s, j) for s in range(NS) for j in range(6)), 4)
            for i in range(2):
                retire(al_sg[i], SGs[i])
            retire(al_tk, TOKB)
            hin = small()
            op("dve", lambda h: h.memset(hin[:, :], 0.0), [], [hin])
            if NR > 1:
                cdst = gather(HPREV[:, :], HPREV, 128, 16)
                G = tm32()
                dma("sp", G.ap[:, 0:NR * 16].rearrange("p (r c) -> p r c", c=16), cdst.ap.rearrange("(r p) c -> p r c", p=128), [cdst], [G])
                for j in range(NR - 1):
                    dp = small()
                    op("dve", lambda h, j=j, dp=dp: h.tensor_scalar(dp[:, 0:6], G[:, 16 * j + 8:16 * j + 14], RANKC[:, 65 + j:66 + j], RANKC[:, 69 + j:70 + j],
                                                                    ALU.mult, ALU.add), [G, RANKC], [dp])
                    op("dve", lambda h, j=j, dp=dp: h.tensor_scalar(dp[:, 6:12], G[:, 16 * j:16 * j + 6], RANKC[:, 65 + j:66 + j], None, ALU.mult), [G, RANKC], [dp])
                    op("dve", lambda h, dp=dp: h.tensor_tensor(hin[:, 0:6], hin[:, 0:6], dp[:, 0:6], ALU.mult), [hin, dp], [hin])
                    op("dve", lambda h, dp=dp: h.tensor_tensor(hin[:, 0:6], hin[:, 0:6], dp[:, 6:12], ALU.add), [hin, dp], [hin])
            op("dve", lambda h: h.tensor_copy(HPREV[:, 0:6], hin[:, 0:6]), [hin], [HPREV])

            def X(s, SG):
                xt = load_xT(pi, s)
                yield from qx_gate_attn(xt, SG)
                for j in range(6):
                    H, A = wk32(), wk32()
                    dma("sp", H[:, 0:512], hloc[s, :, j, :], [hloc], [H])
                    dma("sp", A[:, 0:512], aloc[s, :, j, :], [aloc], [A])
                    op("dve", lambda h, j=j, H=H, A=A: h.scalar_tensor_tensor(H[:, 0:512], A[:, 0:512], HPREV[:, j:j + 1], H[:, 0:512], ALU.mult, ALU.add),
                       [A, HPREV, H], [H])
                    op("dve", lambda h, j=j, H=H: h.tensor_tensor(SG[:, j, :], H[:, 0:512], SG[:, j, :], ALU.mult), [H, SG], [SG])
                    yield
            pipeline(X, pi, last, 2)

        def layer1(l, pi, last, nxt):
            oloc = kb.dram([NS, 128, 6, ST], F32, "oloc")
            qtl = kb.dram([NS, 128, 6, ST], BF16, "qtl")
            ensure_w(l, "WA")
            ensure_w(l, "WB1")
            ensure_w(l, "WC")
            SG = SGs[0]
            if NR > 1:
                op("dve", lambda h: h.memset(HPREV[:, 0:6], 0.0), [], [HPREV])
            flat = LCB.ap.rearrange("p a b -> p (a b)")
            S32 = [Buf(flat[:, 256 * h:256 * h + 256].bitcast(F32), "S32_%d" % h) for h in range(6)]
            SBF = [Buf(flat[:, 1536 + 128 * h:1536 + 128 * h + 128], "SBF_%d" % h) for h in range(6)]
            rqk = Ring([Buf(flat[:, 2304 + 256 * i:2304 + 256 * i + 256], "rqk_%d" % i) for i in range(3)]
                       + [Buf(flat[:, 3712 + 256 * i:3712 + 256 * i + 256], "rqk_%d" % (3 + i)) for i in range(2)])
            r16 = Ring([Buf(flat[:, 3072 + 128 * i:3072 + 128 * i + 128], "r16_%d" % i) for i in range(3)]
                       + [Buf(flat[:, 4224 + 128 * i:4224 + 128 * i + 128], "r16_%d" % (3 + i)) for i in range(3)])
            psU = [Ring(PB[0:3]), Ring(PB[3:6])]
            sg1f = SGs[1].ap.rearrange("p a b -> p (a b)")
            al_b = alias_bufs(SGs[1], [sg1f[:, 768 * k:768 * (k + 1)] for k in range(5)], "sg1al")
            tmA = Ring(tm32.bufs + ln32.bufs)
            tmB = Ring(tm16.bufs + al_b)
            ONESM = Buf(flat[:, 3456:3712].bitcast(F32), "ONESM")
            op("dve", lambda h: h.memset(LCB[:, :, :], 0.0), [], [LCB])
            for b_ in S32 + SBF + r16.bufs + rqk.bufs + [ONESM]:
                b_.lastw = LCB.lastw
            op("dve", lambda h: h.memset(ONESM[:, :], 1.0 / 128), [], [ONESM])
            es_ = []
            for k in range(4):
                t = tm32()
                dma("sp", t[:, :], hlb_in[k:k + 1, :].partition_broadcast(128), [hlb_in], [t])
                op("act", lambda h, t=t: h.activation(t[:, :], t[:, :], AF.Exp), [t], [t])
                es_.append(t)
            tot = tm32()
            op("dve", lambda h: h.tensor_tensor(tot[:, :], es_[0][:, :], es_[1][:, :], ALU.add), [es_[0], es_[1]], [tot])
            op("dve", lambda h: h.tensor_tensor(tot[:, :], tot[:, :], es_[2][:, :], ALU.add), [tot, es_[2]], [tot])
            op("dve", lambda h: h.tensor_tensor(tot[:, :], tot[:, :], es_[3][:, :], ALU.add), [tot, es_[3]], [tot])
            op("dve", lambda h: h.reciprocal(tot[:, :], tot[:, :]), [tot], [tot])
            for k in range(2, l + 1):
                op("dve", lambda h, k=k: h.tensor_tensor(es_[1][:, :], es_[1][:, :], es_[k][:, :], ALU.add), [es_[1], es_[k]], [es_[1]])
            op("dve", lambda h: h.tensor_tensor(LC[:, 0:768], es_[1][:, :], tot[:, :], ALU.mult), [es_[1], tot], [LC])
            op("dve", lambda h: h.tensor_scalar(LC[:, 768:1536], LC[:, 0:768], -1.0, 1.0, ALU.mult, ALU.add), [LC], [LC])
            def subtile_gen(s, sub, xt, par):
                    psb = psU[par]
                    pq1, pq2 = proj_tm(WB, 0, 512, xt, sub, psb), proj_tm(WB, 512, 256, xt, sub, psb)
                    QS = tmA()
                    op("act", lambda h: h.activation(QS[:, 0:512], pq1[:, :], AF.Silu), [pq1], [QS])
                    op("act", lambda h: h.activation(QS[:, 512:768], pq2[:, 0:256], AF.Silu), [pq2], [QS])
                    yield
                    pf1, pf2 = proj_tm(WA, 0, 512, xt, sub, psb), proj_tm(WA, 512, 256, xt, sub, psb)
                    FK = tmA()
                    op("act", lambda h: h.activation(FK[:, 0:512], pf1[:, :], AF.Sigmoid), [pf1], [FK])
                    op("act", lambda h: h.activation(FK[:, 512:768], pf2[:, 0:256], AF.Sigmoid), [pf2], [FK])
                    op("dve", lambda h: h.tensor_tensor(FK[:, 0:768], FK[:, 0:768], LC[:, 768:1536], ALU.mult), [FK, LC], [FK])
                    op("pool", lambda h: h.tensor_tensor(FK[:, 0:768], FK[:, 0:768], LC[:, 0:768], ALU.add), [FK, LC], [FK])
                    LOGF = tmA()
                    op("act", lambda h: h.activation(LOGF[:, 0:768], FK[:, 0:768], AF.Ln), [FK], [LOGF])
                    op("pool", lambda h: h.tensor_scalar(FK[:, 0:768], FK[:, 0:768], -1.0, 1.0, ALU.mult, ALU.add), [FK], [FK])
                    yield
                    pi1, pi2 = proj_tm(WA, 768, 512, xt, sub, psb), proj_tm(WA, 1280, 256, xt, sub, psb)
                    V = tmB()
                    op("act", lambda h: h.copy(V[:, 0:512], pi1[:, :]), [pi1], [V])
                    op("dve", lambda h: h.tensor_copy(V[:, 512:768], pi2[:, 0:256]), [pi2], [V])
                    yield
                    pg1, pg2 = psb(), psb()
                    op("pe", lambda h: h.matmul(pg1[:, :], CONST[:, C_TRI64:C_TRI64 + 128], LOGF[:, 0:512], start=True, stop=True), [CONST, LOGF], [pg1])
                    op("pe", lambda h: h.matmul(pg2[:, 0:256], CONST[:, C_TRI64:C_TRI64 + 128], LOGF[:, 512:768], start=True, stop=True), [CONST, LOGF], [pg2])
                    E1, E2 = tmA(), tmA()
                    op("act", lambda h: h.activation(E1[:, 0:512], pg1[:, :], AF.Exp), [pg1], [E1])
                    op("act", lambda h: h.activation(E1[:, 512:768], pg2[:, 0:256], AF.Exp), [pg2], [E1])
                    op("act", lambda h: h.activation(E2[:, 0:512], pg1[:, :], AF.Exp, scale=-1.0), [pg1], [E2])
                    op("act", lambda h: h.activation(E2[:, 512:768], pg2[:, 0:256], AF.Exp, scale=-1.0), [pg2], [E2])
                    QD, KI = tmB(), tmB()
                    op("dve", lambda h: h.tensor_tensor(QD[:, :], QS[:, 0:768], E1[:, 0:768], ALU.mult), [QS, E1], [QD])
                    op("pool", lambda h: h.tensor_tensor(KI[:, :], FK[:, 0:768], E2[:, 0:768], ALU.mult), [FK, E2], [KI])
                    yield
                    pr1, pr2 = psb(), psb()
                    op("pe", lambda h: h.matmul(pr1[:, :], CONST[:, C_TRIREV64:C_TRIREV64 + 128], LOGF[:, 0:512], start=True, stop=True), [CONST, LOGF], [pr1])
                    op("pe", lambda h: h.matmul(pr2[:, 0:256], CONST[:, C_TRIREV64:C_TRIREV64 + 128], LOGF[:, 512:768], start=True, stop=True), [CONST, LOGF], [pr2])
                    E3 = tmA()
                    op("act", lambda h: h.activation(E3[:, 0:512], pr1[:, :], AF.Exp), [pr1], [E3])
                    op("act", lambda h: h.activation(E3[:, 512:768], pr2[:, 0:256], AF.Exp), [pr2], [E3])
                    KE = tmB()
                    op("dve", lambda h: h.tensor_tensor(KE[:, :], FK[:, 0:768], E3[:, 0:768], ALU.mult), [FK, E3], [KE])
                    yield
                    pgl = psb()
                    for hd in range(6):
                        op("pe", lambda h, hd=hd: h.matmul(pgl[:, hd * 4:hd * 4 + 2], LOGF[:, hd * 128:(hd + 1) * 128], CONST[:, C_IND:C_IND + 2],
                                                           start=True, stop=True), [LOGF, CONST], [pgl])
                    DEC = small()
                    op("act", lambda h: h.activation(DEC[:, 0:12].rearrange("p (h c) -> p h c", c=2),
                                                     pgl[:, 0:24].rearrange("p (h f) -> p h f", f=4)[:, :, 0:2], AF.Exp), [pgl], [DEC])
                    if NR > 1:
                        gl, gs, cdec = small(), small(), small()
                        glv = gl.ap[:, 0:12].rearrange("p (h c) -> p h c", c=2)
                        op("dve", lambda h: h.tensor_copy(glv, pgl[:, 0:24].rearrange("p (h f) -> p h f", f=4)[:, :, 0:2]), [pgl], [gl])
                        op("dve", lambda h: h.tensor_copy(gs[:, 0:6], HPREV[:, 0:6]), [HPREV], [gs])
                        op("dve", lambda h: h.tensor_tensor(gs[:, 6:12], HPREV[:, 0:6], glv[:, :, 0], ALU.add), [HPREV, gl], [gs])
                        op("dve", lambda h: h.tensor_tensor(HPREV[:, 0:6], gs[:, 6:12], glv[:, :, 1], ALU.add), [gs, gl], [HPREV])
                        op("act", lambda h: h.activation(cdec[:, 0:12], gs[:, 0:12], AF.Exp), [gs], [cdec])
                    yield
                    for hd in range(6):
                        hc = slice(hd * 128, (hd + 1) * 128)
                        sl = (hd % 4) * 2
                        op("pe", lambda h: h.transpose(p6bf[:, sl, :], QD[:, hc], IDB[:, :]), [QD, IDB], [P6])
                        op("pe", lambda h: h.transpose(p6bf[:, sl + 1, :], KI[:, hc], IDB[:, :]), [KI, IDB], [P6])
                        QK, SCM = rqk(), r16()
                        op("act", lambda h: h.copy(QK[:, :], p6bf[:, sl:sl + 2, :].rearrange("p a b -> p (a b)")), [P6], [QK])
                        QDT, KIT = QK.ap[:, 0:128], QK.ap[:, 128:256]
                        if NR > 1:
                            for c in range(2):
                                op("dve", lambda h, c=c: h.tensor_scalar(SG[:, hd, sub * 128 + 64 * c:sub * 128 + 64 * c + 64], QDT[:, 64 * c:64 * c + 64],
                                                                         cdec[:, c * 6 + hd:c * 6 + hd + 1], None, ALU.mult), [QK, cdec], [SG])
                        yield
                        SC = psb()
                        op("pe", lambda h: h.matmul(SC[:, 0:128], KIT, QDT, start=True, stop=True), [QK], [SC])
                        op("dve", lambda h: h.tensor_tensor(SCM[:, :], SC[:, 0:128], CONST[:, C_TRI64:C_TRI64 + 128], ALU.mult), [SC, CONST], [SCM])
                        yield
                        OT = psb()
                        op("pe", lambda h: h.matmul(OT[:, 0:128], V[:, hc], SCM[:, :], start=True, stop=False), [V, SCM], [OT])
                        for c in range(2):
                            op("pe", lambda h, c=c: h.matmul(OT[:, 64 * c:64 * c + 64], SBF[hd][:, :], QDT[:, 64 * c:64 * c + 64],
                                                             start=False, stop=(c == 1)), [SBF[hd], QK], [OT])
                            U = psb()
                            op("pe", lambda h, c=c, U=U: h.matmul(U[:, 0:128], KE[64 * c:64 * c + 64, hc], V[64 * c:64 * c + 64, hc],
                                                                  start=True, stop=True), [KE, V], [U])
                            op("dve", lambda h, c=c, U=U: h.scalar_tensor_tensor(SBF[hd][:, :], S32[hd][:, :], DEC[:, hd * 2 + c:hd * 2 + c + 1], U[:, 0:128],
                                                                                 ALU.mult, ALU.add), [S32[hd], DEC, U], [SBF[hd]])
                            op("dve", lambda h, c=c, U=U: h.scalar_tensor_tensor(S32[hd][:, :], S32[hd][:, :], DEC[:, hd * 2 + c:hd * 2 + c + 1], U[:, 0:128],
                                                                                 ALU.mult, ALU.add), [S32[hd], DEC, U], [S32[hd]])
                        op("act", lambda h: h.copy(TOKB[:, hd, sub * 128:(sub + 1) * 128], OT[:, 0:128]), [OT], [TOKB])
                        yield

            for s in range(NS):
                xt = load_xT(pi, s)
                stagger((subtile_gen(s, sub, xt, sub % 2) for sub in range(4)), 2)
                dma("pool", oloc[s, :, :, :], TOKB[:, :, :], [TOKB], [oloc])
                if NR > 1:
                    dma("pool", qtl[s, :, :, :], SG[:, 0:6, :], [SG], [qtl])
            retire(al_b, SGs[1])
            if NR == 1:
                ensure_w(l, "WB")
                ensure_w(nxt, "WA")
            if NR > 1:
                tf = TOKB.ap.rearrange("p a b -> p (a b)")
                for hd in range(6):
                    op("dve", lambda h, hd=hd: h.tensor_copy(tf[:, hd * 128:(hd + 1) * 128], S32[hd][:, :]), [S32[hd]], [TOKB])
                op("act", lambda h: h.activation(tf[:, 768:774], HPREV[:, 0:6], AF.Exp), [HPREV], [TOKB])
                cdst = gather(tf[:, 0:774], TOKB, 128, 774)
                ensure_w(l, "WB")
                ensure_w(nxt, "WA")
                dma("sp", tf[:, 0:(NR - 1) * 774].rearrange("p (r c) -> p r c", c=774),
                    cdst.ap[0:(NR - 1) * 128, :].rearrange("(r p) c -> p r c", p=128), [cdst], [TOKB])
                for hd in range(6):
                    op("dve", lambda h, hd=hd: h.memset(S32[hd][:, :], 0.0), [], [S32[hd]])
                for j in range(NR - 1):
                    dp = small()
                    o = j * 774
                    op("dve", lambda h, j=j, o=o, dp=dp: h.tensor_scalar(dp[:, 0:6], tf[:, o + 768:o + 774], RANKC[:, 65 + j:66 + j], RANKC[:, 69 + j:70 + j],
                                                                         ALU.mult, ALU.add), [TOKB, RANKC], [dp])
                    op("dve", lambda h, j=j, o=o: h.tensor_scalar(tf[:, o:o + 768], tf[:, o:o + 768], RANKC[:, 65 + j:66 + j], None, ALU.mult), [TOKB, RANKC], [TOKB])
                    for hd in range(6):
                        op("dve", lambda h, hd=hd, o=o, dp=dp: h.scalar_tensor_tensor(S32[hd][:, :], S32[hd][:, :], dp[:, hd:hd + 1], tf[:, o + hd * 128:o + (hd + 1) * 128],
                                                                                      ALU.mult, ALU.add), [S32[hd], dp, TOKB], [S32[hd]])
                for hd in range(6):
                    op("act", lambda h, hd=hd: h.copy(SBF[hd][:, :], S32[hd][:, :]), [S32[hd]], [SBF[hd]])
            load_ln(l)
            dma("sp", LC[:, 0:6], b_ng_in[:, :], [b_ng_in], [LC])

            def X(s, SG):
                xt = load_xT(pi, s)
                dma("sp", TOKB[:, :, :], oloc[s, :, :, :], [oloc], [TOKB])
                yield from qx_gate_attn(xt, SG, act_recip=True)
                if NR > 1:
                    for hd in range(6):
                        qt = wk16()
                        dma("sp", qt[:, :], qtl[s, :, hd, :], [qtl], [qt])
                        pc = psX()
                        op("pe", lambda h, hd=hd, qt=qt, pc=pc: h.matmul(pc[:, :], SBF[hd][:, :], qt[:, :], start=True, stop=True), [SBF[hd], qt], [pc])
                        op("dve", lambda h, hd=hd, pc=pc: h.tensor_tensor(TOKB[:, hd, :], TOKB[:, hd, :], pc[:, :], ALU.add), [TOKB, pc], [TOKB])
                        yield
                if s == NS - 1 and nxt is not None:
                    ensure_w(nxt, first_wb(nxt))
                rs = {}
                for grp in ((0, 1, 2), (3, 4, 5)):
                    pms = {}
                    for hd in grp:
                        sq = wk32()
                        op("act", lambda h, hd=hd, sq=sq: h.activation(sq[:, 0:512], TOKB[:, hd, :], AF.Square), [TOKB], [sq])
                        pm = psX()
                        op("pe", lambda h, sq=sq, pm=pm: h.matmul(pm[:, :], ONESM[:, :], sq[:, 0:512], start=True, stop=True), [ONESM, sq], [pm])
                        rs[hd], pms[hd] = sq, pm
                    yield
                    for hd in grp:
                        op("act", lambda h, hd=hd: h.activation(rs[hd][:, 0:512], pms[hd][:, :], AF.Ln, bias=EPSC[:, 1:2], scale=1.0),
                           [pms[hd], EPSC], [rs[hd]])
                    yield
                for hd in range(6):
                    r = rs[hd]
                    op("act", lambda h, r=r: h.activation(r[:, 0:512], r[:, 0:512], AF.Exp, scale=-0.5), [r], [r])
                    op("dve", lambda h, hd=hd, r=r: h.scalar_tensor_tensor(r[:, 0:512], TOKB[:, hd, :], LC[:, hd:hd + 1], r[:, 0:512], ALU.mult, ALU.mult),
                       [TOKB, LC, r], [r])
                    op("dve", lambda h, hd=hd, r=r: h.tensor_tensor(SG[:, hd, :], r[:, 0:512], SG[:, hd, :], ALU.mult), [r, SG], [SG])
                    yield

            def hook():
                if nxt is not None:
                    ensure_w(nxt, first_wb(nxt))
            pipeline(X, pi, last, 3, hook)

        layers = [layer0, layer1, layer2, layer3]
        for i, l in enumerate(ll):
            layers[l](l, i, i == len(ll) - 1, ll[i + 1] if i + 1 < len(ll) else None)
        kb.finish([out_d])
        print("instructions:", kb.ninstr)
    return nc


def host_inputs(inputs, T, NR, S):
    f = lambda a: np.ascontiguousarray(np.asarray(a, dtype=np.float32))
    x = f(inputs["x"])
    mem = f(inputs["mem"])
    consts = np.zeros((128, NCONST), np.float32)
    consts[:, 0:128] = np.eye(128, dtype=np.float32)
    idx = np.arange(128)
    consts[:, 128:256] = (idx[:, None] <= idx[None, :]).astype(np.float32)
    same = (idx[:, None] // 64) == (idx[None, :] // 64)
    consts[:, 256:384] = ((idx[:, None] <= idx[None, :]) & same).astype(np.float32)
    consts[:, 384:512] = ((idx[:, None] > idx[None, :]) & same).astype(np.float32)
    consts[0:64, 512] = 1.0
    consts[64:128, 513] = 1.0
    for j in range(6):
        for p in range(128):
            g = (j * 128 + p) // 192
            consts[p, 516 + j * 4 + g] = 1.0 / (2 ** (g + 1))
    common = {
        "mem_kv_w": f(inputs["mem_kv_w"]), "ln_g": f(inputs["ln_g"]), "ln_b": f(inputs["ln_b"]),
        "w_out": f(inputs["w_out"]), "hgrn_lb_logits": f(inputs["hgrn_lb_logits"]),
        "lbvec": np.zeros((2, TOK), np.float32),
        "a_w_in": f(inputs["a_w_in"][0]), "b_w_in": f(inputs["b_w_in"][0]),
        "c_w_in": f(inputs["c_w_in"][0]), "d_w_in": f(inputs["d_w_in"][0]),
        "a_w_sT": f(np.transpose(np.asarray(inputs["a_w_s"][0]), (2, 0, 1))),
        "a_b_s": f(np.asarray(inputs["a_b_s"][0]).reshape(1, TOK)),
        "b_norm_g": f(np.asarray(inputs["b_norm_g"][0]).reshape(6, 128).T),
        "c_w_pool": f(inputs["c_w_pool"][0]),
        "c_scale": f(np.asarray(inputs["c_scale"][0]).reshape(6, 128).T),
        "d_conv_w": f(np.transpose(np.asarray(inputs["d_conv_w"][0]).reshape(4, 6, 128), (2, 1, 0))),
        "d_conv_b": f(np.asarray(inputs["d_conv_b"][0]).reshape(6, 128).T),
        "d_w_gx": f(np.transpose(np.asarray(inputs["d_w_gx"][0]), (1, 0, 2))),
        "d_b_gx": f(np.asarray(inputs["d_b_gx"][0]).T),
        "d_w_ga": f(np.transpose(np.asarray(inputs["d_w_ga"][0]), (1, 0, 2))),
        "d_b_ga": f(np.asarray(inputs["d_b_ga"][0]).T),
        "d_a_param": f(np.asarray(inputs["d_a_param"][0]).reshape(6, 128).T),
        "consts": consts,
    }
    maps = []
    for c in range(8):
        b, r = (c // NR) % 2, c % NR
        m = dict(common)
        m["x"] = np.ascontiguousarray(x[b, r * T:(r + 1) * T, :])
        m["memT"] = np.ascontiguousarray(mem[b].T.reshape(8, 128, NMEM).transpose(1, 0, 2))
        rk = np.zeros((128, 128), np.float32)
        rk[:, 64] = 1.0 if r == 0 else 0.0
        for j in range(4):
            rk[:, 65 + j] = 1.0 if j < r else 0.0
            rk[:, 69 + j] = 0.0 if j < r else 1.0
        if r > 0:
            for t_ in range(16):
                rk[(r - 1) * 16 + t_, 80 + t_] = 1.0
        for k, wdw in enumerate((2, 4, 8, 16)):
            tt = np.arange(16)
            rk[:, 16 * k:16 * k + 16] = (wdw / np.minimum(tt + 1, wdw))[None, :] if r == 0 else 1.0
        m["rankc"] = rk
        maps.append(m)
    return maps


_NC_CACHE = {}


def run(inputs, S, NR, nlayers=4, layer_list=None):
    T = S // NR
    key = (T, NR, nlayers, tuple(layer_list) if layer_list else None)
    if key not in _NC_CACHE:
        _NC_CACHE[key] = build(T, NR, nlayers, layer_list)
    nc = _NC_CACHE[key]
    maps = host_inputs(inputs, T, NR, S)
    res = run_bass_kernel_spmd(nc, maps, core_ids=list(range(8)))
    out = np.zeros((2, S, D), np.float32)
    for c in range(8):
        b, r = (c // NR) % 2, c % NR
        if c < 2 * NR:
            out[b, r * T:(r + 1) * T, :] = res.results[c]["out"]
    return out


def kernel(**inputs):
    S = inputs["x"].shape[1]
    return run(inputs, S, 4)
```

```python
import contextlib
import numpy as np
import concourse.bass as bass
import concourse.mybir as mybir
from concourse.bass_utils import run_bass_kernel_spmd

F32 = mybir.dt.float32
BF16 = mybir.dt.bfloat16
AF = mybir.ActivationFunctionType
ALU = mybir.AluOpType
AX = mybir.AxisListType

D = 1024
NMEM = 256
TOK = 768
ALPHA = float((2 * 4) ** 0.25)
LN_EPS = 1e-5
RMS_EPS = 1e-6
LRU_C = 8.0
IN_W = [2816, 3584, 2048, 2048]
ST = 512
NCONST = 640
NCSB = 416


class Buf:
    __slots__ = ("ap", "name", "lastw", "readers", "psum", "gen")

    def __init__(self, ap, name, psum=False):
        self.ap = ap
        self.name = name
        self.psum = psum
        self.lastw = None
        self.readers = {}
        self.gen = 0

    def __getitem__(self, k):
        return self.ap[k]


class View:
    __slots__ = ("base", "gen")

    def __init__(self, base):
        base.gen += 1
        self.base, self.gen = base, base.gen

    def __getitem__(self, k):
        return self.base.ap[k]

    @property
    def ap(self):
        return self.base.ap

    @property
    def name(self):
        return self.base.name


class Eng:
    def __init__(self, name, h, sem):
        self.name, self.h, self.sem, self.cnt = name, h, sem, 0
        self.waited = {}


class KB:
    def __init__(self, nc, es):
        self.nc, self.es = nc, es
        self.eng = {}
        for name, h in (("pe", nc.tensor), ("act", nc.scalar), ("dve", nc.vector),
                        ("pool", nc.gpsimd), ("sp", nc.sync)):
            self.eng[name] = Eng(name, h, es.enter_context(nc.semaphore("s_" + name)))
        self.dsems = {}
        for q in ("sp", "pool"):
            self.dsems[q] = [[es.enter_context(nc.semaphore("d_%s%d" % (q, i))), 0] for i in range(20)]
        self.dnext = {"sp": 0, "pool": 0}
        self.ccsem = es.enter_context(nc.semaphore("s_cc"))
        self.cccnt = 0
        self.nb = 0
        self.ninstr = 0

    def sb(self, shape, dt, name=None):
        self.nb += 1
        name = name or "b%d" % self.nb
        t = self.es.enter_context(self.nc.sbuf_tensor(name, list(shape), dt))
        return Buf(t, name)

    def ps(self, shape, dt, name=None):
        self.nb += 1
        name = name or "p%d" % self.nb
        t = self.es.enter_context(self.nc.psum_tensor(name, list(shape), dt))
        return Buf(t, name, psum=True)

    def dram(self, shape, dt, name):
        t = self.nc.dram_tensor(name, list(shape), dt)
        return Buf(t.ap(), name)

    @staticmethod
    def _res(lst):
        out = []
        for x in lst:
            if isinstance(x, View):
                if x.gen != x.base.gen:
                    raise RuntimeError("stale ring buffer use: %s (gen %d, now %d)" % (x.base.name, x.gen, x.base.gen))
                x = x.base
            out.append(x)
        return out

    def _wait(self, e, tok):
        sem, val, sid = tok
        if e.waited.get(sid, 0) < val:
            e.h.wait_ge(sem, val)
            e.waited[sid] = val
            self.ninstr += 1

    def _deps(self, e, reads, writes):
        for b in reads:
            if b.lastw is not None:
                tok, src = b.lastw
                if not (src == "pe" and e.name == "pe"):
                    self._wait(e, tok)
            if b.psum:
                for key, (tok, src) in b.readers.items():
                    if src != e.name:
                        self._wait(e, tok)
        for b in writes:
            if b.lastw is not None:
                tok, src = b.lastw
                if not (src == "pe" and e.name == "pe"):
                    self._wait(e, tok)
            for key, (tok, src) in b.readers.items():
                if not (src == "pe" and e.name == "pe"):
                    self._wait(e, tok)

    def _commit(self, tok, src, reads, writes):
        for b in writes:
            b.lastw = (tok, src)
            b.readers = {}
        for b in reads:
            key = src if not src.startswith("dma") else "dma%d" % id(tok[0])
            b.readers[key] = (tok, src)

    def op(self, engname, fn, reads=(), writes=()):
        e = self.eng[engname]
        reads, writes = self._res(reads), self._res(writes)
        self._deps(e, reads, writes)
        ins = fn(e.h)
        e.cnt += 1
        ins.then_inc(e.sem, 1)
        self.ninstr += 1
        tok = (e.sem, e.cnt, id(e.sem))
        self._commit(tok, engname, reads, writes)
        return tok

    def dma(self, q, out_ap, in_ap, reads=(), writes=()):
        e = self.eng[q]
        reads, writes = self._res(reads), self._res(writes)
        self._deps(e, reads, writes)
        lst = self.dsems[q]
        i = self.dnext[q]
        self.dnext[q] = (i + 1) % len(lst)
        sem, tot = lst[i]
        if tot > 0:
            self._wait(e, (sem, tot, id(sem)))
        e.h.dma_start(out=out_ap, in_=in_ap).then_inc(sem, 16)
        self.ninstr += 1
        lst[i][1] = tot + 16
        tok = (sem, tot + 16, id(sem))
        self._commit(tok, "dma_" + q, reads, writes)
        return tok

    def allgather(self, groups, src, dst):
        e = self.eng["pool"]
        self._deps(e, [src], [dst])
        e.h.collective_compute("AllGather", ALU.bypass, groups, ins=[src.ap.opt()], outs=[dst.ap.opt()]).then_inc(self.ccsem)
        self.cccnt += 1
        self.ninstr += 1
        tok = (self.ccsem, self.cccnt, id(self.ccsem))
        self._commit(tok, "cc", [src], [dst])
        return tok

    def finish(self, bufs):
        e = self.eng["sp"]
        for b in bufs:
            if b.lastw is not None:
                self._wait(e, b.lastw[0])


class Ring:
    def __init__(self, bufs):
        self.bufs, self.i = bufs, 0

    def __call__(self):
        b = self.bufs[self.i]
        self.i = (self.i + 1) % len(self.bufs)
        return View(b)


def run_gen(g):
    for _ in g:
        pass


def stagger(gens, depth):
    it = iter(gens)
    active, more = [], True
    while True:
        if more and len(active) < depth:
            try:
                active.append(next(it))
            except StopIteration:
                more = False
        if not active:
            break
        for g in list(active):
            try:
                next(g)
            except StopIteration:
                active.remove(g)


def stagger_ticks(gens, depth):
    it = iter(gens)
    active, more = [], True
    while True:
        if more and len(active) < depth:
            try:
                active.append(next(it))
            except StopIteration:
                more = False
        if not active:
            break
        for g in list(active):
            try:
                next(g)
            except StopIteration:
                active.remove(g)
        yield


def alias_bufs(parent, aps, prefix):
    out = []
    for i, ap in enumerate(aps):
        b = Buf(ap, "%s%d" % (prefix, i))
        b.lastw = parent.lastw
        b.readers = dict(parent.readers)
        out.append(b)
    return out


def retire(aliases, parent):
    for a in aliases:
        if a.lastw is not None:
            parent.readers[a.name + "#w"] = a.lastw
        for key, v in a.readers.items():
            parent.readers[a.name + "#" + key] = v


def interleave(gx, gy, r):
    dx = dy = False
    while not (dx and dy):
        for _ in range(r):
            if not dx:
                try:
                    next(gx)
                except StopIteration:
                    dx = True
        if not dy:
            try:
                next(gy)
            except StopIteration:
                dy = True


def build(T, NR, nlayers=4, layer_list=None):
    assert T % ST == 0
    NS = T // ST
    nc = bass.Bass("TRN2", target_bir_lowering=False)
    es = contextlib.ExitStack()
    groups = [list(range(g * NR, (g + 1) * NR)) for g in range(8 // NR)]

    def din(name, shape):
        return Buf(nc.dram_tensor(name, list(shape), F32, kind="ExternalInput").ap(), name)

    x_in = din("x", [T, D])
    memT_in = din("memT", [128, 8, NMEM])
    kvw_in = din("mem_kv_w", [D, 512])
    lng_in = din("ln_g", [4, D])
    lnb_in = din("ln_b", [4, D])
    wout_in = din("w_out", [4, D, D])
    lb_in = din("lbvec", [2, TOK])
    hlb_in = din("hgrn_lb_logits", [4, TOK])
    w_in = [din("a_w_in", [D, IN_W[0]]), din("b_w_in", [D, IN_W[1]]),
            din("c_w_in", [D, IN_W[2]]), din("d_w_in", [D, IN_W[3]])]
    a_wsT_in = din("a_w_sT", [128, 6, 128])
    a_bs_in = din("a_b_s", [1, TOK])
    b_ng_in = din("b_norm_g", [128, 6])
    c_wp_in = din("c_w_pool", [4, 192, 192])
    c_sc_in = din("c_scale", [128, 6])
    d_cw_in = din("d_conv_w", [128, 6, 4])
    d_cb_in = din("d_conv_b", [128, 6])
    d_wgx_in = din("d_w_gx", [128, 6, 128])
    d_bgx_in = din("d_b_gx", [128, 6])
    d_wga_in = din("d_w_ga", [128, 6, 128])
    d_bga_in = din("d_b_ga", [128, 6])
    d_ap_in = din("d_a_param", [128, 6])
    consts_in = din("consts", [128, NCONST])
    rankc_in = din("rankc", [128, 128])
    out_d = Buf(nc.dram_tensor("out", [T, D], F32, kind="ExternalOutput").ap(), "out")

    with es:
        kb = KB(nc, es)
        op, dma = kb.op, kb.dma
        xres = [kb.dram([T, D], F32, "xres%d" % i) for i in range(2)]
        xT_d = [kb.dram([128, 8, T], BF16, "xT_d%d" % i) for i in range(2)]
        WA = kb.sb([128, 8, 1536], BF16, "WA")
        WB = kb.sb([128, 8, 1280], BF16, "WB")
        WC = kb.sb([128, 8, 1024], BF16, "WC")
        CONST = kb.sb([128, NCSB], F32, "CONST")
        RANKC = kb.sb([128, 128], F32, "RANKC")
        IDB = kb.sb([128, 128], BF16, "IDB")
        KTP = kb.sb([128, 4, NMEM], BF16, "KTP")
        VP = kb.sb([128, 2, 4, 128], BF16, "VP")
        OP_ = kb.sb([128, 2, 128], BF16, "OP")
        LNG = kb.sb([128, D], F32, "LNG")
        LNB = kb.sb([128, D], F32, "LNB")
        xT_t = Ring([kb.sb([128, 8, ST], BF16, "xT%d" % i) for i in range(2)])
        xst = Ring([kb.sb([128, 8, 128], BF16, "xst%d" % i) for i in range(2)])
        SGs = [kb.sb([128, 8, ST], BF16, "SG%d" % i) for i in range(2)]
        QT = kb.sb([128, 2, ST], BF16, "QT")
        wk32 = Ring([kb.sb([128, ST + 16], F32, "wk32_%d" % i) for i in range(10)])
        wk16 = Ring([kb.sb([128, ST], BF16, "wk16_%d" % i) for i in range(6)])
        ln32 = Ring([kb.sb([128, D], F32, "ln32_%d" % i) for i in range(4)])
        tm32 = Ring([kb.sb([128, TOK], F32, "tm32_%d" % i) for i in range(5)])
        tm16 = Ring([kb.sb([128, TOK], BF16, "tm16_%d" % i) for i in range(4)])
        xnb_r = Ring([kb.sb([128, D], BF16, "xnb%d" % i) for i in range(1)])
        small = Ring([kb.sb([128, 16], F32, "sm%d" % i) for i in range(18)])
        smallY = Ring([kb.sb([128, 24], F32, "smY%d" % i) for i in range(4)])
        LC = kb.sb([128, 1536], F32, "LC")
        LCB = kb.sb([128, 6, 768], BF16, "LCB")
        TOKB = kb.sb([128, 6, ST], F32, "TOKB")
        HALO = kb.sb([128, 6, 16], F32, "HALO")
        HPREV = kb.sb([128, 16], F32, "HPREV")
        EPSC = kb.sb([128, 4], F32, "EPSC")
        XTH = kb.sb([128, 8, 16], BF16, "XTH")
        print("sbuf remaining", nc.sbuf_bytes_remaining)
        PB = [kb.ps([128, 512], F32, "psb%d" % i) for i in range(7)]
        pst = kb.ps([128, 8, 128], BF16, "pst")
        psb = Ring(PB[0:6])
        psX = Ring(PB[0:3])
        psY = Ring(PB[3:7])
        P6 = PB[6]
        p6bf = P6.ap[:, :].bitcast(BF16).rearrange("p (a b) -> p a b", b=128)

        dma("sp", CONST[:, :], consts_in[:, 128:128 + NCSB], [consts_in], [CONST])
        dma("sp", RANKC[:, :], rankc_in[:, :], [rankc_in], [RANKC])
        dma("pool", IDB[:, :], consts_in[:, 0:128], [consts_in], [IDB])
        C_TRI, C_TRI64, C_TRIREV64, C_IND, C_CK = 0, 128, 256, 384, 388
        op("dve", lambda h: h.memset(EPSC[:, 0:1], LN_EPS), [], [EPSC])
        op("dve", lambda h: h.memset(EPSC[:, 1:2], RMS_EPS), [], [EPSC])
        op("dve", lambda h: h.memset(EPSC[:, 2:3], 1.0), [], [EPSC])
        op("dve", lambda h: h.memset(EPSC[:, 3:4], 0.0), [], [EPSC])

        def load_w(dst, src_ap, ncols, c0=0):
            for k in range(8):
                dma("pool", dst[:, k, c0:c0 + ncols], src_ap[k * 128:(k + 1) * 128, :], [], [dst])

        wspec = {
            0: {"WA": (WA, lambda: w_in[0][:, 0:1536], 1536), "WB": (WB, lambda: w_in[0][:, 1536:2816], 1280)},
            1: {"WA": (WA, lambda: w_in[1][:, 768:2304], 1536), "WB1": (WB, lambda: w_in[1][:, 0:768], 768),
                "WB": (WB, lambda: w_in[1][:, 2304:3584], 1280)},
            2: {"WA": (WA, lambda: w_in[2][:, 0:768], 768), "WB": (WB, lambda: w_in[2][:, 768:2048], 1280)},
            3: {"WA": (WA, lambda: w_in[3][:, 0:768], 768), "WB": (WB, lambda: w_in[3][:, 768:2048], 1280)},
        }
        for l_ in range(4):
            wspec[l_]["WC"] = (WC, (lambda l_=l_: wout_in[l_, :, :]), 1024)
        resident = {}

        def ensure_w(l, key):
            if l is None:
                return
            dst, src, ncols = wspec[l][key]
            if resident.get(dst.name) != (l, key):
                load_w(dst, src(), ncols)
                resident[dst.name] = (l, key)

        def first_wb(l):
            return "WB1" if l == 1 else "WB"

        MEMT = xT_t()
        dma("pool", MEMT[:, :, 0:NMEM], memT_in[:, :, :], [memT_in], [MEMT])
        load_w(WB, kvw_in.ap, 512)
        op("pool", lambda h: h.memset(KTP[:, :, :], 0.0), [], [KTP])
        op("pool", lambda h: h.memset(VP[:, :, :, :], 0.0), [], [VP])
        op("pool", lambda h: h.memset(OP_[:, :, :], 0.0), [], [OP_])
        op("pool", lambda h: h.memset(OP_[:, 0, 0:64], 1.0), [], [OP_])
        op("pool", lambda h: h.memset(OP_[:, 1, 64:128], 1.0), [], [OP_])
        for c in range(2):
            p = psb()
            for k in range(8):
                op("pe", lambda h, k=k, c=c, p=p: h.matmul(p[:, 0:NMEM], WB[:, k, c * 128:(c + 1) * 128], MEMT[:, k, 0:NMEM],
                                                           start=(k == 0), stop=(k == 7)), [WB, MEMT], [p])
            op("act", lambda h, c=c, p=p: h.copy(KTP[0:64, 2 * c, :], p[0:64, 0:NMEM]), [p], [KTP])
            op("act", lambda h, c=c, p=p: h.copy(KTP[64:128, 2 * c + 1, :], p[64:128, 0:NMEM]), [p], [KTP])
        for mc in range(2):
            p = psb()
            for k in range(8):
                op("pe", lambda h, k=k, mc=mc, p=p: h.matmul(p[:, 0:256], MEMT[:, k, mc * 128:(mc + 1) * 128], WB[:, k, 256:512],
                                                             start=(k == 0), stop=(k == 7)), [WB, MEMT], [p])
            for hd in range(4):
                o = (hd % 2) * 64
                op("act", lambda h, hd=hd, mc=mc, p=p, o=o: h.copy(VP[:, mc, hd, o:o + 64], p[:, hd * 64:(hd + 1) * 64]), [p], [VP])

        ll = layer_list if layer_list is not None else list(range(nlayers))

        def transposes_to_xT(xn_f32, xT_dst, t0):
            xb = xnb_r()
            op("act", lambda h: h.copy(xb[:, :], xn_f32[:, :]), [xn_f32], [xb])
            for k in range(8):
                op("pe", lambda h, k=k: h.transpose(pst[:, k, :], xb[:, k * 128:(k + 1) * 128], IDB[:, :]), [xb, IDB], [pst])
            stg = xst()
            op("dve", lambda h: h.tensor_copy(stg[:, :, :], pst[:, :, :]), [pst], [stg])
            dma("pool", xT_dst[:, :, t0:t0 + 128], stg[:, :, :], [stg], [xT_dst])

        def pro_unit(t0):
            xr = ln32()
            dma("sp", xr[:, :], x_in[t0:t0 + 128, :], [x_in], [xr])
            yield
            transposes_to_xT(xr, xT_d[0], t0)

        def prologue(s):
            stagger((pro_unit(s * ST + sub * 128) for sub in range(4)), 2)

        lazy_pro = (ll[0] in (0, 2))
        for s in range(min(2, NS) if lazy_pro else NS):
            prologue(s)
        ensure_w(ll[0], "WA")
        ensure_w(ll[0], first_wb(ll[0]))

        def load_ln(l):
            dma("sp", LNG[:, :], lng_in[l:l + 1, :].partition_broadcast(128), [], [LNG])
            dma("sp", LNB[:, :], lnb_in[l:l + 1, :].partition_broadcast(128), [], [LNB])

        def load_xT(pi, s):
            xt = xT_t()
            dma("sp", xt[:, :, :], xT_d[pi % 2][:, :, s * ST:(s + 1) * ST], [xT_d[pi % 2]], [xt])
            return xt

        def proj_fm(W, c0, xt, ring, n=ST):
            p = ring()
            for k in range(8):
                op("pe", lambda h, k=k: h.matmul(p[:, 0:n], W[:, k, c0:c0 + 128], xt[:, k, 0:n],
                                                 start=(k == 0), stop=(k == 7)), [W, xt], [p])
            return p

        def proj_tm(W, c0, ncols, xt, sub, ring):
            p = ring()
            for k in range(8):
                op("pe", lambda h, k=k: h.matmul(p[:, 0:ncols], xt[:, k, sub * 128:(sub + 1) * 128], W[:, k, c0:c0 + ncols],
                                                 start=(k == 0), stop=(k == 7)), [W, xt], [p])
            return p

        def qx_gate_attn(xt, SG, act_recip=False):
            for c in range(2):
                p = proj_fm(WB, c * 128, xt, psX)
                op("act", lambda h, c=c, p=p: h.copy(QT[:, c, :], p[:, :]), [p], [QT])
            yield

            def gates(js):
                for j in js:
                    p = proj_fm(WB, 256 + j * 128, xt, psX)
                    op("act", lambda h, j=j, p=p: h.activation(SG[:, j, :], p[:, :], AF.Silu), [p], [SG])
                    yield

            for pr, gj in ((0, (6, 7, 0, 1)), (1, (2, 3, 4, 5))):
                pts = []
                for hd in (2 * pr, 2 * pr + 1):
                    for mc in range(2):
                        sc = psX()
                        op("pe", lambda h, hd=hd, mc=mc, sc=sc: h.matmul(sc[:, :], KTP[:, hd, mc * 128:(mc + 1) * 128], QT[:, pr, :],
                                                                         start=True, stop=True), [KTP, QT], [sc])
                        pt = wk16()
                        op("act", lambda h, sc=sc, pt=pt: h.activation(pt[:, :], sc[:, :], AF.Exp, scale=0.125), [sc], [pt])
                        pts.append((hd, mc, pt))
                yield
                yield from gates(gj)
                num, den = psX(), psX()
                for i, (hd, mc, pt) in enumerate(pts):
                    op("pe", lambda h, hd=hd, mc=mc, pt=pt, i=i: h.matmul(num[:, :], VP[:, mc, hd, :], pt[:, :],
                                                                          start=(i == 0), stop=(i == 3)), [VP, pt], [num])
                for i, (hd, mc, pt) in enumerate(pts):
                    op("pe", lambda h, hd=hd, pt=pt, i=i: h.matmul(den[:, :], OP_[:, hd % 2, :], pt[:, :],
                                                                   start=(i == 0), stop=(i == 3)), [OP_, pt], [den])
                rec = wk32()
                if act_recip:
                    op("act", lambda h: h.activation(rec[:, 0:ST], den[:, :], AF.Ln), [den], [rec])
                    op("act", lambda h: h.activation(rec[:, 0:ST], rec[:, 0:ST], AF.Exp, scale=-1.0), [rec], [rec])
                else:
                    op("dve", lambda h: h.reciprocal(rec[:, 0:ST], den[:, :]), [den], [rec])
                xo = wk32()
                op("dve", lambda h: h.tensor_tensor(xo[:, 0:ST], num[:, :], rec[:, 0:ST], ALU.mult), [num, rec], [xo])
                op("dve", lambda h, pr=pr: h.tensor_tensor(SG[:, 6 + pr, :], xo[:, 0:ST], SG[:, 6 + pr, :], ALU.mult), [xo, SG], [SG])
                yield

        def Ygen(pi, s, last, SG):
            xsrc = x_in if pi == 0 else xres[pi % 2]
            xdst = out_d if last else xres[(pi + 1) % 2]

            def unit(sub):
                t0 = s * ST + sub * 128
                xr = ln32()
                dma("sp", xr[:, :], xsrc[t0:t0 + 128, :], [xsrc], [xr])
                ys = [psY(), psY()]
                for hf in range(2):
                    for e in range(8):
                        op("pe", lambda h, e=e, hf=hf: h.matmul(ys[hf][:, :], SG[:, e, sub * 128:(sub + 1) * 128], WC[:, e, hf * 512:(hf + 1) * 512],
                                                                start=(e == 0), stop=(e == 7)), [SG, WC], [ys[hf]])
                yield
                smy = smallY()
                st6 = View.__new__(View); st6.base, st6.gen = smy.base, smy.gen
                mv = smy
                for hf in range(2):
                    op("dve", lambda h, hf=hf: h.scalar_tensor_tensor(xr[:, hf * 512:(hf + 1) * 512], xr[:, hf * 512:(hf + 1) * 512], ALPHA,
                                                                      ys[hf][:, :], ALU.mult, ALU.add), [xr, ys[hf]], [xr])
                    op("dve", lambda h, hf=hf: h.bn_stats(st6[:, hf * 6:(hf + 1) * 6], xr[:, hf * 512:(hf + 1) * 512]), [xr], [st6])
                op("dve", lambda h: h.bn_aggr(mv[:, 16:18], st6[:, 0:12]), [st6], [mv])
                yield
                op("act", lambda h: h.activation(mv[:, 20:21], mv[:, 17:18], AF.Sqrt, bias=EPSC[:, 0:1], scale=1.0), [mv, EPSC], [mv])
                yield
                op("dve", lambda h: h.reciprocal(mv[:, 18:19], mv[:, 20:21]), [mv], [mv])
                op("dve", lambda h: h.scalar_tensor_tensor(mv[:, 19:20], mv[:, 16:17], -1.0, mv[:, 18:19], ALU.mult, ALU.mult), [mv], [mv])
                yield
                op("act", lambda h: h.activation(xr[:, :], xr[:, :], AF.Identity, bias=mv[:, 19:20], scale=mv[:, 18:19]), [xr, mv], [xr])
                yield
                op("dve", lambda h: h.tensor_tensor(xr[:, :], xr[:, :], LNG[:, :], ALU.mult), [xr, LNG], [xr])
                if not last:
                    xb = xnb_r()
                    op("dve", lambda h: h.tensor_tensor(xb[:, :], xr[:, :], LNB[:, :], ALU.add), [xr, LNB], [xb])
                yield
                op("pool", lambda h: h.tensor_tensor(xr[:, :], xr[:, :], LNB[:, :], ALU.add), [xr, LNB], [xr])
                dma("pool", xdst[t0:t0 + 128, :], xr[:, :], [xr], [xdst])
                if not last:
                    for k in range(8):
                        op("pe", lambda h, k=k: h.transpose(pst[:, k, :], xb[:, k * 128:(k + 1) * 128], IDB[:, :]), [xb, IDB], [pst])
                    yield
                    stg = xst()
                    op("act", lambda h: h.copy(stg[:, :, :], pst[:, :, :]), [pst], [stg])
                    dma("pool", xT_d[(pi + 1) % 2][:, :, t0:t0 + 128], stg[:, :, :], [stg], [xT_d[(pi + 1) % 2]])

            yield from stagger_ticks((unit(sub) for sub in range(4)), 4)

        def pipeline(X, pi, last, r, hook=None):
            run_gen(X(0, SGs[0]))
            for s in range(1, NS):
                if pi == 0 and lazy_pro and s + 1 < NS:
                    prologue(s + 1)
                interleave(X(s, SGs[s % 2]), Ygen(pi, s - 1, last, SGs[(s - 1) % 2]), r)
            if hook is not None:
                hook()
            run_gen(Ygen(pi, NS - 1, last, SGs[(NS - 1) % 2]))

        cc_n = [0]

        def gather(src_sb_ap, src_buf, rows, cols):
            cc_n[0] += 1
            csrc = kb.dram([rows, cols], F32, "cc_src%d" % cc_n[0])
            cdst = kb.dram([NR * rows, cols], F32, "cc_dst%d" % cc_n[0])
            dma("pool", csrc[:, :], src_sb_ap, [src_buf], [csrc])
            kb.allgather(groups, csrc, cdst)
            return cdst

        def halo_A(pi):
            xsrc = x_in if pi == 0 else xres[pi % 2]
            xl = ln32()
            dma("sp", xl[0:16, :], xsrc[T - 16:T, :], [xsrc], [xl])
            cdst = gather(xl[0:16, :], xl, 16, D)
            g = ln32()
            dma("sp", g[0:NR * 16, :], cdst[:, :], [cdst], [g])
            return g

        def halo_B(g):
            ph = psb()
            for k in range(8):
                op("pe", lambda h, k=k: h.matmul(ph[:, k * 16:(k + 1) * 16], g[0:NR * 16, k * 128:(k + 1) * 128], RANKC[0:NR * 16, 80:96],
                                                 start=True, stop=True), [g, RANKC], [ph])
            op("act", lambda h: h.copy(XTH.ap.rearrange("p k t -> p (k t)"), ph[:, 0:128]), [ph], [XTH])
            for j in range(6):
                p = psb()
                for k in range(8):
                    op("pe", lambda h, k=k, j=j, p=p: h.matmul(p[:, 0:16], WA[:, k, j * 128:(j + 1) * 128], XTH[:, k, :],
                                                               start=(k == 0), stop=(k == 7)), [WA, XTH], [p])
                op("act", lambda h, j=j, p=p: h.copy(HALO[:, j, :], p[:, 0:16]), [p], [HALO])

        def layer0(l, pi, last, nxt):
            ensure_w(l, "WA")
            ensure_w(l, "WB")
            ensure_w(l, "WC")
            load_ln(l)
            WST32 = tm32()
            dma("sp", WST32[:, :], a_wsT_in.ap.rearrange("p g t -> p (g t)"), [a_wsT_in], [WST32])
            for g in range(6):
                op("dve", lambda h, g=g: h.tensor_tensor(LCB[:, g, 0:128], WST32[:, g * 128:(g + 1) * 128], CONST[:, C_TRI:C_TRI + 128], ALU.mult),
                   [WST32, CONST], [LCB])
            dma("sp", LC[:, 0:TOK], a_bs_in[0:1, :].partition_broadcast(128), [a_bs_in], [LC])

            def X(s, SG):
                xt = load_xT(pi, s)
                gus = []
                for j in range(6):
                    p = proj_fm(WA, j * 128, xt, psX)
                    gu = wk32()
                    op("act", lambda h, p=p, gu=gu: h.activation(gu[:, 0:ST], p[:, :], AF.Gelu_apprx_tanh), [p], [gu])
                    gus.append(gu)
                    yield
                vns = {}

                def stA(sub):
                    p1 = proj_tm(WA, 768, 512, xt, sub, psX)
                    p2 = proj_tm(WA, 1280, 256, xt, sub, psX)
                    gv, sq = tm32(), tm32()
                    op("act", lambda h: h.activation(gv[:, 0:512], p1[:, :], AF.Gelu_apprx_tanh), [p1], [gv])
                    op("act", lambda h: h.activation(gv[:, 512:768], p2[:, 0:256], AF.Gelu_apprx_tanh), [p2], [gv])
                    op("act", lambda h: h.activation(sq[:, :], gv[:, :], AF.Square), [gv], [sq])
                    sm = small()
                    op("dve", lambda h: h.reduce_sum(sm[:, 0:6], gv.ap.rearrange("p (g c) -> p g c", c=128), axis=AX.X), [gv], [sm])
                    op("dve", lambda h: h.reduce_sum(sm[:, 6:12], sq.ap.rearrange("p (g c) -> p g c", c=128), axis=AX.X), [sq], [sm])
                    sm2 = small()
                    op("dve", lambda h: h.tensor_scalar(sm2[:, 0:6], sm[:, 0:6], 1.0 / 128, None, ALU.mult), [sm], [sm2])
                    op("dve", lambda h: h.tensor_tensor(sm2[:, 6:12], sm2[:, 0:6], sm2[:, 0:6], ALU.mult), [sm2], [sm2])
                    sm3 = small()
                    op("dve", lambda h: h.scalar_tensor_tensor(sm3[:, 0:6], sm[:, 6:12], 1.0 / 128, sm2[:, 6:12], ALU.mult, ALU.subtract), [sm, sm2], [sm3])
                    sm4 = small()
                    op("act", lambda h: h.activation(sm4[:, 0:6], sm3[:, 0:6], AF.Sqrt, bias=EPSC[:, 0:1], scale=1.0), [sm3, EPSC], [sm4])
                    op("dve", lambda h: h.reciprocal(sm3[:, 6:12], sm4[:, 0:6]), [sm4], [sm3])
                    gv3 = gv.ap.rearrange("p (g c) -> p g c", c=128)
                    op("dve", lambda h: h.tensor_tensor(gv3, gv3, sm2[:, 0:6].unsqueeze(2).to_broadcast([128, 6, 128]), ALU.subtract), [gv, sm2], [gv])
                    vn = tm16()
                    op("dve", lambda h: h.tensor_tensor(vn.ap.rearrange("p (g c) -> p g c", c=128), gv3,
                                                        sm3[:, 6:12].unsqueeze(2).to_broadcast([128, 6, 128]), ALU.mult), [gv, sm3], [vn])
                    vns[sub] = vn

                def stB(sub):
                    vn = vns[sub]
                    pa, pb = psX(), psX()
                    for g in range(6):
                        dst = pa[:, g * 128:(g + 1) * 128] if g < 4 else pb[:, (g - 4) * 128:(g - 3) * 128]
                        op("pe", lambda h, g=g, dst=dst: h.matmul(dst, vn[:, g * 128:(g + 1) * 128], LCB[:, g, 0:128], start=True, stop=True),
                           [vn, LCB], [pa if g < 4 else pb])
                    op("dve", lambda h: h.tensor_tensor(TOKB[:, 0:4, sub * 128:(sub + 1) * 128], pa.ap.rearrange("p (g t) -> p g t", t=128),
                                                        LC[:, 0:512].rearrange("p (g t) -> p g t", t=128), ALU.add), [pa, LC], [TOKB])
                    op("dve", lambda h: h.tensor_tensor(TOKB[:, 4:6, sub * 128:(sub + 1) * 128], pb[:, 0:256].rearrange("p (g t) -> p g t", t=128),
                                                        LC[:, 512:768].rearrange("p (g t) -> p g t", t=128), ALU.add), [pb, LC], [TOKB])

                for f, a in ((stA, 0), (stA, 1), (stB, 0), (stA, 2), (stB, 1), (stA, 3), (stB, 2), (stB, 3)):
                    f(a)
                    yield
                if s == NS - 1:
                    ensure_w(nxt, "WA")
                yield from qx_gate_attn(xt, SG)
                if s == NS - 1 and nxt is not None:
                    ensure_w(nxt, first_wb(nxt))
                for j in range(6):
                    op("pool", lambda h, j=j: h.tensor_tensor(TOKB[:, j, :], TOKB[:, j, :], gus[j][:, 0:ST], ALU.mult), [TOKB, gus[j]], [TOKB])
                    op("dve", lambda h, j=j: h.tensor_tensor(SG[:, j, :], TOKB[:, j, :], SG[:, j, :], ALU.mult), [TOKB, SG], [SG])
                    yield

            def hook():
                ensure_w(nxt, "WA")
                if nxt is not None:
                    ensure_w(nxt, first_wb(nxt))
            pipeline(X, pi, last, 3, hook)

        def layer2(l, pi, last, nxt):
            ensure_w(l, "WA")
            op("dve", lambda h: h.memset(HALO[:, :, :], 0.0), [], [HALO])
            hg = halo_A(pi) if NR > 1 else None
            ensure_w(l, "WB")
            ensure_w(l, "WC")
            load_ln(l)
            op("pool", lambda h: h.memset(LCB[:, :, :], 0.0), [], [LCB])
            for g in range(4):
                r0 = 192 * g
                done = 0
                while done < 192:
                    r = r0 + done
                    ck, p0 = r // 128, r % 128
                    n = min(128 - p0, 192 - done)
                    dma("pool", LCB[p0:p0 + n, ck, 192 * g:192 * g + 192], c_wp_in[g, done:done + n, :], [c_wp_in], [LCB])
                    done += n
            dma("sp", LC[:, 0:6], c_sc_in[:, :], [c_sc_in], [LC])
            wkc_ring = Ring(wk32.bufs + tm32.bufs)
            overlap = {}
            for jd in range(6):
                gs = set(c // 192 for c in (128 * jd, 128 * jd + 127))
                overlap[jd] = [ci for ci in range(6) if set(r // 192 for r in (128 * ci, 128 * ci + 127)) & gs]

            def X(s, SG):
                xt = load_xT(pi, s)
                yield from qx_gate_attn(xt, SG, act_recip=True)
                if s == NS - 1 and nxt is not None:
                    ensure_w(nxt, first_wb(nxt))
                if s == 0 and hg is not None:
                    halo_B(hg)
                diffs = [None] * 6
                wkc = wkc_ring

                def cgen(j):
                    p = proj_fm(WA, j * 128, xt, psX)
                    pe_ = wkc()
                    op("dve", lambda h: h.tensor_copy(pe_[:, 0:16], HALO[:, j, :]), [HALO], [pe_])
                    op("act", lambda h: h.copy(pe_[:, 16:528], p[:, :]), [p], [pe_])
                    op("pool", lambda h: h.tensor_copy(HALO[:, j, :], pe_[:, 512:528]), [pe_], [HALO])
                    yield
                    g_lo, g_hi = (128 * j) // 192, (128 * j + 127) // 192

                    def shadd(eng, cur, prev, sh):
                        lo = 2 * sh - 1
                        op(eng, lambda h: h.tensor_tensor(cur[:, lo:528], prev[:, lo:528], prev[:, lo - sh:528 - sh], ALU.add), [prev], [cur])

                    def corr(t, k):
                        if s == 0:
                            op("dve", lambda h: h.tensor_tensor(t[:, 16:32], t[:, 16:32], RANKC[:, 16 * k:16 * k + 16], ALU.mult), [t, RANKC], [t])

                    ss = [pe_]
                    for k in range(g_hi + 1):
                        cur = wkc()
                        shadd("pool", cur, ss[-1], 1 << k)
                        ss.append(cur)
                        if k % 2 == 1:
                            yield
                    yield
                    df = wk16()
                    ranges = [(0, 128, g_lo)] if g_lo == g_hi else [(0, 192 * g_hi - 128 * j, g_lo), (192 * g_hi - 128 * j, 128, g_hi)]
                    for (p0, p1, g) in ranges:
                        sk = ss[g + 1]
                        corr(sk, g)
                        op("dve", lambda h, p0=p0, p1=p1, g=g, sk=sk: h.scalar_tensor_tensor(df[p0:p1, :], sk[p0:p1, 16:528], 1.0 / (2 << g),
                                                                                         pe_[p0:p1, 16:528], ALU.mult, ALU.subtract), [sk, pe_], [df])
                    diffs[j] = df

                stagger((cgen(j) for j in range(6)), 3)
                if s == NS - 1:
                    ensure_w(nxt, "WA")
                yield
                for jd in range(6):
                    p = psX()
                    lst = overlap[jd]
                    for i, ci in enumerate(lst):
                        op("pe", lambda h, ci=ci, jd=jd, i=i, n=len(lst): h.matmul(p[:, :], LCB[:, ci, jd * 128:(jd + 1) * 128], diffs[ci][:, :],
                                                                                  start=(i == 0), stop=(i == n - 1)), [LCB, diffs[ci]], [p])
                    op("act", lambda h, jd=jd, p=p: h.activation(TOKB[:, jd, :], p[:, :], AF.Identity, scale=LC[:, jd:jd + 1]), [p, LC], [TOKB])
                    op("dve", lambda h, jd=jd: h.tensor_tensor(SG[:, jd, :], TOKB[:, jd, :], SG[:, jd, :], ALU.mult), [TOKB, SG], [SG])
                    yield

            def hook():
                ensure_w(nxt, "WA")
                if nxt is not None:
                    ensure_w(nxt, first_wb(nxt))
            pipeline(X, pi, last, 4, hook)

        def layer3(l, pi, last, nxt):
            ensure_w(l, "WA")
            dma("pool", LCB[:, :, 0:128], d_wgx_in[:, :, :], [d_wgx_in], [LCB])
            dma("pool", LCB[:, :, 128:256], d_wga_in[:, :, :], [d_wga_in], [LCB])
            dma("sp", LC[:, 0:24], d_cw_in.ap.rearrange("p h j -> p (h j)"), [d_cw_in], [LC])
            dma("sp", LC[:, 24:30], d_cb_in[:, :], [d_cb_in], [LC])
            dma("sp", LC[:, 30:36], d_bgx_in[:, :], [d_bgx_in], [LC])
            dma("sp", LC[:, 36:42], d_bga_in[:, :], [d_bga_in], [LC])
            dma("sp", LC[:, 42:48], d_ap_in[:, :], [d_ap_in], [LC])
            op("act", lambda h: h.activation(LC[:, 54:60], LC[:, 42:48], AF.Exp, scale=-1.0), [LC], [LC])
            op("dve", lambda h: h.tensor_scalar(LC[:, 54:60], LC[:, 54:60], 1.0, None, ALU.add), [LC], [LC])
            op("act", lambda h: h.activation(LC[:, 60:66], LC[:, 54:60], AF.Ln), [LC], [LC])
            op("dve", lambda h: h.tensor_scalar(LC[:, 48:54], LC[:, 60:66], -LRU_C, None, ALU.mult), [LC], [LC])
            op("dve", lambda h: h.memset(HALO[:, :, :], 0.0), [], [HALO])
            op("dve", lambda h: h.memset(HPREV[:, :], 0.0), [], [HPREV])
            op("dve", lambda h: h.memset(HPREV[:, 8:14], 1.0), [], [HPREV])
            op("dve", lambda h: h.memset(TOKB[:, 0, :], 0.0), [], [TOKB])
            if NR > 1:
                halo_B(halo_A(pi))
            ensure_w(l, "WB")
            ensure_w(l, "WC")
            load_ln(l)
            hloc = kb.dram([NS, 128, 6, ST], F32, "hloc")
            aloc = kb.dram([NS, 128, 6, ST], F32, "aloc")

            sgf = [SGs[i].ap.rearrange("p a b -> p (a b)").bitcast(F32) for i in range(2)]
            al_sg = [alias_bufs(SGs[i], [sgf[i][:, 512 * k:512 * (k + 1)] for k in range(4)], "sgal%d_" % i) for i in range(2)]
            al_tk = alias_bufs(TOKB, [TOKB.ap[:, k, :] for k in range(1, 6)], "tkal")
            r32 = Ring(al_sg[0] + al_sg[1] + tm32.bufs + ln32.bufs)
            rHA = Ring(al_tk)
            xts = {}

            def chunk_gen(s, j):
                if j == 0:
                    xts[s] = load_xT(pi, s)
                xt = xts[s]
                p = proj_fm(WA, j * 128, xt, psb)
                xb = wk32()
                op("dve", lambda h: h.tensor_copy(xb[:, 0:16], HALO[:, j, :]), [HALO], [xb])
                op("act", lambda h: h.copy(xb[:, 16:528], p[:, :]), [p], [xb])
                op("pool", lambda h: h.tensor_copy(HALO[:, j, :], xb[:, 512:528]), [xb], [HALO])
                yield
                xc = wk32()
                cw = lambda k: LC[:, j * 4 + k:j * 4 + k + 1]
                op("dve", lambda h: h.tensor_scalar(xc[:, 0:512], xb[:, 13:525], cw(0), LC[:, 24 + j:25 + j], ALU.mult, ALU.add), [xb, LC], [xc])
                op("dve", lambda h: h.scalar_tensor_tensor(xc[:, 0:512], xb[:, 14:526], cw(1), xc[:, 0:512], ALU.mult, ALU.add), [xb, LC, xc], [xc])
                op("dve", lambda h: h.scalar_tensor_tensor(xc[:, 0:512], xb[:, 15:527], cw(2), xc[:, 0:512], ALU.mult, ALU.add), [xb, LC, xc], [xc])
                op("dve", lambda h: h.scalar_tensor_tensor(xc[:, 0:512], xb[:, 16:528], cw(3), xc[:, 0:512], ALU.mult, ALU.add), [xb, LC, xc], [xc])
                yield
                xcb = wk16()
                op("act", lambda h: h.copy(xcb[:, :], xc[:, 0:512]), [xc], [xcb])
                pgx, pga = psb(), psb()
                op("pe", lambda h: h.matmul(pgx[:, 0:512], LCB[:, j, 0:128], xcb[:, :], start=True, stop=True), [LCB, xcb], [pgx])
                op("pe", lambda h: h.matmul(pga[:, 0:512], LCB[:, j, 128:256], xcb[:, :], start=True, stop=True), [LCB, xcb], [pga])
                yield
                gx, ga = r32(), r32()
                op("act", lambda h: h.activation(gx[:, 0:512], pgx[:, 0:512], AF.Sigmoid, bias=LC[:, 30 + j:31 + j], scale=1.0), [pgx, LC], [gx])
                op("act", lambda h: h.activation(ga[:, 0:512], pga[:, 0:512], AF.Sigmoid, bias=LC[:, 36 + j:37 + j], scale=1.0), [pga, LC], [ga])
                op("act", lambda h: h.activation(ga[:, 0:512], ga[:, 0:512], AF.Exp, scale=LC[:, 48 + j:49 + j]), [ga, LC], [ga])
                yield
                m = r32()
                op("dve", lambda h: h.tensor_tensor(m[:, 0:512], ga[:, 0:512], ga[:, 0:512], ALU.mult), [ga], [m])
                op("dve", lambda h: h.tensor_scalar(m[:, 0:512], m[:, 0:512], -1.0, 1.0, ALU.mult, ALU.add), [m], [m])
                op("dve", lambda h: h.tensor_tensor(gx[:, 0:512], gx[:, 0:512], xc[:, 0:512], ALU.mult), [gx, xc], [gx])
                yield
                op("act", lambda h: h.activation(m[:, 0:512], m[:, 0:512], AF.Sqrt), [m], [m])
                if s == 0:
                    t = small()
                    op("dve", lambda h: h.tensor_scalar(t[:, 0:1], m[:, 0:1], -1.0, 1.0, ALU.mult, ALU.add), [m], [t])
                    op("dve", lambda h: h.scalar_tensor_tensor(m[:, 0:1], t[:, 0:1], RANKC[:, 64:65], m[:, 0:1], ALU.mult, ALU.add), [t, RANKC, m], [m])
                yield
                op("pool", lambda h: h.tensor_tensor(gx[:, 0:512], gx[:, 0:512], m[:, 0:512], ALU.mult), [gx, m], [gx])
                yield
                H, A = rHA(), rHA()
                op("dve", lambda h: h.tensor_tensor_scan(H[:, :], ga[:, 0:512], gx[:, 0:512], HPREV[:, j:j + 1], ALU.mult, ALU.add), [ga, gx, HPREV], [H])
                op("dve", lambda h: h.tensor_copy(HPREV[:, j:j + 1], H[:, 511:512]), [H], [HPREV])
                op("dve", lambda h: h.tensor_tensor_scan(A[:, :], ga[:, 0:512], TOKB[:, 0, :], HPREV[:, 8 + j:9 + j], ALU.mult, ALU.add),
                   [ga, TOKB, HPREV], [A])
                op("dve", lambda h: h.tensor_copy(HPREV[:, 8 + j:9 + j], A[:, 511:512]), [A], [HPREV])
                dma("pool", hloc[s, :, j, :], H[:, :], [H], [hloc])
                dma("pool", aloc[s, :, j, :], A[:, :], [A], [aloc])

            stagger((chunk_gen(s, j) for s in range(NS) for j in range(6)), 4)
            for i in range(2):
                retire(al_sg[i], SGs[i])
            retire(al_tk, TOKB)
            hin = small()
            op("dve", lambda h: h.memset(hin[:, :], 0.0), [], [hin])
            if NR > 1:
                cdst = gather(HPREV[:, :], HPREV, 128, 16)
                G = tm32()
                dma("sp", G.ap[:, 0:NR * 16].rearrange("p (r c) -> p r c", c=16), cdst.ap.rearrange("(r p) c -> p r c", p=128), [cdst], [G])
                for j in range(NR - 1):
                    dp = small()
                    op("dve", lambda h, j=j, dp=dp: h.tensor_scalar(dp[:, 0:6], G[:, 16 * j + 8:16 * j + 14], RANKC[:, 65 + j:66 + j], RANKC[:, 69 + j:70 + j],
                                                                    ALU.mult, ALU.add), [G, RANKC], [dp])
                    op("dve", lambda h, j=j, dp=dp: h.tensor_scalar(dp[:, 6:12], G[:, 16 * j:16 * j + 6], RANKC[:, 65 + j:66 + j], None, ALU.mult), [G, RANKC], [dp])
                    op("dve", lambda h, dp=dp: h.tensor_tensor(hin[:, 0:6], hin[:, 0:6], dp[:, 0:6], ALU.mult), [hin, dp], [hin])
                    op("dve", lambda h, dp=dp: h.tensor_tensor(hin[:, 0:6], hin[:, 0:6], dp[:, 6:12], ALU.add), [hin, dp], [hin])
            op("dve", lambda h: h.tensor_copy(HPREV[:, 0:6], hin[:, 0:6]), [hin], [HPREV])

            def X(s, SG):
                xt = load_xT(pi, s)
                yield from qx_gate_attn(xt, SG)
                for j in range(6):
                    H, A = wk32(), wk32()
                    dma("sp", H[:, 0:512], hloc[s, :, j, :], [hloc], [H])
                    dma("sp", A[:, 0:512], aloc[s, :, j, :], [aloc], [A])
                    op("dve", lambda h, j=j, H=H, A=A: h.scalar_tensor_tensor(H[:, 0:512], A[:, 0:512], HPREV[:, j:j + 1], H[:, 0:512], ALU.mult, ALU.add),
                       [A, HPREV, H], [H])
                    op("dve", lambda h, j=j, H=H: h.tensor_tensor(SG[:, j, :], H[:, 0:512], SG[:, j, :], ALU.mult), [H, SG], [SG])
                    yield
            pipeline(X, pi, last, 2)

        def layer1(l, pi, last, nxt):
            oloc = kb.dram([NS, 128, 6, ST], F32, "oloc")
            qtl = kb.dram([NS, 128, 6, ST], BF16, "qtl")
            ensure_w(l, "WA")
            ensure_w(l, "WB1")
            ensure_w(l, "WC")
            SG = SGs[0]
            if NR > 1:
                op("dve", lambda h: h.memset(HPREV[:, 0:6], 0.0), [], [HPREV])
            flat = LCB.ap.rearrange("p a b -> p (a b)")
            S32 = [Buf(flat[:, 256 * h:256 * h + 256].bitcast(F32), "S32_%d" % h) for h in range(6)]
            SBF = [Buf(flat[:, 1536 + 128 * h:1536 + 128 * h + 128], "SBF_%d" % h) for h in range(6)]
            rqk = Ring([Buf(flat[:, 2304 + 256 * i:2304 + 256 * i + 256], "rqk_%d" % i) for i in range(3)]
                       + [Buf(flat[:, 3712 + 256 * i:3712 + 256 * i + 256], "rqk_%d" % (3 + i)) for i in range(2)])
            r16 = Ring([Buf(flat[:, 3072 + 128 * i:3072 + 128 * i + 128], "r16_%d" % i) for i in range(3)]
                       + [Buf(flat[:, 4224 + 128 * i:4224 + 128 * i + 128], "r16_%d" % (3 + i)) for i in range(3)])
            psU = [Ring(PB[0:3]), Ring(PB[3:6])]
            sg1f = SGs[1].ap.rearrange("p a b -> p (a b)")
            al_b = alias_bufs(SGs[1], [sg1f[:, 768 * k:768 * (k + 1)] for k in range(5)], "sg1al")
            tmA = Ring(tm32.bufs + ln32.bufs)
            tmB = Ring(tm16.bufs + al_b)
            ONESM = Buf(flat[:, 3456:3712].bitcast(F32), "ONESM")
            op("dve", lambda h: h.memset(LCB[:, :, :], 0.0), [], [LCB])
            for b_ in S32 + SBF + r16.bufs + rqk.bufs + [ONESM]:
                b_.lastw = LCB.lastw
            op("dve", lambda h: h.memset(ONESM[:, :], 1.0 / 128), [], [ONESM])
            es_ = []
            for k in range(4):
                t = tm32()
                dma("sp", t[:, :], hlb_in[k:k + 1, :].partition_broadcast(128), [hlb_in], [t])
                op("act", lambda h, t=t: h.activation(t[:, :], t[:, :], AF.Exp), [t], [t])
                es_.append(t)
            tot = tm32()
            op("dve", lambda h: h.tensor_tensor(tot[:, :], es_[0][:, :], es_[1][:, :], ALU.add), [es_[0], es_[1]], [tot])
            op("dve", lambda h: h.tensor_tensor(tot[:, :], tot[:, :], es_[2][:, :], ALU.add), [tot, es_[2]], [tot])
            op("dve", lambda h: h.tensor_tensor(tot[:, :], tot[:, :], es_[3][:, :], ALU.add), [tot, es_[3]], [tot])
            op("dve", lambda h: h.reciprocal(tot[:, :], tot[:, :]), [tot], [tot])
            for k in range(2, l + 1):
                op("dve", lambda h, k=k: h.tensor_tensor(es_[1][:, :], es_[1][:, :], es_[k][:, :], ALU.add), [es_[1], es_[k]], [es_[1]])
            op("dve", lambda h: h.tensor_tensor(LC[:, 0:768], es_[1][:, :], tot[:, :], ALU.mult), [es_[1], tot], [LC])
            op("dve", lambda h: h.tensor_scalar(LC[:, 768:1536], LC[:, 0:768], -1.0, 1.0, ALU.mult, ALU.add), [LC], [LC])
            def subtile_gen(s, sub, xt, par):
                    psb = psU[par]
                    pq1, pq2 = proj_tm(WB, 0, 512, xt, sub, psb), proj_tm(WB, 512, 256, xt, sub, psb)
                    QS = tmA()
                    op("act", lambda h: h.activation(QS[:, 0:512], pq1[:, :], AF.Silu), [pq1], [QS])
                    op("act", lambda h: h.activation(QS[:, 512:768], pq2[:, 0:256], AF.Silu), [pq2], [QS])
                    yield
                    pf1, pf2 = proj_tm(WA, 0, 512, xt, sub, psb), proj_tm(WA, 512, 256, xt, sub, psb)
                    FK = tmA()
                    op("act", lambda h: h.activation(FK[:, 0:512], pf1[:, :], AF.Sigmoid), [pf1], [FK])
                    op("act", lambda h: h.activation(FK[:, 512:768], pf2[:, 0:256], AF.Sigmoid), [pf2], [FK])
                    op("dve", lambda h: h.tensor_tensor(FK[:, 0:768], FK[:, 0:768], LC[:, 768:1536], ALU.mult), [FK, LC], [FK])
                    op("pool", lambda h: h.tensor_tensor(FK[:, 0:768], FK[:, 0:768], LC[:, 0:768], ALU.add), [FK, LC], [FK])
                    LOGF = tmA()
                    op("act", lambda h: h.activation(LOGF[:, 0:768], FK[:, 0:768], AF.Ln), [FK], [LOGF])
                    op("pool", lambda h: h.tensor_scalar(FK[:, 0:768], FK[:, 0:768], -1.0, 1.0, ALU.mult, ALU.add), [FK], [FK])
                    yield
                    pi1, pi2 = proj_tm(WA, 768, 512, xt, sub, psb), proj_tm(WA, 1280, 256, xt, sub, psb)
                    V = tmB()
                    op("act", lambda h: h.copy(V[:, 0:512], pi1[:, :]), [pi1], [V])
                    op("dve", lambda h: h.tensor_copy(V[:, 512:768], pi2[:, 0:256]), [pi2], [V])
                    yield
                    pg1, pg2 = psb(), psb()
                    op("pe", lambda h: h.matmul(pg1[:, :], CONST[:, C_TRI64:C_TRI64 + 128], LOGF[:, 0:512], start=True, stop=True), [CONST, LOGF], [pg1])
                    op("pe", lambda h: h.matmul(pg2[:, 0:256], CONST[:, C_TRI64:C_TRI64 + 128], LOGF[:, 512:768], start=True, stop=True), [CONST, LOGF], [pg2])
                    E1, E2 = tmA(), tmA()
                    op("act", lambda h: h.activation(E1[:, 0:512], pg1[:, :], AF.Exp), [pg1], [E1])
                    op("act", lambda h: h.activation(E1[:, 512:768], pg2[:, 0:256], AF.Exp), [pg2], [E1])
                    op("act", lambda h: h.activation(E2[:, 0:512], pg1[:, :], AF.Exp, scale=-1.0), [pg1], [E2])
                    op("act", lambda h: h.activation(E2[:, 512:768], pg2[:, 0:256], AF.Exp, scale=-1.0), [pg2], [E2])
                    QD, KI = tmB(), tmB()
                    op("dve", lambda h: h.tensor_tensor(QD[:, :], QS[:, 0:768], E1[:, 0:768], ALU.mult), [QS, E1], [QD])
                    op("pool", lambda h: h.tensor_tensor(KI[:, :], FK[:, 0:768], E2[:, 0:768], ALU.mult), [FK, E2], [KI])
                    yield
                    pr1, pr2 = psb(), psb()
                    op("pe", lambda h: h.matmul(pr1[:, :], CONST[:, C_TRIREV64:C_TRIREV64 + 128], LOGF[:, 0:512], start=True, stop=True), [CONST, LOGF], [pr1])
                    op("pe", lambda h: h.matmul(pr2[:, 0:256], CONST[:, C_TRIREV64:C_TRIREV64 + 128], LOGF[:, 512:768], start=True, stop=True), [CONST, LOGF], [pr2])
                    E3 = tmA()
                    op("act", lambda h: h.activation(E3[:, 0:512], pr1[:, :], AF.Exp), [pr1], [E3])
                    op("act", lambda h: h.activation(E3[:, 512:768], pr2[:, 0:256], AF.Exp), [pr2], [E3])
                    KE = tmB()
                    op("dve", lambda h: h.tensor_tensor(KE[:, :], FK[:, 0:768], E3[:, 0:768], ALU.mult), [FK, E3], [KE])
                    yield
                    pgl = psb()
                    for hd in range(6):
                        op("pe", lambda h, hd=hd: h.matmul(pgl[:, hd * 4:hd * 4 + 2], LOGF[:, hd * 128:(hd + 1) * 128], CONST[:, C_IND:C_IND + 2],
                                                           start=True, stop=True), [LOGF, CONST], [pgl])
                    DEC = small()
                    op("act", lambda h: h.activation(DEC[:, 0:12].rearrange("p (h c) -> p h c", c=2),
                                                     pgl[:, 0:24].rearrange("p (h f) -> p h f", f=4)[:, :, 0:2], AF.Exp), [pgl], [DEC])
                    if NR > 1:
                        gl, gs, cdec = small(), small(), small()
                        glv = gl.ap[:, 0:12].rearrange("p (h c) -> p h c", c=2)
                        op("dve", lambda h: h.tensor_copy(glv, pgl[:, 0:24].rearrange("p (h f) -> p h f", f=4)[:, :, 0:2]), [pgl], [gl])
                        op("dve", lambda h: h.tensor_copy(gs[:, 0:6], HPREV[:, 0:6]), [HPREV], [gs])
                        op("dve", lambda h: h.tensor_tensor(gs[:, 6:12], HPREV[:, 0:6], glv[:, :, 0], ALU.add), [HPREV, gl], [gs])
                        op("dve", lambda h: h.tensor_tensor(HPREV[:, 0:6], gs[:, 6:12], glv[:, :, 1], ALU.add), [gs, gl], [HPREV])
                        op("act", lambda h: h.activation(cdec[:, 0:12], gs[:, 0:12], AF.Exp), [gs], [cdec])
                    yield
                    for hd in range(6):
                        hc = slice(hd * 128, (hd + 1) * 128)
                        sl = (hd % 4) * 2
                        op("pe", lambda h: h.transpose(p6bf[:, sl, :], QD[:, hc], IDB[:, :]), [QD, IDB], [P6])
                        op("pe", lambda h: h.transpose(p6bf[:, sl + 1, :], KI[:, hc], IDB[:, :]), [KI, IDB], [P6])
                        QK, SCM = rqk(), r16()
                        op("act", lambda h: h.copy(QK[:, :], p6bf[:, sl:sl + 2, :].rearrange("p a b -> p (a b)")), [P6], [QK])
                        QDT, KIT = QK.ap[:, 0:128], QK.ap[:, 128:256]
                        if NR > 1:
                            for c in range(2):
                                op("dve", lambda h, c=c: h.tensor_scalar(SG[:, hd, sub * 128 + 64 * c:sub * 128 + 64 * c + 64], QDT[:, 64 * c:64 * c + 64],
                                                                         cdec[:, c * 6 + hd:c * 6 + hd + 1], None, ALU.mult), [QK, cdec], [SG])
                        yield
                        SC = psb()
                        op("pe", lambda h: h.matmul(SC[:, 0:128], KIT, QDT, start=True, stop=True), [QK], [SC])
                        op("dve", lambda h: h.tensor_tensor(SCM[:, :], SC[:, 0:128], CONST[:, C_TRI64:C_TRI64 + 128], ALU.mult), [SC, CONST], [SCM])
                        yield
                        OT = psb()
                        op("pe", lambda h: h.matmul(OT[:, 0:128], V[:, hc], SCM[:, :], start=True, stop=False), [V, SCM], [OT])
                        for c in range(2):
                            op("pe", lambda h, c=c: h.matmul(OT[:, 64 * c:64 * c + 64], SBF[hd][:, :], QDT[:, 64 * c:64 * c + 64],
                                                             start=False, stop=(c == 1)), [SBF[hd], QK], [OT])
                            U = psb()
                            op("pe", lambda h, c=c, U=U: h.matmul(U[:, 0:128], KE[64 * c:64 * c + 64, hc], V[64 * c:64 * c + 64, hc],
                                                                  start=True, stop=True), [KE, V], [U])
                            op("dve", lambda h, c=c, U=U: h.scalar_tensor_tensor(SBF[hd][:, :], S32[hd][:, :], DEC[:, hd * 2 + c:hd * 2 + c + 1], U[:, 0:128],
                                                                                 ALU.mult, ALU.add), [S32[hd], DEC, U], [SBF[hd]])
                            op("dve", lambda h, c=c, U=U: h.scalar_tensor_tensor(S32[hd][:, :], S32[hd][:, :], DEC[:, hd * 2 + c:hd * 2 + c + 1], U[:, 0:128],
                                                                                 ALU.mult, ALU.add), [S32[hd], DEC, U], [S32[hd]])
                        op("act", lambda h: h.copy(TOKB[:, hd, sub * 128:(sub + 1) * 128], OT[:, 0:128]), [OT], [TOKB])
                        yield

            for s in range(NS):
                xt = load_xT(pi, s)
                stagger((subtile_gen(s, sub, xt, sub % 2) for sub in range(4)), 2)
                dma("pool", oloc[s, :, :, :], TOKB[:, :, :], [TOKB], [oloc])
                if NR > 1:
                    dma("pool", qtl[s, :, :, :], SG[:, 0:6, :], [SG], [qtl])
            retire(al_b, SGs[1])
            if NR == 1:
                ensure_w(l, "WB")
                ensure_w(nxt, "WA")
            if NR > 1:
                tf = TOKB.ap.rearrange("p a b -> p (a b)")
                for hd in range(6):
                    op("dve", lambda h, hd=hd: h.tensor_copy(tf[:, hd * 128:(hd + 1) * 128], S32[hd][:, :]), [S32[hd]], [TOKB])
                op("act", lambda h: h.activation(tf[:, 768:774], HPREV[:, 0:6], AF.Exp), [HPREV], [TOKB])
                cdst = gather(tf[:, 0:774], TOKB, 128, 774)
                ensure_w(l, "WB")
                ensure_w(nxt, "WA")
                dma("sp", tf[:, 0:(NR - 1) * 774].rearrange("p (r c) -> p r c", c=774),
                    cdst.ap[0:(NR - 1) * 128, :].rearrange("(r p) c -> p r c", p=128), [cdst], [TOKB])
                for hd in range(6):
                    op("dve", lambda h, hd=hd: h.memset(S32[hd][:, :], 0.0), [], [S32[hd]])
                for j in range(NR - 1):
                    dp = small()
                    o = j * 774
                    op("dve", lambda h, j=j, o=o, dp=dp: h.tensor_scalar(dp[:, 0:6], tf[:, o + 768:o + 774], RANKC[:, 65 + j:66 + j], RANKC[:, 69 + j:70 + j],
                                                                         ALU.mult, ALU.add), [TOKB, RANKC], [dp])
                    op("dve", lambda h, j=j, o=o: h.tensor_scalar(tf[:, o:o + 768], tf[:, o:o + 768], RANKC[:, 65 + j:66 + j], None, ALU.mult), [TOKB, RANKC], [TOKB])
                    for hd in range(6):
                        op("dve", lambda h, hd=hd, o=o, dp=dp: h.scalar_tensor_tensor(S32[hd][:, :], S32[hd][:, :], dp[:, hd:hd + 1], tf[:, o + hd * 128:o + (hd + 1) * 128],
                                                                                      ALU.mult, ALU.add), [S32[hd], dp, TOKB], [S32[hd]])
                for hd in range(6):
                    op("act", lambda h, hd=hd: h.copy(SBF[hd][:, :], S32[hd][:, :]), [S32[hd]], [SBF[hd]])
            load_ln(l)
            dma("sp", LC[:, 0:6], b_ng_in[:, :], [b_ng_in], [LC])

            def X(s, SG):
                xt = load_xT(pi, s)
                dma("sp", TOKB[:, :, :], oloc[s, :, :, :], [oloc], [TOKB])
                yield from qx_gate_attn(xt, SG, act_recip=True)
                if NR > 1:
                    for hd in range(6):
                        qt = wk16()
                        dma("sp", qt[:, :], qtl[s, :, hd, :], [qtl], [qt])
                        pc = psX()
                        op("pe", lambda h, hd=hd, qt=qt, pc=pc: h.matmul(pc[:, :], SBF[hd][:, :], qt[:, :], start=True, stop=True), [SBF[hd], qt], [pc])
                        op("dve", lambda h, hd=hd, pc=pc: h.tensor_tensor(TOKB[:, hd, :], TOKB[:, hd, :], pc[:, :], ALU.add), [TOKB, pc], [TOKB])
                        yield
                if s == NS - 1 and nxt is not None:
                    ensure_w(nxt, first_wb(nxt))
                rs = {}
                for grp in ((0, 1, 2), (3, 4, 5)):
                    pms = {}
                    for hd in grp:
                        sq = wk32()
                        op("act", lambda h, hd=hd, sq=sq: h.activation(sq[:, 0:512], TOKB[:, hd, :], AF.Square), [TOKB], [sq])
                        pm = psX()
                        op("pe", lambda h, sq=sq, pm=pm: h.matmul(pm[:, :], ONESM[:, :], sq[:, 0:512], start=True, stop=True), [ONESM, sq], [pm])
                        rs[hd], pms[hd] = sq, pm
                    yield
                    for hd in grp:
                        op("act", lambda h, hd=hd: h.activation(rs[hd][:, 0:512], pms[hd][:, :], AF.Ln, bias=EPSC[:, 1:2], scale=1.0),
                           [pms[hd], EPSC], [rs[hd]])
                    yield
                for hd in range(6):
                    r = rs[hd]
                    op("act", lambda h, r=r: h.activation(r[:, 0:512], r[:, 0:512], AF.Exp, scale=-0.5), [r], [r])
                    op("dve", lambda h, hd=hd, r=r: h.scalar_tensor_tensor(r[:, 0:512], TOKB[:, hd, :], LC[:, hd:hd + 1], r[:, 0:512], ALU.mult, ALU.mult),
                       [TOKB, LC, r], [r])
                    op("dve", lambda h, hd=hd, r=r: h.tensor_tensor(SG[:, hd, :], r[:, 0:512], SG[:, hd, :], ALU.mult), [r, SG], [SG])
                    yield

            def hook():
                if nxt is not None:
                    ensure_w(nxt, first_wb(nxt))
            pipeline(X, pi, last, 3, hook)

        layers = [layer0, layer1, layer2, layer3]
        for i, l in enumerate(ll):
            layers[l](l, i, i == len(ll) - 1, ll[i + 1] if i + 1 < len(ll) else None)
        kb.finish([out_d])
        print("instructions:", kb.ninstr)
    return nc


def host_inputs(inputs, T, NR, S):
    f = lambda a: np.ascontiguousarray(np.asarray(a, dtype=np.float32))
    x = f(inputs["x"])
    mem = f(inputs["mem"])
    consts = np.zeros((128, NCONST), np.float32)
    consts[:, 0:128] = np.eye(128, dtype=np.float32)
    idx = np.arange(128)
    consts[:, 128:256] = (idx[:, None] <= idx[None, :]).astype(np.float32)
    same = (idx[:, None] // 64) == (idx[None, :] // 64)
    consts[:, 256:384] = ((idx[:, None] <= idx[None, :]) & same).astype(np.float32)
    consts[:, 384:512] = ((idx[:, None] > idx[None, :]) & same).astype(np.float32)
    consts[0:64, 512] = 1.0
    consts[64:128, 513] = 1.0
    for j in range(6):
        for p in range(128):
            g = (j * 128 + p) // 192
            consts[p, 516 + j * 4 + g] = 1.0 / (2 ** (g + 1))
    common = {
        "mem_kv_w": f(inputs["mem_kv_w"]), "ln_g": f(inputs["ln_g"]), "ln_b": f(inputs["ln_b"]),
        "w_out": f(inputs["w_out"]), "hgrn_lb_logits": f(inputs["hgrn_lb_logits"]),
        "lbvec": np.zeros((2, TOK), np.float32),
        "a_w_in": f(inputs["a_w_in"][0]), "b_w_in": f(inputs["b_w_in"][0]),
        "c_w_in": f(inputs["c_w_in"][0]), "d_w_in": f(inputs["d_w_in"][0]),
        "a_w_sT": f(np.transpose(np.asarray(inputs["a_w_s"][0]), (2, 0, 1))),
        "a_b_s": f(np.asarray(inputs["a_b_s"][0]).reshape(1, TOK)),
        "b_norm_g": f(np.asarray(inputs["b_norm_g"][0]).reshape(6, 128).T),
        "c_w_pool": f(inputs["c_w_pool"][0]),
        "c_scale": f(np.asarray(inputs["c_scale"][0]).reshape(6, 128).T),
        "d_conv_w": f(np.transpose(np.asarray(inputs["d_conv_w"][0]).reshape(4, 6, 128), (2, 1, 0))),
        "d_conv_b": f(np.asarray(inputs["d_conv_b"][0]).reshape(6, 128).T),
        "d_w_gx": f(np.transpose(np.asarray(inputs["d_w_gx"][0]), (1, 0, 2))),
        "d_b_gx": f(np.asarray(inputs["d_b_gx"][0]).T),
        "d_w_ga": f(np.transpose(np.asarray(inputs["d_w_ga"][0]), (1, 0, 2))),
        "d_b_ga": f(np.asarray(inputs["d_b_ga"][0]).T),
        "d_a_param": f(np.asarray(inputs["d_a_param"][0]).reshape(6, 128).T),
        "consts": consts,
    }
    maps = []
    for c in range(8):
        b, r = (c // NR) % 2, c % NR
        m = dict(common)
        m["x"] = np.ascontiguousarray(x[b, r * T:(r + 1) * T, :])
        m["memT"] = np.ascontiguousarray(mem[b].T.reshape(8, 128, NMEM).transpose(1, 0, 2))
        rk = np.zeros((128, 128), np.float32)
        rk[:, 64] = 1.0 if r == 0 else 0.0
        for j in range(4):
            rk[:, 65 + j] = 1.0 if j < r else 0.0
            rk[:, 69 + j] = 0.0 if j < r else 1.0
        if r > 0:
            for t_ in range(16):
                rk[(r - 1) * 16 + t_, 80 + t_] = 1.0
        for k, wdw in enumerate((2, 4, 8, 16)):
            tt = np.arange(16)
            rk[:, 16 * k:16 * k + 16] = (wdw / np.minimum(tt + 1, wdw))[None, :] if r == 0 else 1.0
        m["rankc"] = rk
        maps.append(m)
    return maps


_NC_CACHE = {}


def run(inputs, S, NR, nlayers=4, layer_list=None):
    T = S // NR
    key = (T, NR, nlayers, tuple(layer_list) if layer_list else None)
    if key not in _NC_CACHE:
        _NC_CACHE[key] = build(T, NR, nlayers, layer_list)
    nc = _NC_CACHE[key]
    maps = host_inputs(inputs, T, NR, S)
    res = run_bass_kernel_spmd(nc, maps, core_ids=list(range(8)))
    out = np.zeros((2, S, D), np.float32)
    for c in range(8):
        b, r = (c // NR) % 2, c % NR
        if c < 2 * NR:
            out[b, r * T:(r + 1) * T, :] = res.results[c]["out"]
    return out


def kernel(**inputs):
    S = inputs["x"].shape[1]
    return run(inputs, S, 4)
```

```python
import contextlib
import numpy as np
import concourse.bass as bass
import concourse.mybir as mybir
from concourse.bass_utils import run_bass_kernel_spmd

F32 = mybir.dt.float32
BF16 = mybir.dt.bfloat16
AF = mybir.ActivationFunctionType
ALU = mybir.AluOpType
AX = mybir.AxisListType

D = 1024
NMEM = 256
TOK = 768
ALPHA = float((2 * 4) ** 0.25)
LN_EPS = 1e-5
RMS_EPS = 1e-6
LRU_C = 8.0
IN_W = [2816, 3584, 2048, 2048]
ST = 512
NCONST = 640
NCSB = 416


class Buf:
    __slots__ = ("ap", "name", "lastw", "readers", "psum", "gen")

    def __init__(self, ap, name, psum=False):
        self.ap = ap
        self.name = name
        self.psum = psum
        self.lastw = None
        self.readers = {}
        self.gen = 0

    def __getitem__(self, k):
        return self.ap[k]


class View:
    __slots__ = ("base", "gen")

    def __init__(self, base):
        base.gen += 1
        self.base, self.gen = base, base.gen

    def __getitem__(self, k):
        return self.base.ap[k]

    @property
    def ap(self):
        return self.base.ap

    @property
    def name(self):
        return self.base.name


class Eng:
    def __init__(self, name, h, sem):
        self.name, self.h, self.sem, self.cnt = name, h, sem, 0
        self.waited = {}


class KB:
    def __init__(self, nc, es):
        self.nc, self.es = nc, es
        self.eng = {}
        for name, h in (("pe", nc.tensor), ("act", nc.scalar), ("dve", nc.vector),
                        ("pool", nc.gpsimd), ("sp", nc.sync)):
            self.eng[name] = Eng(name, h, es.enter_context(nc.semaphore("s_" + name)))
        self.dsems = {}
        for q in ("sp", "pool"):
            self.dsems[q] = [[es.enter_context(nc.semaphore("d_%s%d" % (q, i))), 0] for i in range(20)]
        self.dnext = {"sp": 0, "pool": 0}
        self.ccsem = es.enter_context(nc.semaphore("s_cc"))
        self.cccnt = 0
        self.nb = 0
        self.ninstr = 0

    def sb(self, shape, dt, name=None):
        self.nb += 1
        name = name or "b%d" % self.nb
        t = self.es.enter_context(self.nc.sbuf_tensor(name, list(shape), dt))
        return Buf(t, name)

    def ps(self, shape, dt, name=None):
        self.nb += 1
        name = name or "p%d" % self.nb
        t = self.es.enter_context(self.nc.psum_tensor(name, list(shape), dt))
        return Buf(t, name, psum=True)

    def dram(self, shape, dt, name):
        t = self.nc.dram_tensor(name, list(shape), dt)
        return Buf(t.ap(), name)

    @staticmethod
    def _res(lst):
        out = []
        for x in lst:
            if isinstance(x, View):
                if x.gen != x.base.gen:
                    raise RuntimeError("stale ring buffer use: %s (gen %d, now %d)" % (x.base.name, x.gen, x.base.gen))
                x = x.base
            out.append(x)
        return out

    def _wait(self, e, tok):
        sem, val, sid = tok
        if e.waited.get(sid, 0) < val:
            e.h.wait_ge(sem, val)
            e.waited[sid] = val
            self.ninstr += 1

    def _deps(self, e, reads, writes):
        for b in reads:
            if b.lastw is not None:
                tok, src = b.lastw
                if not (src == "pe" and e.name == "pe"):
                    self._wait(e, tok)
            if b.psum:
                for key, (tok, src) in b.readers.items():
                    if src != e.name:
                        self._wait(e, tok)
        for b in writes:
            if b.lastw is not None:
                tok, src = b.lastw
                if not (src == "pe" and e.name == "pe"):
                    self._wait(e, tok)
            for key, (tok, src) in b.readers.items():
                if not (src == "pe" and e.name == "pe"):
                    self._wait(e, tok)

    def _commit(self, tok, src, reads, writes):
        for b in writes:
            b.lastw = (tok, src)
            b.readers = {}
        for b in reads:
            key = src if not src.startswith("dma") else "dma%d" % id(tok[0])
            b.readers[key] = (tok, src)

    def op(self, engname, fn, reads=(), writes=()):
        e = self.eng[engname]
        reads, writes = self._res(reads), self._res(writes)
        self._deps(e, reads, writes)
        ins = fn(e.h)
        e.cnt += 1
        ins.then_inc(e.sem, 1)
        self.ninstr += 1
        tok = (e.sem, e.cnt, id(e.sem))
        self._commit(tok, engname, reads, writes)
        return tok

    def dma(self, q, out_ap, in_ap, reads=(), writes=()):
        e = self.eng[q]
        reads, writes = self._res(reads), self._res(writes)
        self._deps(e, reads, writes)
        lst = self.dsems[q]
        i = self.dnext[q]
        self.dnext[q] = (i + 1) % len(lst)
        sem, tot = lst[i]
        if tot > 0:
            self._wait(e, (sem, tot, id(sem)))
        e.h.dma_start(out=out_ap, in_=in_ap).then_inc(sem, 16)
        self.ninstr += 1
        lst[i][1] = tot + 16
        tok = (sem, tot + 16, id(sem))
        self._commit(tok, "dma_" + q, reads, writes)
        return tok

    def allgather(self, groups, src, dst):
        e = self.eng["pool"]
        self._deps(e, [src], [dst])
        e.h.collective_compute("AllGather", ALU.bypass, groups, ins=[src.ap.opt()], outs=[dst.ap.opt()]).then_inc(self.ccsem)
        self.cccnt += 1
        self.ninstr += 1
        tok = (self.ccsem, self.cccnt, id(self.ccsem))
        self._commit(tok, "cc", [src], [dst])
        return tok

    def finish(self, bufs):
        e = self.eng["sp"]
        for b in bufs:
            if b.lastw is not None:
                self._wait(e, b.lastw[0])


class Ring:
    def __init__(self, bufs):
        self.bufs, self.i = bufs, 0

    def __call__(self):
        b = self.bufs[self.i]
        self.i = (self.i + 1) % len(self.bufs)
        return View(b)


def run_gen(g):
    for _ in g:
        pass


def stagger(gens, depth):
    it = iter(gens)
    active, more = [], True
    while True:
        if more and len(active) < depth:
            try:
                active.append(next(it))
            except StopIteration:
                more = False
        if not active:
            break
        for g in list(active):
            try:
                next(g)
            except StopIteration:
                active.remove(g)


def stagger_ticks(gens, depth):
    it = iter(gens)
    active, more = [], True
    while True:
        if more and len(active) < depth:
            try:
                active.append(next(it))
            except StopIteration:
                more = False
        if not active:
            break
        for g in list(active):
            try:
                next(g)
            except StopIteration:
                active.remove(g)
        yield


def alias_bufs(parent, aps, prefix):
    out = []
    for i, ap in enumerate(aps):
        b = Buf(ap, "%s%d" % (prefix, i))
        b.lastw = parent.lastw
        b.readers = dict(parent.readers)
        out.append(b)
    return out


def retire(aliases, parent):
    for a in aliases:
        if a.lastw is not None:
            parent.readers[a.name + "#w"] = a.lastw
        for key, v in a.readers.items():
            parent.readers[a.name + "#" + key] = v


def interleave(gx, gy, r):
    dx = dy = False
    while not (dx and dy):
        for _ in range(r):
            if not dx:
                try:
                    next(gx)
                except StopIteration:
                    dx = True
        if not dy:
            try:
                next(gy)
            except StopIteration:
                dy = True


def build(T, NR, nlayers=4, layer_list=None):
    assert T % ST == 0
    NS = T // ST
    nc = bass.Bass("TRN2", target_bir_lowering=False)
    es = contextlib.ExitStack()
    groups = [list(range(g * NR, (g + 1) * NR)) for g in range(8 // NR)]

    def din(name, shape):
        return Buf(nc.dram_tensor(name, list(shape), F32, kind="ExternalInput").ap(), name)

    x_in = din("x", [T, D])
    memT_in = din("memT", [128, 8, NMEM])
    kvw_in = din("mem_kv_w", [D, 512])
    lng_in = din("ln_g", [4, D])
    lnb_in = din("ln_b", [4, D])
    wout_in = din("w_out", [4, D, D])
    lb_in = din("lbvec", [2, TOK])
    hlb_in = din("hgrn_lb_logits", [4, TOK])
    w_in = [din("a_w_in", [D, IN_W[0]]), din("b_w_in", [D, IN_W[1]]),
            din("c_w_in", [D, IN_W[2]]), din("d_w_in", [D, IN_W[3]])]
    a_wsT_in = din("a_w_sT", [128, 6, 128])
    a_bs_in = din("a_b_s", [1, TOK])
    b_ng_in = din("b_norm_g", [128, 6])
    c_wp_in = din("c_w_pool", [4, 192, 192])
    c_sc_in = din("c_scale", [128, 6])
    d_cw_in = din("d_conv_w", [128, 6, 4])
    d_cb_in = din("d_conv_b", [128, 6])
    d_wgx_in = din("d_w_gx", [128, 6, 128])
    d_bgx_in = din("d_b_gx", [128, 6])
    d_wga_in = din("d_w_ga", [128, 6, 128])
    d_bga_in = din("d_b_ga", [128, 6])
    d_ap_in = din("d_a_param", [128, 6])
    consts_in = din("consts", [128, NCONST])
    rankc_in = din("rankc", [128, 128])
    out_d = Buf(nc.dram_tensor("out", [T, D], F32, kind="ExternalOutput").ap(), "out")

    with es:
        kb = KB(nc, es)
        op, dma = kb.op, kb.dma
        xres = [kb.dram([T, D], F32, "xres%d" % i) for i in range(2)]
        xT_d = [kb.dram([128, 8, T], BF16, "xT_d%d" % i) for i in range(2)]
        WA = kb.sb([128, 8, 1536], BF16, "WA")
        WB = kb.sb([128, 8, 1280], BF16, "WB")
        WC = kb.sb([128, 8, 1024], BF16, "WC")
        CONST = kb.sb([128, NCSB], F32, "CONST")
        RANKC = kb.sb([128, 128], F32, "RANKC")
        IDB = kb.sb([128, 128], BF16, "IDB")
        KTP = kb.sb([128, 4, NMEM], BF16, "KTP")
        VP = kb.sb([128, 2, 4, 128], BF16, "VP")
        OP_ = kb.sb([128, 2, 128], BF16, "OP")
        LNG = kb.sb([128, D], F32, "LNG")
        LNB = kb.sb([128, D], F32, "LNB")
        xT_t = Ring([kb.sb([128, 8, ST], BF16, "xT%d" % i) for i in range(2)])
        xst = Ring([kb.sb([128, 8, 128], BF16, "xst%d" % i) for i in range(2)])
        SGs = [kb.sb([128, 8, ST], BF16, "SG%d" % i) for i in range(2)]
        QT = kb.sb([128, 2, ST], BF16, "QT")
        wk32 = Ring([kb.sb([128, ST + 16], F32, "wk32_%d" % i) for i in range(10)])
        wk16 = Ring([kb.sb([128, ST], BF16, "wk16_%d" % i) for i in range(6)])
        ln32 = Ring([kb.sb([128, D], F32, "ln32_%d" % i) for i in range(4)])
        tm32 = Ring([kb.sb([128, TOK], F32, "tm32_%d" % i) for i in range(5)])
        tm16 = Ring([kb.sb([128, TOK], BF16, "tm16_%d" % i) for i in range(4)])
        xnb_r = Ring([kb.sb([128, D], BF16, "xnb%d" % i) for i in range(1)])
        small = Ring([kb.sb([128, 16], F32, "sm%d" % i) for i in range(18)])
        smallY = Ring([kb.sb([128, 24], F32, "smY%d" % i) for i in range(4)])
        LC = kb.sb([128, 1536], F32, "LC")
        LCB = kb.sb([128, 6, 768], BF16, "LCB")
        TOKB = kb.sb([128, 6, ST], F32, "TOKB")
        HALO = kb.sb([128, 6, 16], F32, "HALO")
        HPREV = kb.sb([128, 16], F32, "HPREV")
        EPSC = kb.sb([128, 4], F32, "EPSC")
        XTH = kb.sb([128, 8, 16], BF16, "XTH")
        print("sbuf remaining", nc.sbuf_bytes_remaining)
        PB = [kb.ps([128, 512], F32, "psb%d" % i) for i in range(7)]
        pst = kb.ps([128, 8, 128], BF16, "pst")
        psb = Ring(PB[0:6])
        psX = Ring(PB[0:3])
        psY = Ring(PB[3:7])
        P6 = PB[6]
        p6bf = P6.ap[:, :].bitcast(BF16).rearrange("p (a b) -> p a b", b=128)

        dma("sp", CONST[:, :], consts_in[:, 128:128 + NCSB], [consts_in], [CONST])
        dma("sp", RANKC[:, :], rankc_in[:, :], [rankc_in], [RANKC])
        dma("pool", IDB[:, :], consts_in[:, 0:128], [consts_in], [IDB])
        C_TRI, C_TRI64, C_TRIREV64, C_IND, C_CK = 0, 128, 256, 384, 388
        op("dve", lambda h: h.memset(EPSC[:, 0:1], LN_EPS), [], [EPSC])
        op("dve", lambda h: h.memset(EPSC[:, 1:2], RMS_EPS), [], [EPSC])
        op("dve", lambda h: h.memset(EPSC[:, 2:3], 1.0), [], [EPSC])
        op("dve", lambda h: h.memset(EPSC[:, 3:4], 0.0), [], [EPSC])

        def load_w(dst, src_ap, ncols, c0=0):
            for k in range(8):
                dma("pool", dst[:, k, c0:c0 + ncols], src_ap[k * 128:(k + 1) * 128, :], [], [dst])

        wspec = {
            0: {"WA": (WA, lambda: w_in[0][:, 0:1536], 1536), "WB": (WB, lambda: w_in[0][:, 1536:2816], 1280)},
            1: {"WA": (WA, lambda: w_in[1][:, 768:2304], 1536), "WB1": (WB, lambda: w_in[1][:, 0:768], 768),
                "WB": (WB, lambda: w_in[1][:, 2304:3584], 1280)},
            2: {"WA": (WA, lambda: w_in[2][:, 0:768], 768), "WB": (WB, lambda: w_in[2][:, 768:2048], 1280)},
            3: {"WA": (WA, lambda: w_in[3][:, 0:768], 768), "WB": (WB, lambda: w_in[3][:, 768:2048], 1280)},
        }
        for l_ in range(4):
            wspec[l_]["WC"] = (WC, (lambda l_=l_: wout_in[l_, :, :]), 1024)
        resident = {}

        def ensure_w(l, key):
            if l is None:
                return
            dst, src, ncols = wspec[l][key]
            if resident.get(dst.name) != (l, key):
                load_w(dst, src(), ncols)
                resident[dst.name] = (l, key)

        def first_wb(l):
            return "WB1" if l == 1 else "WB"

        MEMT = xT_t()
        dma("pool", MEMT[:, :, 0:NMEM], memT_in[:, :, :], [memT_in], [MEMT])
        load_w(WB, kvw_in.ap, 512)
        op("pool", lambda h: h.memset(KTP[:, :, :], 0.0), [], [KTP])
        op("pool", lambda h: h.memset(VP[:, :, :, :], 0.0), [], [VP])
        op("pool", lambda h: h.memset(OP_[:, :, :], 0.0), [], [OP_])
        op("pool", lambda h: h.memset(OP_[:, 0, 0:64], 1.0), [], [OP_])
        op("pool", lambda h: h.memset(OP_[:, 1, 64:128], 1.0), [], [OP_])
        for c in range(2):
            p = psb()
            for k in range(8):
                op("pe", lambda h, k=k, c=c, p=p: h.matmul(p[:, 0:NMEM], WB[:, k, c * 128:(c + 1) * 128], MEMT[:, k, 0:NMEM],
                                                           start=(k == 0), stop=(k == 7)), [WB, MEMT], [p])
            op("act", lambda h, c=c, p=p: h.copy(KTP[0:64, 2 * c, :], p[0:64, 0:NMEM]), [p], [KTP])
            op("act", lambda h, c=c, p=p: h.copy(KTP[64:128, 2 * c + 1, :], p[64:128, 0:NMEM]), [p], [KTP])
        for mc in range(2):
            p = psb()
            for k in range(8):
                op("pe", lambda h, k=k, mc=mc, p=p: h.matmul(p[:, 0:256], MEMT[:, k, mc * 128:(mc + 1) * 128], WB[:, k, 256:512],
                                                             start=(k == 0), stop=(k == 7)), [WB, MEMT], [p])
            for hd in range(4):
                o = (hd % 2) * 64
                op("act", lambda h, hd=hd, mc=mc, p=p, o=o: h.copy(VP[:, mc, hd, o:o + 64], p[:, hd * 64:(hd + 1) * 64]), [p], [VP])

        ll = layer_list if layer_list is not None else list(range(nlayers))

        def transposes_to_xT(xn_f32, xT_dst, t0):
            xb = xnb_r()
            op("act", lambda h: h.copy(xb[:, :], xn_f32[:, :]), [xn_f32], [xb])
            for k in range(8):
                op("pe", lambda h, k=k: h.transpose(pst[:, k, :], xb[:, k * 128:(k + 1) * 128], IDB[:, :]), [xb, IDB], [pst])
            stg = xst()
            op("dve", lambda h: h.tensor_copy(stg[:, :, :], pst[:, :, :]), [pst], [stg])
            dma("pool", xT_dst[:, :, t0:t0 + 128], stg[:, :, :], [stg], [xT_dst])

        def pro_unit(t0):
            xr = ln32()
            dma("sp", xr[:, :], x_in[t0:t0 + 128, :], [x_in], [xr])
            yield
            transposes_to_xT(xr, xT_d[0], t0)

        def prologue(s):
            stagger((pro_unit(s * ST + sub * 128) for sub in range(4)), 2)

        lazy_pro = (ll[0] in (0, 2))
        for s in range(min(2, NS) if lazy_pro else NS):
            prologue(s)
        ensure_w(ll[0], "WA")
        ensure_w(ll[0], first_wb(ll[0]))

        def load_ln(l):
            dma("sp", LNG[:, :], lng_in[l:l + 1, :].partition_broadcast(128), [], [LNG])
            dma("sp", LNB[:, :], lnb_in[l:l + 1, :].partition_broadcast(128), [], [LNB])

        def load_xT(pi, s):
            xt = xT_t()
            dma("sp", xt[:, :, :], xT_d[pi % 2][:, :, s * ST:(s + 1) * ST], [xT_d[pi % 2]], [xt])
            return xt

        def proj_fm(W, c0, xt, ring, n=ST):
            p = ring()
            for k in range(8):
                op("pe", lambda h, k=k: h.matmul(p[:, 0:n], W[:, k, c0:c0 + 128], xt[:, k, 0:n],
                                                 start=(k == 0), stop=(k == 7)), [W, xt], [p])
            return p

        def proj_tm(W, c0, ncols, xt, sub, ring):
            p = ring()
            for k in range(8):
                op("pe", lambda h, k=k: h.matmul(p[:, 0:ncols], xt[:, k, sub * 128:(sub + 1) * 128], W[:, k, c0:c0 + ncols],
                                                 start=(k == 0), stop=(k == 7)), [W, xt], [p])
            return p

        def qx_gate_attn(xt, SG, act_recip=False):
            for c in range(2):
                p = proj_fm(WB, c * 128, xt, psX)
                op("act", lambda h, c=c, p=p: h.copy(QT[:, c, :], p[:, :]), [p], [QT])
            yield

            def gates(js):
                for j in js:
                    p = proj_fm(WB, 256 + j * 128, xt, psX)
                    op("act", lambda h, j=j, p=p: h.activation(SG[:, j, :], p[:, :], AF.Silu), [p], [SG])
                    yield

            for pr, gj in ((0, (6, 7, 0, 1)), (1, (2, 3, 4, 5))):
                pts = []
                for hd in (2 * pr, 2 * pr + 1):
                    for mc in range(2):
                        sc = psX()
                        op("pe", lambda h, hd=hd, mc=mc, sc=sc: h.matmul(sc[:, :], KTP[:, hd, mc * 128:(mc + 1) * 128], QT[:, pr, :],
                                                                         start=True, stop=True), [KTP, QT], [sc])
                        pt = wk16()
                        op("act", lambda h, sc=sc, pt=pt: h.activation(pt[:, :], sc[:, :], AF.Exp, scale=0.125), [sc], [pt])
                        pts.append((hd, mc, pt))
                yield
                yield from gates(gj)
                num, den = psX(), psX()
                for i, (hd, mc, pt) in enumerate(pts):
                    op("pe", lambda h, hd=hd, mc=mc, pt=pt, i=i: h.matmul(num[:, :], VP[:, mc, hd, :], pt[:, :],
                                                                          start=(i == 0), stop=(i == 3)), [VP, pt], [num])
                for i, (hd, mc, pt) in enumerate(pts):
                    op("pe", lambda h, hd=hd, pt=pt, i=i: h.matmul(den[:, :], OP_[:, hd % 2, :], pt[:, :],
                                                                   start=(i == 0), stop=(i == 3)), [OP_, pt], [den])
                rec = wk32()
                if act_recip:
                    op("act", lambda h: h.activation(rec[:, 0:ST], den[:, :], AF.Ln), [den], [rec])
                    op("act", lambda h: h.activation(rec[:, 0:ST], rec[:, 0:ST], AF.Exp, scale=-1.0), [rec], [rec])
                else:
                    op("dve", lambda h: h.reciprocal(rec[:, 0:ST], den[:, :]), [den], [rec])
                xo = wk32()
                op("dve", lambda h: h.tensor_tensor(xo[:, 0:ST], num[:, :], rec[:, 0:ST], ALU.mult), [num, rec], [xo])
                op("dve", lambda h, pr=pr: h.tensor_tensor(SG[:, 6 + pr, :], xo[:, 0:ST], SG[:, 6 + pr, :], ALU.mult), [xo, SG], [SG])
                yield

        def Ygen(pi, s, last, SG):
            xsrc = x_in if pi == 0 else xres[pi % 2]
            xdst = out_d if last else xres[(pi + 1) % 2]

            def unit(sub):
                t0 = s * ST + sub * 128
                xr = ln32()
                dma("sp", xr[:, :], xsrc[t0:t0 + 128, :], [xsrc], [xr])
                ys = [psY(), psY()]
                for hf in range(2):
                    for e in range(8):
                        op("pe", lambda h, e=e, hf=hf: h.matmul(ys[hf][:, :], SG[:, e, sub * 128:(sub + 1) * 128], WC[:, e, hf * 512:(hf + 1) * 512],
                                                                start=(e == 0), stop=(e == 7)), [SG, WC], [ys[hf]])
                yield
                smy = smallY()
                st6 = View.__new__(View); st6.base, st6.gen = smy.base, smy.gen
                mv = smy
                for hf in range(2):
                    op("dve", lambda h, hf=hf: h.scalar_tensor_tensor(xr[:, hf * 512:(hf + 1) * 512], xr[:, hf * 512:(hf + 1) * 512], ALPHA,
                                                                      ys[hf][:, :], ALU.mult, ALU.add), [xr, ys[hf]], [xr])
                    op("dve", lambda h, hf=hf: h.bn_stats(st6[:, hf * 6:(hf + 1) * 6], xr[:, hf * 512:(hf + 1) * 512]), [xr], [st6])
                op("dve", lambda h: h.bn_aggr(mv[:, 16:18], st6[:, 0:12]), [st6], [mv])
                yield
                op("act", lambda h: h.activation(mv[:, 20:21], mv[:, 17:18], AF.Sqrt, bias=EPSC[:, 0:1], scale=1.0), [mv, EPSC], [mv])
                yield
                op("dve", lambda h: h.reciprocal(mv[:, 18:19], mv[:, 20:21]), [mv], [mv])
                op("dve", lambda h: h.scalar_tensor_tensor(mv[:, 19:20], mv[:, 16:17], -1.0, mv[:, 18:19], ALU.mult, ALU.mult), [mv], [mv])
                yield
                op("act", lambda h: h.activation(xr[:, :], xr[:, :], AF.Identity, bias=mv[:, 19:20], scale=mv[:, 18:19]), [xr, mv], [xr])
                yield
                op("dve", lambda h: h.tensor_tensor(xr[:, :], xr[:, :], LNG[:, :], ALU.mult), [xr, LNG], [xr])
                if not last:
                    xb = xnb_r()
                    op("dve", lambda h: h.tensor_tensor(xb[:, :], xr[:, :], LNB[:, :], ALU.add), [xr, LNB], [xb])
                yield
                op("pool", lambda h: h.tensor_tensor(xr[:, :], xr[:, :], LNB[:, :], ALU.add), [xr, LNB], [xr])
                dma("pool", xdst[t0:t0 + 128, :], xr[:, :], [xr], [xdst])
                if not last:
                    for k in range(8):
                        op("pe", lambda h, k=k: h.transpose(pst[:, k, :], xb[:, k * 128:(k + 1) * 128], IDB[:, :]), [xb, IDB], [pst])
                    yield
                    stg = xst()
                    op("act", lambda h: h.copy(stg[:, :, :], pst[:, :, :]), [pst], [stg])
                    dma("pool", xT_d[(pi + 1) % 2][:, :, t0:t0 + 128], stg[:, :, :], [stg], [xT_d[(pi + 1) % 2]])

            yield from stagger_ticks((unit(sub) for sub in range(4)), 4)

        def pipeline(X, pi, last, r, hook=None):
            run_gen(X(0, SGs[0]))
            for s in range(1, NS):
                if pi == 0 and lazy_pro and s + 1 < NS:
                    prologue(s + 1)
                interleave(X(s, SGs[s % 2]), Ygen(pi, s - 1, last, SGs[(s - 1) % 2]), r)
            if hook is not None:
                hook()
            run_gen(Ygen(pi, NS - 1, last, SGs[(NS - 1) % 2]))

        cc_n = [0]

        def gather(src_sb_ap, src_buf, rows, cols):
            cc_n[0] += 1
            csrc = kb.dram([rows, cols], F32, "cc_src%d" % cc_n[0])
            cdst = kb.dram([NR * rows, cols], F32, "cc_dst%d" % cc_n[0])
            dma("pool", csrc[:, :], src_sb_ap, [src_buf], [csrc])
            kb.allgather(groups, csrc, cdst)
            return cdst

        def halo_A(pi):
            xsrc = x_in if pi == 0 else xres[pi % 2]
            xl = ln32()
            dma("sp", xl[0:16, :], xsrc[T - 16:T, :], [xsrc], [xl])
            cdst = gather(xl[0:16, :], xl, 16, D)
            g = ln32()
            dma("sp", g[0:NR * 16, :], cdst[:, :], [cdst], [g])
            return g

        def halo_B(g):
            ph = psb()
            for k in range(8):
                op("pe", lambda h, k=k: h.matmul(ph[:, k * 16:(k + 1) * 16], g[0:NR * 16, k * 128:(k + 1) * 128], RANKC[0:NR * 16, 80:96],
                                                 start=True, stop=True), [g, RANKC], [ph])
            op("act", lambda h: h.copy(XTH.ap.rearrange("p k t -> p (k t)"), ph[:, 0:128]), [ph], [XTH])
            for j in range(6):
                p = psb()
                for k in range(8):
                    op("pe", lambda h, k=k, j=j, p=p: h.matmul(p[:, 0:16], WA[:, k, j * 128:(j + 1) * 128], XTH[:, k, :],
                                                               start=(k == 0), stop=(k == 7)), [WA, XTH], [p])
                op("act", lambda h, j=j, p=p: h.copy(HALO[:, j, :], p[:, 0:16]), [p], [HALO])

        def layer0(l, pi, last, nxt):
            ensure_w(l, "WA")
            ensure_w(l, "WB")
            ensure_w(l, "WC")
            load_ln(l)
            WST32 = tm32()
            dma("sp", WST32[:, :], a_wsT_in.ap.rearrange("p g t -> p (g t)"), [a_wsT_in], [WST32])
            for g in range(6):
                op("dve", lambda h, g=g: h.tensor_tensor(LCB[:, g, 0:128], WST32[:, g * 128:(g + 1) * 128], CONST[:, C_TRI:C_TRI + 128], ALU.mult),
                   [WST32, CONST], [LCB])
            dma("sp", LC[:, 0:TOK], a_bs_in[0:1, :].partition_broadcast(128), [a_bs_in], [LC])

            def X(s, SG):
                xt = load_xT(pi, s)
                gus = []
                for j in range(6):
                    p = proj_fm(WA, j * 128, xt, psX)
                    gu = wk32()
                    op("act", lambda h, p=p, gu=gu: h.activation(gu[:, 0:ST], p[:, :], AF.Gelu_apprx_tanh), [p], [gu])
                    gus.append(gu)
                    yield
                vns = {}

                def stA(sub):
                    p1 = proj_tm(WA, 768, 512, xt, sub, psX)
                    p2 = proj_tm(WA, 1280, 256, xt, sub, psX)
                    gv, sq = tm32(), tm32()
                    op("act", lambda h: h.activation(gv[:, 0:512], p1[:, :], AF.Gelu_apprx_tanh), [p1], [gv])
                    op("act", lambda h: h.activation(gv[:, 512:768], p2[:, 0:256], AF.Gelu_apprx_tanh), [p2], [gv])
                    op("act", lambda h: h.activation(sq[:, :], gv[:, :], AF.Square), [gv], [sq])
                    sm = small()
                    op("dve", lambda h: h.reduce_sum(sm[:, 0:6], gv.ap.rearrange("p (g c) -> p g c", c=128), axis=AX.X), [gv], [sm])
                    op("dve", lambda h: h.reduce_sum(sm[:, 6:12], sq.ap.rearrange("p (g c) -> p g c", c=128), axis=AX.X), [sq], [sm])
                    sm2 = small()
                    op("dve", lambda h: h.tensor_scalar(sm2[:, 0:6], sm[:, 0:6], 1.0 / 128, None, ALU.mult), [sm], [sm2])
                    op("dve", lambda h: h.tensor_tensor(sm2[:, 6:12], sm2[:, 0:6], sm2[:, 0:6], ALU.mult), [sm2], [sm2])
                    sm3 = small()
                    op("dve", lambda h: h.scalar_tensor_tensor(sm3[:, 0:6], sm[:, 6:12], 1.0 / 128, sm2[:, 6:12], ALU.mult, ALU.subtract), [sm, sm2], [sm3])
                    sm4 = small()
                    op("act", lambda h: h.activation(sm4[:, 0:6], sm3[:, 0:6], AF.Sqrt, bias=EPSC[:, 0:1], scale=1.0), [sm3, EPSC], [sm4])
                    op("dve", lambda h: h.reciprocal(sm3[:, 6:12], sm4[:, 0:6]), [sm4], [sm3])
                    gv3 = gv.ap.rearrange("p (g c) -> p g c", c=128)
                    op("dve", lambda h: h.tensor_tensor(gv3, gv3, sm2[:, 0:6].unsqueeze(2).to_broadcast([128, 6, 128]), ALU.subtract), [gv, sm2], [gv])
                    vn = tm16()
                    op("dve", lambda h: h.tensor_tensor(vn.ap.rearrange("p (g c) -> p g c", c=128), gv3,
                                                        sm3[:, 6:12].unsqueeze(2).to_broadcast([128, 6, 128]), ALU.mult), [gv, sm3], [vn])
                    vns[sub] = vn

                def stB(sub):
                    vn = vns[sub]
                    pa, pb = psX(), psX()
                    for g in range(6):
                        dst = pa[:, g * 128:(g + 1) * 128] if g < 4 else pb[:, (g - 4) * 128:(g - 3) * 128]
                        op("pe", lambda h, g=g, dst=dst: h.matmul(dst, vn[:, g * 128:(g + 1) * 128], LCB[:, g, 0:128], start=True, stop=True),
                           [vn, LCB], [pa if g < 4 else pb])
                    op("dve", lambda h: h.tensor_tensor(TOKB[:, 0:4, sub * 128:(sub + 1) * 128], pa.ap.rearrange("p (g t) -> p g t", t=128),
                                                        LC[:, 0:512].rearrange("p (g t) -> p g t", t=128), ALU.add), [pa, LC], [TOKB])
                    op("dve", lambda h: h.tensor_tensor(TOKB[:, 4:6, sub * 128:(sub + 1) * 128], pb[:, 0:256].rearrange("p (g t) -> p g t", t=128),
                                                        LC[:, 512:768].rearrange("p (g t) -> p g t", t=128), ALU.add), [pb, LC], [TOKB])

                for f, a in ((stA, 0), (stA, 1), (stB, 0), (stA, 2), (stB, 1), (stA, 3), (stB, 2), (stB, 3)):
                    f(a)
                    yield
                if s == NS - 1:
                    ensure_w(nxt, "WA")
                yield from qx_gate_attn(xt, SG)
                if s == NS - 1 and nxt is not None:
                    ensure_w(nxt, first_wb(nxt))
                for j in range(6):
                    op("pool", lambda h, j=j: h.tensor_tensor(TOKB[:, j, :], TOKB[:, j, :], gus[j][:, 0:ST], ALU.mult), [TOKB, gus[j]], [TOKB])
                    op("dve", lambda h, j=j: h.tensor_tensor(SG[:, j, :], TOKB[:, j, :], SG[:, j, :], ALU.mult), [TOKB, SG], [SG])
                    yield

            def hook():
                ensure_w(nxt, "WA")
                if nxt is not None:
                    ensure_w(nxt, first_wb(nxt))
            pipeline(X, pi, last, 3, hook)

        def layer2(l, pi, last, nxt):
            ensure_w(l, "WA")
            op("dve", lambda h: h.memset(HALO[:, :, :], 0.0), [], [HALO])
            hg = halo_A(pi) if NR > 1 else None
            ensure_w(l, "WB")
            ensure_w(l, "WC")
            load_ln(l)
            op("pool", lambda h: h.memset(LCB[:, :, :], 0.0), [], [LCB])
            for g in range(4):
                r0 = 192 * g
                done = 0
                while done < 192:
                    r = r0 + done
                    ck, p0 = r // 128, r % 128
                    n = min(128 - p0, 192 - done)
                    dma("pool", LCB[p0:p0 + n, ck, 192 * g:192 * g + 192], c_wp_in[g, done:done + n, :], [c_wp_in], [LCB])
                    done += n
            dma("sp", LC[:, 0:6], c_sc_in[:, :], [c_sc_in], [LC])
            wkc_ring = Ring(wk32.bufs + tm32.bufs)
            overlap = {}
            for jd in range(6):
                gs = set(c // 192 for c in (128 * jd, 128 * jd + 127))
                overlap[jd] = [ci for ci in range(6) if set(r // 192 for r in (128 * ci, 128 * ci + 127)) & gs]

            def X(s, SG):
                xt = load_xT(pi, s)
                yield from qx_gate_attn(xt, SG, act_recip=True)
                if s == NS - 1 and nxt is not None:
                    ensure_w(nxt, first_wb(nxt))
                if s == 0 and hg is not None:
                    halo_B(hg)
                diffs = [None] * 6
                wkc = wkc_ring

                def cgen(j):
                    p = proj_fm(WA, j * 128, xt, psX)
                    pe_ = wkc()
                    op("dve", lambda h: h.tensor_copy(pe_[:, 0:16], HALO[:, j, :]), [HALO], [pe_])
                    op("act", lambda h: h.copy(pe_[:, 16:528], p[:, :]), [p], [pe_])
                    op("pool", lambda h: h.tensor_copy(HALO[:, j, :], pe_[:, 512:528]), [pe_], [HALO])
                    yield
                    g_lo, g_hi = (128 * j) // 192, (128 * j + 127) // 192

                    def shadd(eng, cur, prev, sh):
                        lo = 2 * sh - 1
                        op(eng, lambda h: h.tensor_tensor(cur[:, lo:528], prev[:, lo:528], prev[:, lo - sh:528 - sh], ALU.add), [prev], [cur])

                    def corr(t, k):
                        if s == 0:
                            op("dve", lambda h: h.tensor_tensor(t[:, 16:32], t[:, 16:32], RANKC[:, 16 * k:16 * k + 16], ALU.mult), [t, RANKC], [t])

                    ss = [pe_]
                    for k in range(g_hi + 1):
                        cur = wkc()
                        shadd("dve" if k == 0 else "pool", cur, ss[-1], 1 << k)
                        ss.append(cur)
                        if k % 2 == 1:
                            yield
                    yield
                    df = wk16()
                    ranges = [(0, 128, g_lo)] if g_lo == g_hi else [(0, 192 * g_hi - 128 * j, g_lo), (192 * g_hi - 128 * j, 128, g_hi)]
                    for (p0, p1, g) in ranges:
                        sk = ss[g + 1]
                        corr(sk, g)
                        op("dve", lambda h, p0=p0, p1=p1, g=g, sk=sk: h.scalar_tensor_tensor(df[p0:p1, :], sk[p0:p1, 16:528], 1.0 / (2 << g),
                                                                                         pe_[p0:p1, 16:528], ALU.mult, ALU.subtract), [sk, pe_], [df])
                    diffs[j] = df

                stagger((cgen(j) for j in range(6)), 4)
                if s == NS - 1:
                    ensure_w(nxt, "WA")
                yield
                for jd in range(6):
                    p = psX()
                    lst = overlap[jd]
                    for i, ci in enumerate(lst):
                        op("pe", lambda h, ci=ci, jd=jd, i=i, n=len(lst): h.matmul(p[:, :], LCB[:, ci, jd * 128:(jd + 1) * 128], diffs[ci][:, :],
                                                                                  start=(i == 0), stop=(i == n - 1)), [LCB, diffs[ci]], [p])
                    op("act", lambda h, jd=jd, p=p: h.activation(TOKB[:, jd, :], p[:, :], AF.Identity, scale=LC[:, jd:jd + 1]), [p, LC], [TOKB])
                    op("dve", lambda h, jd=jd: h.tensor_tensor(SG[:, jd, :], TOKB[:, jd, :], SG[:, jd, :], ALU.mult), [TOKB, SG], [SG])
                    yield

            def hook():
                ensure_w(nxt, "WA")
                if nxt is not None:
                    ensure_w(nxt, first_wb(nxt))
            pipeline(X, pi, last, 4, hook)

        def layer3(l, pi, last, nxt):
            ensure_w(l, "WA")
            dma("pool", LCB[:, :, 0:128], d_wgx_in[:, :, :], [d_wgx_in], [LCB])
            dma("pool", LCB[:, :, 128:256], d_wga_in[:, :, :], [d_wga_in], [LCB])
            dma("sp", LC[:, 0:24], d_cw_in.ap.rearrange("p h j -> p (h j)"), [d_cw_in], [LC])
            dma("sp", LC[:, 24:30], d_cb_in[:, :], [d_cb_in], [LC])
            dma("sp", LC[:, 30:36], d_bgx_in[:, :], [d_bgx_in], [LC])
            dma("sp", LC[:, 36:42], d_bga_in[:, :], [d_bga_in], [LC])
            dma("sp", LC[:, 42:48], d_ap_in[:, :], [d_ap_in], [LC])
            op("act", lambda h: h.activation(LC[:, 54:60], LC[:, 42:48], AF.Exp, scale=-1.0), [LC], [LC])
            op("dve", lambda h: h.tensor_scalar(LC[:, 54:60], LC[:, 54:60], 1.0, None, ALU.add), [LC], [LC])
            op("act", lambda h: h.activation(LC[:, 60:66], LC[:, 54:60], AF.Ln), [LC], [LC])
            op("dve", lambda h: h.tensor_scalar(LC[:, 48:54], LC[:, 60:66], -LRU_C, None, ALU.mult), [LC], [LC])
            op("dve", lambda h: h.memset(HALO[:, :, :], 0.0), [], [HALO])
            op("dve", lambda h: h.memset(HPREV[:, :], 0.0), [], [HPREV])
            op("dve", lambda h: h.memset(HPREV[:, 8:14], 1.0), [], [HPREV])
            op("dve", lambda h: h.memset(TOKB[:, 0, :], 0.0), [], [TOKB])
            if NR > 1:
                halo_B(halo_A(pi))
            ensure_w(l, "WB")
            ensure_w(l, "WC")
            load_ln(l)
            hloc = kb.dram([NS, 128, 6, ST], F32, "hloc")
            aloc = kb.dram([NS, 128, 6, ST], F32, "aloc")

            sgf = [SGs[i].ap.rearrange("p a b -> p (a b)").bitcast(F32) for i in range(2)]
            al_sg = [alias_bufs(SGs[i], [sgf[i][:, 512 * k:512 * (k + 1)] for k in range(4)], "sgal%d_" % i) for i in range(2)]
            al_tk = alias_bufs(TOKB, [TOKB.ap[:, k, :] for k in range(1, 6)], "tkal")
            r32 = Ring(al_sg[0] + al_sg[1] + tm32.bufs + ln32.bufs)
            rHA = Ring(al_tk)
            xts = {}

            def chunk_gen(s, j):
                if j == 0:
                    xts[s] = load_xT(pi, s)
                xt = xts[s]
                p = proj_fm(WA, j * 128, xt, psb)
                xb = wk32()
                op("dve", lambda h: h.tensor_copy(xb[:, 0:16], HALO[:, j, :]), [HALO], [xb])
                op("act", lambda h: h.copy(xb[:, 16:528], p[:, :]), [p], [xb])
                op("pool", lambda h: h.tensor_copy(HALO[:, j, :], xb[:, 512:528]), [xb], [HALO])
                yield
                xc = wk32()
                cw = lambda k: LC[:, j * 4 + k:j * 4 + k + 1]
                op("dve", lambda h: h.tensor_scalar(xc[:, 0:512], xb[:, 13:525], cw(0), LC[:, 24 + j:25 + j], ALU.mult, ALU.add), [xb, LC], [xc])
                op("dve", lambda h: h.scalar_tensor_tensor(xc[:, 0:512], xb[:, 14:526], cw(1), xc[:, 0:512], ALU.mult, ALU.add), [xb, LC, xc], [xc])
                op("dve", lambda h: h.scalar_tensor_tensor(xc[:, 0:512], xb[:, 15:527], cw(2), xc[:, 0:512], ALU.mult, ALU.add), [xb, LC, xc], [xc])
                op("dve", lambda h: h.scalar_tensor_tensor(xc[:, 0:512], xb[:, 16:528], cw(3), xc[:, 0:512], ALU.mult, ALU.add), [xb, LC, xc], [xc])
                yield
                xcb = wk16()
                op("act", lambda h: h.copy(xcb[:, :], xc[:, 0:512]), [xc], [xcb])
                pgx, pga = psb(), psb()
                op("pe", lambda h: h.matmul(pgx[:, 0:512], LCB[:, j, 0:128], xcb[:, :], start=True, stop=True), [LCB, xcb], [pgx])
                op("pe", lambda h: h.matmul(pga[:, 0:512], LCB[:, j, 128:256], xcb[:, :], start=True, stop=True), [LCB, xcb], [pga])
                yield
                gx, ga = r32(), r32()
                op("act", lambda h: h.activation(gx[:, 0:512], pgx[:, 0:512], AF.Sigmoid, bias=LC[:, 30 + j:31 + j], scale=1.0), [pgx, LC], [gx])
                op("act", lambda h: h.activation(ga[:, 0:512], pga[:, 0:512], AF.Sigmoid, bias=LC[:, 36 + j:37 + j], scale=1.0), [pga, LC], [ga])
                op("act", lambda h: h.activation(ga[:, 0:512], ga[:, 0:512], AF.Exp, scale=LC[:, 48 + j:49 + j]), [ga, LC], [ga])
                yield
                m = r32()
                op("dve", lambda h: h.tensor_tensor(m[:, 0:512], ga[:, 0:512], ga[:, 0:512], ALU.mult), [ga], [m])
                op("dve", lambda h: h.tensor_scalar(m[:, 0:512], m[:, 0:512], -1.0, 1.0, ALU.mult, ALU.add), [m], [m])
                op("dve", lambda h: h.tensor_tensor(gx[:, 0:512], gx[:, 0:512], xc[:, 0:512], ALU.mult), [gx, xc], [gx])
                yield
                op("act", lambda h: h.activation(m[:, 0:512], m[:, 0:512], AF.Sqrt), [m], [m])
                if s == 0:
                    t = small()
                    op("dve", lambda h: h.tensor_scalar(t[:, 0:1], m[:, 0:1], -1.0, 1.0, ALU.mult, ALU.add), [m], [t])
                    op("dve", lambda h: h.scalar_tensor_tensor(m[:, 0:1], t[:, 0:1], RANKC[:, 64:65], m[:, 0:1], ALU.mult, ALU.add), [t, RANKC, m], [m])
                yield
                op("pool", lambda h: h.tensor_tensor(gx[:, 0:512], gx[:, 0:512], m[:, 0:512], ALU.mult), [gx, m], [gx])
                yield
                H, A = rHA(), rHA()
                op("dve", lambda h: h.tensor_tensor_scan(H[:, :], ga[:, 0:512], gx[:, 0:512], HPREV[:, j:j + 1], ALU.mult, ALU.add), [ga, gx, HPREV], [H])
                op("dve", lambda h: h.tensor_copy(HPREV[:, j:j + 1], H[:, 511:512]), [H], [HPREV])
                op("dve", lambda h: h.tensor_tensor_scan(A[:, :], ga[:, 0:512], TOKB[:, 0, :], HPREV[:, 8 + j:9 + j], ALU.mult, ALU.add),
                   [ga, TOKB, HPREV], [A])
                op("dve", lambda h: h.tensor_copy(HPREV[:, 8 + j:9 + j], A[:, 511:512]), [A], [HPREV])
                dma("pool", hloc[s, :, j, :], H[:, :], [H], [hloc])
                dma("pool", aloc[s, :, j, :], A[:, :], [A], [aloc])

            stagger((chunk_gen(s, j) for s in range(NS) for j in range(6)), 4)
            for i in range(2):
                retire(al_sg[i], SGs[i])
            retire(al_tk, TOKB)
            hin = small()
            op("dve", lambda h: h.memset(hin[:, :], 0.0), [], [hin])
            if NR > 1:
                cdst = gather(HPREV[:, :], HPREV, 128, 16)
                G = tm32()
                dma("sp", G.ap[:, 0:NR * 16].rearrange("p (r c) -> p r c", c=16), cdst.ap.rearrange("(r p) c -> p r c", p=128), [cdst], [G])
                for j in range(NR - 1):
                    dp = small()
                    op("dve", lambda h, j=j, dp=dp: h.tensor_scalar(dp[:, 0:6], G[:, 16 * j + 8:16 * j + 14], RANKC[:, 65 + j:66 + j], RANKC[:, 69 + j:70 + j],
                                                                    ALU.mult, ALU.add), [G, RANKC], [dp])
                    op("dve", lambda h, j=j, dp=dp: h.tensor_scalar(dp[:, 6:12], G[:, 16 * j:16 * j + 6], RANKC[:, 65 + j:66 + j], None, ALU.mult), [G, RANKC], [dp])
                    op("dve", lambda h, dp=dp: h.tensor_tensor(hin[:, 0:6], hin[:, 0:6], dp[:, 0:6], ALU.mult), [hin, dp], [hin])
                    op("dve", lambda h, dp=dp: h.tensor_tensor(hin[:, 0:6], hin[:, 0:6], dp[:, 6:12], ALU.add), [hin, dp], [hin])
            op("dve", lambda h: h.tensor_copy(HPREV[:, 0:6], hin[:, 0:6]), [hin], [HPREV])

            def X(s, SG):
                xt = load_xT(pi, s)
                yield from qx_gate_attn(xt, SG)
                for j in range(6):
                    H, A = wk32(), wk32()
                    dma("sp", H[:, 0:512], hloc[s, :, j, :], [hloc], [H])
                    dma("sp", A[:, 0:512], aloc[s, :, j, :], [aloc], [A])
                    op("dve", lambda h, j=j, H=H, A=A: h.scalar_tensor_tensor(H[:, 0:512], A[:, 0:512], HPREV[:, j:j + 1], H[:, 0:512], ALU.mult, ALU.add),
                       [A, HPREV, H], [H])
                    op("dve", lambda h, j=j, H=H: h.tensor_tensor(SG[:, j, :], H[:, 0:512], SG[:, j, :], ALU.mult), [H, SG], [SG])
                    yield
            pipeline(X, pi, last, 2)

        def layer1(l, pi, last, nxt):
            oloc = kb.dram([NS, 128, 6, ST], F32, "oloc")
            qtl = kb.dram([NS, 128, 6, ST], BF16, "qtl")
            ensure_w(l, "WA")
            ensure_w(l, "WB1")
            ensure_w(l, "WC")
            SG = SGs[0]
            if NR > 1:
                op("dve", lambda h: h.memset(HPREV[:, 0:6], 0.0), [], [HPREV])
            flat = LCB.ap.rearrange("p a b -> p (a b)")
            S32 = [Buf(flat[:, 256 * h:256 * h + 256].bitcast(F32), "S32_%d" % h) for h in range(6)]
            SBF = [Buf(flat[:, 1536 + 128 * h:1536 + 128 * h + 128], "SBF_%d" % h) for h in range(6)]
            rqk = Ring([Buf(flat[:, 2304 + 256 * i:2304 + 256 * i + 256], "rqk_%d" % i) for i in range(3)]
                       + [Buf(flat[:, 3712 + 256 * i:3712 + 256 * i + 256], "rqk_%d" % (3 + i)) for i in range(2)])
            r16 = Ring([Buf(flat[:, 3072 + 128 * i:3072 + 128 * i + 128], "r16_%d" % i) for i in range(3)]
                       + [Buf(flat[:, 4224 + 128 * i:4224 + 128 * i + 128], "r16_%d" % (3 + i)) for i in range(3)])
            psU = [Ring(PB[0:3]), Ring(PB[3:6])]
            sg1f = SGs[1].ap.rearrange("p a b -> p (a b)")
            al_b = alias_bufs(SGs[1], [sg1f[:, 768 * k:768 * (k + 1)] for k in range(5)], "sg1al")
            tmA = Ring(tm32.bufs + ln32.bufs)
            tmB = Ring(tm16.bufs + al_b)
            ONESM = Buf(flat[:, 3456:3712].bitcast(F32), "ONESM")
            op("dve", lambda h: h.memset(LCB[:, :, :], 0.0), [], [LCB])
            for b_ in S32 + SBF + r16.bufs + rqk.bufs + [ONESM]:
                b_.lastw = LCB.lastw
            op("dve", lambda h: h.memset(ONESM[:, :], 1.0 / 128), [], [ONESM])
            es_ = []
            for k in range(4):
                t = tm32()
                dma("sp", t[:, :], hlb_in[k:k + 1, :].partition_broadcast(128), [hlb_in], [t])
                op("act", lambda h, t=t: h.activation(t[:, :], t[:, :], AF.Exp), [t], [t])
                es_.append(t)
            tot = tm32()
            op("dve", lambda h: h.tensor_tensor(tot[:, :], es_[0][:, :], es_[1][:, :], ALU.add), [es_[0], es_[1]], [tot])
            op("dve", lambda h: h.tensor_tensor(tot[:, :], tot[:, :], es_[2][:, :], ALU.add), [tot, es_[2]], [tot])
            op("dve", lambda h: h.tensor_tensor(tot[:, :], tot[:, :], es_[3][:, :], ALU.add), [tot, es_[3]], [tot])
            op("dve", lambda h: h.reciprocal(tot[:, :], tot[:, :]), [tot], [tot])
            for k in range(2, l + 1):
                op("dve", lambda h, k=k: h.tensor_tensor(es_[1][:, :], es_[1][:, :], es_[k][:, :], ALU.add), [es_[1], es_[k]], [es_[1]])
            op("dve", lambda h: h.tensor_tensor(LC[:, 0:768], es_[1][:, :], tot[:, :], ALU.mult), [es_[1], tot], [LC])
            op("dve", lambda h: h.tensor_scalar(LC[:, 768:1536], LC[:, 0:768], -1.0, 1.0, ALU.mult, ALU.add), [LC], [LC])
            def subtile_gen(s, sub, xt, par):
                    psb = psU[par]
                    pq1, pq2 = proj_tm(WB, 0, 512, xt, sub, psb), proj_tm(WB, 512, 256, xt, sub, psb)
                    QS = tmA()
                    op("act", lambda h: h.activation(QS[:, 0:512], pq1[:, :], AF.Silu), [pq1], [QS])
                    op("act", lambda h: h.activation(QS[:, 512:768], pq2[:, 0:256], AF.Silu), [pq2], [QS])
                    yield
                    pf1, pf2 = proj_tm(WA, 0, 512, xt, sub, psb), proj_tm(WA, 512, 256, xt, sub, psb)
                    FK = tmA()
                    op("act", lambda h: h.activation(FK[:, 0:512], pf1[:, :], AF.Sigmoid), [pf1], [FK])
                    op("act", lambda h: h.activation(FK[:, 512:768], pf2[:, 0:256], AF.Sigmoid), [pf2], [FK])
                    op("dve", lambda h: h.tensor_tensor(FK[:, 0:768], FK[:, 0:768], LC[:, 768:1536], ALU.mult), [FK, LC], [FK])
                    op("pool", lambda h: h.tensor_tensor(FK[:, 0:768], FK[:, 0:768], LC[:, 0:768], ALU.add), [FK, LC], [FK])
                    LOGF = tmA()
                    op("act", lambda h: h.activation(LOGF[:, 0:768], FK[:, 0:768], AF.Ln), [FK], [LOGF])
                    op("pool", lambda h: h.tensor_scalar(FK[:, 0:768], FK[:, 0:768], -1.0, 1.0, ALU.mult, ALU.add), [FK], [FK])
                    yield
                    pi1, pi2 = proj_tm(WA, 768, 512, xt, sub, psb), proj_tm(WA, 1280, 256, xt, sub, psb)
                    V = tmB()
                    op("act", lambda h: h.copy(V[:, 0:512], pi1[:, :]), [pi1], [V])
                    op("dve", lambda h: h.tensor_copy(V[:, 512:768], pi2[:, 0:256]), [pi2], [V])
                    yield
                    pg1, pg2 = psb(), psb()
                    op("pe", lambda h: h.matmul(pg1[:, :], CONST[:, C_TRI64:C_TRI64 + 128], LOGF[:, 0:512], start=True, stop=True), [CONST, LOGF], [pg1])
                    op("pe", lambda h: h.matmul(pg2[:, 0:256], CONST[:, C_TRI64:C_TRI64 + 128], LOGF[:, 512:768], start=True, stop=True), [CONST, LOGF], [pg2])
                    E1, E2 = tmA(), tmA()
                    op("act", lambda h: h.activation(E1[:, 0:512], pg1[:, :], AF.Exp), [pg1], [E1])
                    op("act", lambda h: h.activation(E1[:, 512:768], pg2[:, 0:256], AF.Exp), [pg2], [E1])
                    op("act", lambda h: h.activation(E2[:, 0:512], pg1[:, :], AF.Exp, scale=-1.0), [pg1], [E2])
                    op("act", lambda h: h.activation(E2[:, 512:768], pg2[:, 0:256], AF.Exp, scale=-1.0), [pg2], [E2])
                    QD, KI = tmB(), tmB()
                    op("dve", lambda h: h.tensor_tensor(QD[:, :], QS[:, 0:768], E1[:, 0:768], ALU.mult), [QS, E1], [QD])
                    op("pool", lambda h: h.tensor_tensor(KI[:, :], FK[:, 0:768], E2[:, 0:768], ALU.mult), [FK, E2], [KI])
                    yield
                    pr1, pr2 = psb(), psb()
                    op("pe", lambda h: h.matmul(pr1[:, :], CONST[:, C_TRIREV64:C_TRIREV64 + 128], LOGF[:, 0:512], start=True, stop=True), [CONST, LOGF], [pr1])
                    op("pe", lambda h: h.matmul(pr2[:, 0:256], CONST[:, C_TRIREV64:C_TRIREV64 + 128], LOGF[:, 512:768], start=True, stop=True), [CONST, LOGF], [pr2])
                    E3 = tmA()
                    op("act", lambda h: h.activation(E3[:, 0:512], pr1[:, :], AF.Exp), [pr1], [E3])
                    op("act", lambda h: h.activation(E3[:, 512:768], pr2[:, 0:256], AF.Exp), [pr2], [E3])
                    KE = tmB()
                    op("dve", lambda h: h.tensor_tensor(KE[:, :], FK[:, 0:768], E3[:, 0:768], ALU.mult), [FK, E3], [KE])
                    yield
                    pgl = psb()
                    for hd in range(6):
                        op("pe", lambda h, hd=hd: h.matmul(pgl[:, hd * 4:hd * 4 + 2], LOGF[:, hd * 128:(hd + 1) * 128], CONST[:, C_IND:C_IND + 2],
                                                           start=True, stop=True), [LOGF, CONST], [pgl])
                    DEC = small()
                    op("act", lambda h: h.activation(DEC[:, 0:12].rearrange("p (h c) -> p h c", c=2),
                                                     pgl[:, 0:24].rearrange("p (h f) -> p h f", f=4)[:, :, 0:2], AF.Exp), [pgl], [DEC])
                    if NR > 1:
                        gl, gs, cdec = small(), small(), small()
                        glv = gl.ap[:, 0:12].rearrange("p (h c) -> p h c", c=2)
                        op("dve", lambda h: h.tensor_copy(glv, pgl[:, 0:24].rearrange("p (h f) -> p h f", f=4)[:, :, 0:2]), [pgl], [gl])
                        op("dve", lambda h: h.tensor_copy(gs[:, 0:6], HPREV[:, 0:6]), [HPREV], [gs])
                        op("dve", lambda h: h.tensor_tensor(gs[:, 6:12], HPREV[:, 0:6], glv[:, :, 0], ALU.add), [HPREV, gl], [gs])
                        op("dve", lambda h: h.tensor_tensor(HPREV[:, 0:6], gs[:, 6:12], glv[:, :, 1], ALU.add), [gs, gl], [HPREV])
                        op("act", lambda h: h.activation(cdec[:, 0:12], gs[:, 0:12], AF.Exp), [gs], [cdec])
                    yield
                    for hd in range(6):
                        hc = slice(hd * 128, (hd + 1) * 128)
                        sl = (hd % 4) * 2
                        op("pe", lambda h: h.transpose(p6bf[:, sl, :], QD[:, hc], IDB[:, :]), [QD, IDB], [P6])
                        op("pe", lambda h: h.transpose(p6bf[:, sl + 1, :], KI[:, hc], IDB[:, :]), [KI, IDB], [P6])
                        QK, SCM = rqk(), r16()
                        op("act", lambda h: h.copy(QK[:, :], p6bf[:, sl:sl + 2, :].rearrange("p a b -> p (a b)")), [P6], [QK])
                        QDT, KIT = QK.ap[:, 0:128], QK.ap[:, 128:256]
                        if NR > 1:
                            for c in range(2):
                                op("dve", lambda h, c=c: h.tensor_scalar(SG[:, hd, sub * 128 + 64 * c:sub * 128 + 64 * c + 64], QDT[:, 64 * c:64 * c + 64],
                                                                         cdec[:, c * 6 + hd:c * 6 + hd + 1], None, ALU.mult), [QK, cdec], [SG])
                        yield
                        SC = psb()
                        op("pe", lambda h: h.matmul(SC[:, 0:128], KIT, QDT, start=True, stop=True), [QK], [SC])
                        op("dve", lambda h: h.tensor_tensor(SCM[:, :], SC[:, 0:128], CONST[:, C_TRI64:C_TRI64 + 128], ALU.mult), [SC, CONST], [SCM])
                        yield
                        OT = psb()
                        op("pe", lambda h: h.matmul(OT[:, 0:128], V[:, hc], SCM[:, :], start=True, stop=False), [V, SCM], [OT])
                        for c in range(2):
                            op("pe", lambda h, c=c: h.matmul(OT[:, 64 * c:64 * c + 64], SBF[hd][:, :], QDT[:, 64 * c:64 * c + 64],
                                                             start=False, stop=(c == 1)), [SBF[hd], QK], [OT])
                            U = psb()
                            op("pe", lambda h, c=c, U=U: h.matmul(U[:, 0:128], KE[64 * c:64 * c + 64, hc], V[64 * c:64 * c + 64, hc],
                                                                  start=True, stop=True), [KE, V], [U])
                            op("dve", lambda h, c=c, U=U: h.scalar_tensor_tensor(SBF[hd][:, :], S32[hd][:, :], DEC[:, hd * 2 + c:hd * 2 + c + 1], U[:, 0:128],
                                                                                 ALU.mult, ALU.add), [S32[hd], DEC, U], [SBF[hd]])
                            op("dve", lambda h, c=c, U=U: h.scalar_tensor_tensor(S32[hd][:, :], S32[hd][:, :], DEC[:, hd * 2 + c:hd * 2 + c + 1], U[:, 0:128],
                                                                                 ALU.mult, ALU.add), [S32[hd], DEC, U], [S32[hd]])
                        op("act", lambda h: h.copy(TOKB[:, hd, sub * 128:(sub + 1) * 128], OT[:, 0:128]), [OT], [TOKB])
                        yield

            for s in range(NS):
                xt = load_xT(pi, s)
                stagger((subtile_gen(s, sub, xt, sub % 2) for sub in range(4)), 2)
                dma("pool", oloc[s, :, :, :], TOKB[:, :, :], [TOKB], [oloc])
                if NR > 1:
                    dma("pool", qtl[s, :, :, :], SG[:, 0:6, :], [SG], [qtl])
            retire(al_b, SGs[1])
            if NR == 1:
                ensure_w(l, "WB")
                ensure_w(nxt, "WA")
            if NR > 1:
                tf = TOKB.ap.rearrange("p a b -> p (a b)")
                for hd in range(6):
                    op("dve", lambda h, hd=hd: h.tensor_copy(tf[:, hd * 128:(hd + 1) * 128], S32[hd][:, :]), [S32[hd]], [TOKB])
                op("act", lambda h: h.activation(tf[:, 768:774], HPREV[:, 0:6], AF.Exp), [HPREV], [TOKB])
                cdst = gather(tf[:, 0:774], TOKB, 128, 774)
                ensure_w(l, "WB")
                ensure_w(nxt, "WA")
                dma("sp", tf[:, 0:(NR - 1) * 774].rearrange("p (r c) -> p r c", c=774),
                    cdst.ap[0:(NR - 1) * 128, :].rearrange("(r p) c -> p r c", p=128), [cdst], [TOKB])
                for hd in range(6):
                    op("dve", lambda h, hd=hd: h.memset(S32[hd][:, :], 0.0), [], [S32[hd]])
                for j in range(NR - 1):
                    dp = small()
                    o = j * 774
                    op("dve", lambda h, j=j, o=o, dp=dp: h.tensor_scalar(dp[:, 0:6], tf[:, o + 768:o + 774], RANKC[:, 65 + j:66 + j], RANKC[:, 69 + j:70 + j],
                                                                         ALU.mult, ALU.add), [TOKB, RANKC], [dp])
                    op("dve", lambda h, j=j, o=o: h.tensor_scalar(tf[:, o:o + 768], tf[:, o:o + 768], RANKC[:, 65 + j:66 + j], None, ALU.mult), [TOKB, RANKC], [TOKB])
                    for hd in range(6):
                        op("dve", lambda h, hd=hd, o=o, dp=dp: h.scalar_tensor_tensor(S32[hd][:, :], S32[hd][:, :], dp[:, hd:hd + 1], tf[:, o + hd * 128:o + (hd + 1) * 128],
                                                                                      ALU.mult, ALU.add), [S32[hd], dp, TOKB], [S32[hd]])
                for hd in range(6):
                    op("act", lambda h, hd=hd: h.copy(SBF[hd][:, :], S32[hd][:, :]), [S32[hd]], [SBF[hd]])
            load_ln(l)
            dma("sp", LC[:, 0:6], b_ng_in[:, :], [b_ng_in], [LC])

            def X(s, SG):
                xt = load_xT(pi, s)
                dma("sp", TOKB[:, :, :], oloc[s, :, :, :], [oloc], [TOKB])
                yield from qx_gate_attn(xt, SG, act_recip=True)
                if NR > 1:
                    for hd in range(6):
                        qt = wk16()
                        dma("sp", qt[:, :], qtl[s, :, hd, :], [qtl], [qt])
                        pc = psX()
                        op("pe", lambda h, hd=hd, qt=qt, pc=pc: h.matmul(pc[:, :], SBF[hd][:, :], qt[:, :], start=True, stop=True), [SBF[hd], qt], [pc])
                        op("dve", lambda h, hd=hd, pc=pc: h.tensor_tensor(TOKB[:, hd, :], TOKB[:, hd, :], pc[:, :], ALU.add), [TOKB, pc], [TOKB])
                        yield
                if s == NS - 1 and nxt is not None:
                    ensure_w(nxt, first_wb(nxt))
                rs = {}
                for grp in ((0, 1, 2), (3, 4, 5)):
                    pms = {}
                    for hd in grp:
                        sq = wk32()
                        op("act", lambda h, hd=hd, sq=sq: h.activation(sq[:, 0:512], TOKB[:, hd, :], AF.Square), [TOKB], [sq])
                        pm = psX()
                        op("pe", lambda h, sq=sq, pm=pm: h.matmul(pm[:, :], ONESM[:, :], sq[:, 0:512], start=True, stop=True), [ONESM, sq], [pm])
                        rs[hd], pms[hd] = sq, pm
                    yield
                    for hd in grp:
                        op("act", lambda h, hd=hd: h.activation(rs[hd][:, 0:512], pms[hd][:, :], AF.Ln, bias=EPSC[:, 1:2], scale=1.0),
                           [pms[hd], EPSC], [rs[hd]])
                    yield
                for hd in range(6):
                    r = rs[hd]
                    op("act", lambda h, r=r: h.activation(r[:, 0:512], r[:, 0:512], AF.Exp, scale=-0.5), [r], [r])
                    op("dve", lambda h, hd=hd, r=r: h.scalar_tensor_tensor(r[:, 0:512], TOKB[:, hd, :], LC[:, hd:hd + 1], r[:, 0:512], ALU.mult, ALU.mult),
                       [TOKB, LC, r], [r])
                    op("dve", lambda h, hd=hd, r=r: h.tensor_tensor(SG[:, hd, :], r[:, 0:512], SG[:, hd, :], ALU.mult), [r, SG], [SG])
                    yield

            def hook():
                if nxt is not None:
                    ensure_w(nxt, first_wb(nxt))
            pipeline(X, pi, last, 3, hook)

        layers = [layer0, layer1, layer2, layer3]
        for i, l in enumerate(ll):
            layers[l](l, i, i == len(ll) - 1, ll[i + 1] if i + 1 < len(ll) else None)
        kb.finish([out_d])
        print("instructions:", kb.ninstr)
    return nc


def host_inputs(inputs, T, NR, S):
    f = lambda a: np.ascontiguousarray(np.asarray(a, dtype=np.float32))
    x = f(inputs["x"])
    mem = f(inputs["mem"])
    consts = np.zeros((128, NCONST), np.float32)
    consts[:, 0:128] = np.eye(128, dtype=np.float32)
    idx = np.arange(128)
    consts[:, 128:256] = (idx[:, None] <= idx[None, :]).astype(np.float32)
    same = (idx[:, None] // 64) == (idx[None, :] // 64)
    consts[:, 256:384] = ((idx[:, None] <= idx[None, :]) & same).astype(np.float32)
    consts[:, 384:512] = ((idx[:, None] > idx[None, :]) & same).astype(np.float32)
    consts[0:64, 512] = 1.0
    consts[64:128, 513] = 1.0
    for j in range(6):
        for p in range(128):
            g = (j * 128 + p) // 192
            consts[p, 516 + j * 4 + g] = 1.0 / (2 ** (g + 1))
    common = {
        "mem_kv_w": f(inputs["mem_kv_w"]), "ln_g": f(inputs["ln_g"]), "ln_b": f(inputs["ln_b"]),
        "w_out": f(inputs["w_out"]), "hgrn_lb_logits": f(inputs["hgrn_lb_logits"]),
        "lbvec": np.zeros((2, TOK), np.float32),
        "a_w_in": f(inputs["a_w_in"][0]), "b_w_in": f(inputs["b_w_in"][0]),
        "c_w_in": f(inputs["c_w_in"][0]), "d_w_in": f(inputs["d_w_in"][0]),
        "a_w_sT": f(np.transpose(np.asarray(inputs["a_w_s"][0]), (2, 0, 1))),
        "a_b_s": f(np.asarray(inputs["a_b_s"][0]).reshape(1, TOK)),
        "b_norm_g": f(np.asarray(inputs["b_norm_g"][0]).reshape(6, 128).T),
        "c_w_pool": f(inputs["c_w_pool"][0]),
        "c_scale": f(np.asarray(inputs["c_scale"][0]).reshape(6, 128).T),
        "d_conv_w": f(np.transpose(np.asarray(inputs["d_conv_w"][0]).reshape(4, 6, 128), (2, 1, 0))),
        "d_conv_b": f(np.asarray(inputs["d_conv_b"][0]).reshape(6, 128).T),
        "d_w_gx": f(np.transpose(np.asarray(inputs["d_w_gx"][0]), (1, 0, 2))),
        "d_b_gx": f(np.asarray(inputs["d_b_gx"][0]).T),
        "d_w_ga": f(np.transpose(np.asarray(inputs["d_w_ga"][0]), (1, 0, 2))),
        "d_b_ga": f(np.asarray(inputs["d_b_ga"][0]).T),
        "d_a_param": f(np.asarray(inputs["d_a_param"][0]).reshape(6, 128).T),
        "consts": consts,
    }
    maps = []
    for c in range(8):
        b, r = (c // NR) % 2, c % NR
        m = dict(common)
        m["x"] = np.ascontiguousarray(x[b, r * T:(r + 1) * T, :])
        m["memT"] = np.ascontiguousarray(mem[b].T.reshape(8, 128, NMEM).transpose(1, 0, 2))
        rk = np.zeros((128, 128), np.float32)
        rk[:, 64] = 1.0 if r == 0 else 0.0
        for j in range(4):
            rk[:, 65 + j] = 1.0 if j < r else 0.0
            rk[:, 69 + j] = 0.0 if j < r else 1.0
        if r > 0:
            for t_ in range(16):
                rk[(r - 1) * 16 + t_, 80 + t_] = 1.0
        for k, wdw in enumerate((2, 4, 8, 16)):
            tt = np.arange(16)
            rk[:, 16 * k:16 * k + 16] = (wdw / np.minimum(tt + 1, wdw))[None, :] if r == 0 else 1.0
        m["rankc"] = rk
        maps.append(m)
    return maps


_NC_CACHE = {}


def run(inputs, S, NR, nlayers=4, layer_list=None):
    T = S // NR
    key = (T, NR, nlayers, tuple(layer_list) if layer_list else None)
    if key not in _NC_CACHE:
        _NC_CACHE[key] = build(T, NR, nlayers, layer_list)
    nc = _NC_CACHE[key]
    maps = host_inputs(inputs, T, NR, S)
    res = run_bass_kernel_spmd(nc, maps, core_ids=list(range(8)))
    out = np.zeros((2, S, D), np.float32)
    for c in range(8):
        b, r = (c // NR) % 2, c % NR
        if c < 2 * NR:
            out[b, r * T:(r + 1) * T, :] = res.results[c]["out"]
    return out


def kernel(**inputs):
    S = inputs["x"].shape[1]
    return run(inputs, S, 4)
```
